# Optimizing a Trainium2 kernel written in Bass

```python
import math
import jax, jax.numpy as jnp
from jax import lax
import numpy as np


D_MODEL = 2048
BATCH = 32
SEQ = 256
DEPTH = 2
DEC_BATCH = 8
DEC_SEQ = 1024
PAST_LEN = 512

GRID_W = 64
Q_BLOCK = 128
ROPE_BASE = 10000.0
EPS = 1e-6
SSM_CH = 512
SSM_GROUP = 16
SSM_GROUPS = SSM_CH // SSM_GROUP
SSM_STATE = 64
GQA_HEADS = 6
GQA_KV_HEADS = 2
GQA_GROUP = GQA_HEADS // GQA_KV_HEADS
GQA_HEAD_DIM = 128
MLA_HEADS = 6
MLA_NOPE = 128
MLA_ROPE = 64
MLA_QK = MLA_NOPE + MLA_ROPE
MLA_V = 128
MLA_KV_RANK = 512
D_FF = 4 * D_MODEL
A_U_W = SSM_CH
B_Q_W = GQA_HEADS * GQA_HEAD_DIM
B_KV_W = GQA_KV_HEADS * GQA_HEAD_DIM
C_Q_W = MLA_HEADS * MLA_QK
IN_SPLITS = (A_U_W, A_U_W + B_Q_W, A_U_W + B_Q_W + B_KV_W, A_U_W + B_Q_W + 2 * B_KV_W, A_U_W + B_Q_W + 2 * B_KV_W + C_Q_W, A_U_W + B_Q_W + 2 * B_KV_W + C_Q_W + MLA_KV_RANK)
IN_WIDTH = IN_SPLITS[-1] + MLA_ROPE
MIX_WIDTH = SSM_CH + GQA_HEADS * GQA_HEAD_DIM + MLA_HEADS * MLA_V

kernel_name = 'hybrid_s5_gqa_mla_prefix_dit_step'


def rms_norm(x, g):
    xf = x.astype(jnp.float32)
    y = xf * lax.rsqrt(jnp.mean(xf * xf, axis=-1, keepdims=True) + EPS)
    return (y * g.astype(jnp.float32)).astype(x.dtype)


def modulate(h, shift, scale):
    return h * (1.0 + scale) + shift


def adaln(cvec, w, b):
    m = (jax.nn.silu(cvec) @ w + b)[:, None, :]
    return jnp.split(m, 6, axis=-1)


def axial_rope(x):
    t, d = x.shape[1], x.shape[-1]
    rows = t // GRID_W
    half = d // 2
    quarter = half // 2
    row = jnp.repeat(jnp.arange(rows, dtype=jnp.float32), GRID_W)
    col = jnp.tile(jnp.arange(GRID_W, dtype=jnp.float32), rows)
    inv = ROPE_BASE ** (-(jnp.arange(quarter, dtype=jnp.float32) / quarter))
    xf = x.astype(jnp.float32)

    def rotate(xa, pos):
        ang = pos[:, None] * inv[None, :]
        cos = jnp.cos(ang)[None, :, None, :]
        sin = jnp.sin(ang)[None, :, None, :]
        x1, x2 = xa[..., :quarter], xa[..., quarter:]
        return jnp.concatenate([x1 * cos - x2 * sin, x2 * cos + x1 * sin], axis=-1)

    out = jnp.concatenate([rotate(xf[..., :half], row), rotate(xf[..., half:], col)], axis=-1)
    return out.astype(x.dtype)


def rope_tail(x):
    return jnp.concatenate([x[..., :MLA_NOPE], axial_rope(x[..., MLA_NOPE:])], axis=-1)


def blocked_attention(q, k, v):
    bsz, s = q.shape[0], q.shape[1]
    nb = s // Q_BLOCK
    scale = 1.0 / math.sqrt(q.shape[-1])
    kf = k.astype(jnp.float32)
    vf = v.astype(jnp.float32)
    qb = q.reshape((bsz, nb, Q_BLOCK) + q.shape[2:]).swapaxes(0, 1)

    def one_block(qblk):
        sc = jnp.einsum('bqhgd,bkhd->bhgqk', qblk.astype(jnp.float32), kf) * scale
        pr = jax.nn.softmax(sc, axis=-1)
        return jnp.einsum('bhgqk,bkhd->bqhgd', pr, vf).astype(q.dtype)

    out = lax.map(one_block, qb)
    return out.swapaxes(0, 1).reshape((bsz, s) + out.shape[3:])


def linear_scan(lam_bar, bu, h0, reverse):
    if h0 is not None:
        edge = bu.shape[1] - 1 if reverse else 0
        bu = bu.at[:, edge].add(lam_bar[None] * h0)
    a = jnp.broadcast_to(lam_bar, bu.shape)

    def combine(e1, e2):
        a1, b1 = e1
        a2, b2 = e2
        return a1 * a2, a2 * b1 + b2

    _, states = lax.associative_scan(combine, (a, bu), reverse=reverse, axis=1)
    return states


def s5_bidirectional(u, lam_re, lam_im, log_dt, b_re, b_im, c_re, c_im, d_skip, w_glu, h0_f, h0_b):
    bsz, t = u.shape[0], u.shape[1]
    uf = u.astype(jnp.float32).reshape(bsz, t, SSM_GROUPS, SSM_GROUP)
    uc = uf.astype(jnp.complex64)
    y = d_skip.astype(jnp.float32).reshape(SSM_GROUPS, SSM_GROUP) * uf
    finals = []
    for dr, (h0, reverse) in enumerate(((h0_f, False), (h0_b, True))):
        lam = lax.complex(lam_re[dr].astype(jnp.float32), lam_im[dr].astype(jnp.float32))
        dt = jnp.exp(log_dt[dr].astype(jnp.float32))[:, None]
        lam_bar = jnp.exp(lam * dt)
        b_mat = lax.complex(b_re[dr].astype(jnp.float32), b_im[dr].astype(jnp.float32))
        b_bar = ((lam_bar - 1.0) / lam)[..., None] * b_mat
        bu = jnp.einsum('btgc,gpc->btgp', uc, b_bar)
        states = linear_scan(lam_bar, bu, h0, reverse)
        c_mat = lax.complex(c_re[dr].astype(jnp.float32), c_im[dr].astype(jnp.float32))
        y = y + jnp.real(jnp.einsum('gcp,btgp->btgc', c_mat, states))
        finals.append(states[:, 0] if reverse else states[:, -1])
    y = y.reshape(bsz, t, SSM_CH).astype(u.dtype)
    zg = y @ w_glu
    out = zg[..., :SSM_CH] * jax.nn.sigmoid(zg[..., SSM_CH:])
    return out, finals[0], finals[1]


def mla_expand(ckv_n, kr, w_uk, w_uv, k_norm):
    k_nope = jnp.einsum('btr,rhd->bthd', ckv_n, w_uk)
    v = jnp.einsum('btr,rhd->bthd', ckv_n, w_uv)
    k_rope = jnp.broadcast_to(kr[:, :, None, :], kr.shape[:2] + (MLA_HEADS, MLA_ROPE))
    k = rms_norm(jnp.concatenate([k_nope, k_rope], axis=-1), k_norm)
    return k, v


def trunk_layer(x, mod, p, ctx):
    latent = ctx is not None
    shift1, scale1, gate1, shift2, scale2, gate2 = mod
    bsz, t = x.shape[0], x.shape[1]
    h = modulate(rms_norm(x, p['norm_mix']), shift1, scale1)
    z = h @ p['w_in']
    u, qb, kb, vb, qc, ckv, kr = jnp.split(z, IN_SPLITS, axis=-1)

    if latent:
        s = ctx['ssm']
        h0 = lax.complex(s[..., 0].astype(jnp.float32), s[..., 1].astype(jnp.float32))
        h0_f, h0_b = h0[:, 0], h0[:, 1]
    else:
        h0_f, h0_b = None, None
    y_a, hf, hb = s5_bidirectional(u, p['ssm_lam_re'], p['ssm_lam_im'], p['ssm_log_dt'], p['ssm_b_re'], p['ssm_b_im'], p['ssm_c_re'], p['ssm_c_im'], p['ssm_d'], p['ssm_w_glu'], h0_f, h0_b)

    q = rms_norm(qb.reshape(bsz, t, GQA_HEADS, GQA_HEAD_DIM), p['gqa_q_norm'])
    k = rms_norm(kb.reshape(bsz, t, GQA_KV_HEADS, GQA_HEAD_DIM), p['gqa_k_norm'])
    v = vb.reshape(bsz, t, GQA_KV_HEADS, GQA_HEAD_DIM)
    if latent:
        q_b = axial_rope(q)
        k_b = jnp.concatenate([ctx['k'].astype(k.dtype), axial_rope(k)], axis=1)
        v_b = jnp.concatenate([ctx['v'].astype(v.dtype), v], axis=1)
    else:
        q_b, k_b, v_b = q, k, v
    y_b = blocked_attention(q_b.reshape(bsz, t, GQA_KV_HEADS, GQA_GROUP, GQA_HEAD_DIM), k_b, v_b)
    y_b = y_b.reshape(bsz, t, GQA_HEADS * GQA_HEAD_DIM)

    ckv_n = rms_norm(ckv, p['mla_kv_norm'])
    q_c = rms_norm(qc.reshape(bsz, t, MLA_HEADS, MLA_QK), p['mla_q_norm'])
    k_c, v_c = mla_expand(ckv_n, kr, p['mla_w_uk'], p['mla_w_uv'], p['mla_k_norm'])
    if latent:
        q_c = rope_tail(q_c)
        k_ctx, v_ctx = mla_expand(ctx['ckv'].astype(ckv_n.dtype), ctx['kr'].astype(kr.dtype), p['mla_w_uk'], p['mla_w_uv'], p['mla_k_norm'])
        k_c = jnp.concatenate([k_ctx, rope_tail(k_c)], axis=1)
        v_c = jnp.concatenate([v_ctx, v_c], axis=1)
    y_c = blocked_attention(q_c[:, :, :, None, :], k_c, v_c).reshape(bsz, t, MLA_HEADS * MLA_V)

    o = jnp.concatenate([y_a, y_b, y_c], axis=-1) @ p['w_out']
    x = x + gate1 * o
    h2 = modulate(rms_norm(x, p['norm_mlp']), shift2, scale2)
    x = x + gate2 * (jnp.square(jax.nn.relu(h2 @ p['w_ff1'])) @ p['w_ff2'])
    new_ctx = None if latent else (k, v, ckv_n, kr, hf, hb)
    return x, new_ctx


def setup_inputs(seed: int = 0) -> dict:
    key = jax.random.key(seed)
    ks = iter(jax.random.split(key, 48))
    f32 = jnp.float32

    def nrm(shape, scale):
        return scale * jax.random.normal(next(ks), shape, f32)

    def gain(shape):
        return 1.0 + 0.05 * jax.random.normal(next(ks), shape, f32)

    ssm_shape = (DEPTH, 2, SSM_GROUPS, SSM_STATE)
    return {
        'x_prompt': nrm((BATCH, SEQ, D_MODEL), 1.0),
        'x_sample': nrm((DEC_BATCH, DEC_SEQ, D_MODEL), 1.0),
        'cache_attn_k': nrm((DEC_BATCH, DEPTH, PAST_LEN, GQA_KV_HEADS, GQA_HEAD_DIM), 1.0),
        'cache_attn_v': nrm((DEC_BATCH, DEPTH, PAST_LEN, GQA_KV_HEADS, GQA_HEAD_DIM), 1.0),
        'cache_mla_ckv': nrm((DEC_BATCH, DEPTH, PAST_LEN, MLA_KV_RANK), 1.0),
        'cache_mla_krope': nrm((DEC_BATCH, DEPTH, PAST_LEN, MLA_ROPE), 1.0),
        'state_ssm': nrm((DEC_BATCH, DEPTH, 2, SSM_GROUPS, SSM_STATE, 2), 0.3),
        'c': nrm((DEC_BATCH, D_MODEL), 1.0),
        'c_ctx': nrm((D_MODEL,), 1.0),
        'w_mod': nrm((DEPTH, D_MODEL, 6 * D_MODEL), D_MODEL ** -0.5),
        'b_mod': nrm((DEPTH, 6 * D_MODEL), 0.02),
        'norm_mix': gain((DEPTH, D_MODEL)),
        'norm_mlp': gain((DEPTH, D_MODEL)),
        'w_in': nrm((DEPTH, D_MODEL, IN_WIDTH), D_MODEL ** -0.5),
        'gqa_q_norm': gain((DEPTH, GQA_HEAD_DIM)),
        'gqa_k_norm': gain((DEPTH, GQA_HEAD_DIM)),
        'mla_kv_norm': gain((DEPTH, MLA_KV_RANK)),
        'mla_q_norm': gain((DEPTH, MLA_QK)),
        'mla_k_norm': gain((DEPTH, MLA_QK)),
        'mla_w_uk': nrm((DEPTH, MLA_KV_RANK, MLA_HEADS, MLA_NOPE), MLA_KV_RANK ** -0.5),
        'mla_w_uv': nrm((DEPTH, MLA_KV_RANK, MLA_HEADS, MLA_V), MLA_KV_RANK ** -0.5),
        'ssm_lam_re': -0.5 * jnp.exp(nrm(ssm_shape, 0.02)),
        'ssm_lam_im': math.pi * jnp.arange(SSM_STATE, dtype=f32) + nrm(ssm_shape, 0.02),
        'ssm_log_dt': jax.random.uniform(next(ks), (DEPTH, 2, SSM_GROUPS), f32, math.log(1e-3), math.log(1e-1)),
        'ssm_b_re': nrm((DEPTH, 2, SSM_GROUPS, SSM_STATE, SSM_GROUP), SSM_GROUP ** -0.5),
        'ssm_b_im': nrm((DEPTH, 2, SSM_GROUPS, SSM_STATE, SSM_GROUP), SSM_GROUP ** -0.5),
        'ssm_c_re': nrm((DEPTH, 2, SSM_GROUPS, SSM_GROUP, SSM_STATE), SSM_STATE ** -0.5),
        'ssm_c_im': nrm((DEPTH, 2, SSM_GROUPS, SSM_GROUP, SSM_STATE), SSM_STATE ** -0.5),
        'ssm_d': nrm((DEPTH, SSM_CH), 1.0),
        'ssm_w_glu': nrm((DEPTH, SSM_CH, 2 * SSM_CH), SSM_CH ** -0.5),
        'w_out': nrm((DEPTH, MIX_WIDTH, D_MODEL), MIX_WIDTH ** -0.5),
        'w_ff1': nrm((DEPTH, D_MODEL, D_FF), D_MODEL ** -0.5),
        'w_ff2': nrm((DEPTH, D_FF, D_MODEL), D_FF ** -0.5),
    }


def reference(x_prompt, x_sample, cache_attn_k, cache_attn_v, cache_mla_ckv, cache_mla_krope, state_ssm, c, c_ctx, w_mod, b_mod, norm_mix, norm_mlp, w_in, gqa_q_norm, gqa_k_norm, mla_kv_norm, mla_q_norm, mla_k_norm, mla_w_uk, mla_w_uv, ssm_lam_re, ssm_lam_im, ssm_log_dt, ssm_b_re, ssm_b_im, ssm_c_re, ssm_c_im, ssm_d, ssm_w_glu, w_out, w_ff1, w_ff2):
    xp = x_prompt
    xs = x_sample
    new_k, new_v, new_ckv, new_kr, new_ssm = [], [], [], [], []
    for l in range(DEPTH):
        p = {
            'norm_mix': norm_mix[l], 'norm_mlp': norm_mlp[l], 'w_in': w_in[l], 'w_out': w_out[l],
            'w_ff1': w_ff1[l], 'w_ff2': w_ff2[l],
            'gqa_q_norm': gqa_q_norm[l], 'gqa_k_norm': gqa_k_norm[l],
            'mla_kv_norm': mla_kv_norm[l], 'mla_q_norm': mla_q_norm[l], 'mla_k_norm': mla_k_norm[l],
            'mla_w_uk': mla_w_uk[l], 'mla_w_uv': mla_w_uv[l],
            'ssm_lam_re': ssm_lam_re[l], 'ssm_lam_im': ssm_lam_im[l], 'ssm_log_dt': ssm_log_dt[l],
            'ssm_b_re': ssm_b_re[l], 'ssm_b_im': ssm_b_im[l], 'ssm_c_re': ssm_c_re[l], 'ssm_c_im': ssm_c_im[l],
            'ssm_d': ssm_d[l], 'ssm_w_glu': ssm_w_glu[l],
        }
        mod_ctx = adaln(c_ctx[None, :], w_mod[l], b_mod[l])
        mod_lat = adaln(c, w_mod[l], b_mod[l])
        xp, (k, v, ckv_n, kr, hf, hb) = trunk_layer(xp, mod_ctx, p, None)
        new_k.append(k)
        new_v.append(v)
        new_ckv.append(ckv_n)
        new_kr.append(kr)
        hs = jnp.stack([hf, hb], axis=1)
        new_ssm.append(jnp.stack([jnp.real(hs), jnp.imag(hs)], axis=-1))
        ctx = {'k': cache_attn_k[:, l], 'v': cache_attn_v[:, l], 'ckv': cache_mla_ckv[:, l], 'kr': cache_mla_krope[:, l], 'ssm': state_ssm[:, l]}
        xs, _ = trunk_layer(xs, mod_lat, p, ctx)
    return (xp, xs, jnp.stack(new_k, axis=1), jnp.stack(new_v, axis=1), jnp.stack(new_ckv, axis=1), jnp.stack(new_kr, axis=1), jnp.stack(new_ssm, axis=1))
```

```python
import math
import numpy as np
from contextlib import ExitStack
import concourse.bass as bass
import concourse.mybir as mybir
from concourse.bass_utils import run_bass_kernel_spmd

F32 = mybir.dt.float32
BF16 = mybir.dt.bfloat16
ALU = mybir.AluOpType
AF = mybir.ActivationFunctionType
AX = mybir.AxisListType

D = 2048
NT = 1024
KT = 16
DFF = 8192
INW = 3520
EPS = 1e-6
NCORES = 8
TWO_PI = 2.0 * math.pi


class _Rec:
    def __init__(self):
        self.calls = []

    def __getattr__(self, name):
        def f(*a, **k):
            self.calls.append((name, a, k))
            return None
        return f


class Prog:
    NDMA = 40

    def __init__(self, nc):
        self.nc = nc
        self.ops = []

    def op(self, eng, fn, reads=(), writes=(), dma=False):
        rec = _Rec()
        fn(rec)
        calls = rec.calls
        assert calls

        def emit(E, calls=calls):
            ins = None
            for (name, a, k) in calls:
                ins = getattr(E, name)(*a, **k)
            return ins
        self.ops.append((eng, emit, tuple(reads), tuple(writes), dma))

    def finalize(self, stack):
        nc = self.nc
        ops = self.ops
        n = len(ops)
        engs = ["pe", "act", "dve", "pool", "sp"]
        last_w, readers = {}, {}
        deps = [None] * n
        dma_prev, dma_slot, ndma = {}, [None] * n, 0
        for j, (eng, fn, reads, writes, dma) in enumerate(ops):
            d = set()
            for k in reads:
                w = last_w.get(k)
                if w is not None:
                    d.add(w)
            for k in writes:
                w = last_w.get(k)
                if w is not None:
                    d.add(w)
                d.update(readers.get(k, ()))
            if dma:
                slot = ndma % self.NDMA
                ndma += 1
                dma_slot[j] = slot
                if slot in dma_prev:
                    d.add(dma_prev[slot])
                dma_prev[slot] = j
            d.discard(j)
            deps[j] = d
            for k in reads:
                readers.setdefault(k, []).append(j)
            for k in writes:
                last_w[k] = j
                readers[k] = []
        needed = [False] * n
        for j in range(n):
            ej = ops[j][0]
            for i in deps[j]:
                if ops[i][4]:
                    continue
                if ops[i][0] == "pe" and ej == "pe" and not ops[j][4]:
                    continue
                needed[i] = True
        esem = {e: stack.enter_context(nc.semaphore("s_" + e)) for e in engs}
        dsem = [stack.enter_context(nc.semaphore("d%d" % i)) for i in range(self.NDMA)]
        ecount = {e: 0 for e in engs}
        dcount = [0] * self.NDMA
        sig = [None] * n
        seen = {e: {} for e in engs}
        snap = [None] * n
        plan = {e: [] for e in engs}
        nwait = 0
        for j, (eng, fn, reads, writes, dma) in enumerate(ops):
            sj = seen[eng]
            pl = plan[eng]
            for i in sorted(deps[j]):
                if (not ops[i][4]) and ops[i][0] == "pe" and eng == "pe" and not dma:
                    continue
                key, val = sig[i]
                if sj.get(key, 0) >= val:
                    continue
                pl.append((0, esem[key] if isinstance(key, str) else dsem[key], val))
                nwait += 1
                sj[key] = val
                for k2, v2 in snap[i].items():
                    if sj.get(k2, 0) < v2:
                        sj[k2] = v2
            if dma:
                slot = dma_slot[j]
                dcount[slot] += 16
                pl.append((1, fn, dsem[slot], 16))
                sig[j] = (slot, dcount[slot])
                snap[j] = dict(sj)
            elif needed[j]:
                ecount[eng] += 1
                pl.append((1, fn, esem[eng], 1))
                sig[j] = (eng, ecount[eng])
                snap[j] = dict(sj)
            else:
                pl.append((1, fn, None, 0))
        for slot in range(self.NDMA):
            if dcount[slot]:
                plan["sp"].append((0, dsem[slot], dcount[slot]))

        def replay(E, pl):
            for it in pl:
                if it[0] == 0:
                    E.wait_ge(it[1], it[2])
                else:
                    ins = it[1](E)
                    if it[2] is not None:
                        ins.then_inc(it[2], it[3])

        block = stack.enter_context(nc.Block())

        @block.tensor
        def _(e):
            replay(e, plan["pe"])

        @block.scalar
        def _(e):
            replay(e, plan["act"])

        @block.vector
        def _(e):
            replay(e, plan["dve"])

        @block.gpsimd
        def _(e):
            replay(e, plan["pool"])

        @block.sync
        def _(e):
            replay(e, plan["sp"])

        return dict(n_ops=n, n_wait=nwait, counts=ecount)


class _Stop(Exception):
    pass


class T:
    def __init__(self, ap, keys):
        self.ap = ap
        self.keys = list(keys)

    def __getitem__(self, idx):
        return T(self.ap[idx], self.keys)


def _keys(ts):
    out = []
    for t in ts:
        if isinstance(t, T):
            out.extend(t.keys)
        else:
            out.append(t)
    return out


def build(cfg=None):
    cfg = cfg or {}
    GROUPS = cfg.get("groups", [0, 1])
    NL = cfg.get("layers", 2)
    DBG = cfg.get("debug", None)

    nc = bass.Bass("TRN2", target_bir_lowering=False)

    def din(name, shape):
        return nc.dram_tensor(name, list(shape), F32, kind="ExternalInput").ap()

    def dout(name, shape):
        return nc.dram_tensor(name, list(shape), F32, kind="ExternalOutput").ap()

    xin = din("xin", [2, NT, D])
    ck = din("ck", [2, 512, 256])
    cv = din("cv", [2, 512, 256])
    cckv = din("cckv", [2, 512, 512])
    ckr = din("ckr", [2, 512, 64])
    h0r_d = din("h0r", [2, 128, 32])
    h0i_d = din("h0i", [2, 128, 32])
    cT_d = din("cT", [128, 16, 2])
    w_mod = din("w_mod", [2, D, 6 * D])
    bmod_d = din("bmodT", [128, 2, 96])
    gmix_d = din("gmixT", [128, 2, 16])
    gmlp_d = din("gmlpT", [128, 2, 16])
    w_in = din("w_in", [2, D, INW])
    sm_d = din("smallp", [128, 2, 16])
    w_uk = din("w_uk", [2, 512, 768])
    w_uv = din("w_uv", [2, 512, 768])
    lamr_d = din("lamr", [128, 2, 32])
    lami_d = din("lami", [128, 2, 32])
    ldt_d = din("ldt", [128, 2, 32])
    Bpad_d = din("Bpad", [2, 16, 128, 4, 128])
    Cpad_d = din("Cpad", [2, 16, 128, 4, 128])
    w_glu = din("w_glu", [2, 512, 1024])
    w_out = din("w_out", [2, D, D])
    w_ff1 = din("w_ff1", [2, D, DFF])
    w_ff2 = din("w_ff2", [2, DFF, D])
    ident_d = din("ident", [128, 128])
    ropeB_d = din("ropeB", [128, 2, NT])
    ropeC_d = din("ropeC", [64, 2, NT])
    permB_d = din("permB", [128, 128])
    permC_d = din("permC", [64, 64])
    jj_d = din("jj", [128, 2, 256])

    y_d = dout("y", [2, NT, D])
    nk_d = dout("nk", [2, NT, 256])
    nv_d = dout("nv", [2, NT, 256])
    nckv_d = dout("nckv", [2, NT, 512])
    nkr_d = dout("nkr", [2, NT, 64])
    nssm_d = dout("nssm", [512, 128])

    tabc = nc.dram_tensor("tabc", [2, 32, 128, 4 * 512], F32, kind="Internal").ap()
    btc = nc.dram_tensor("btc", [2, 16, 128, 4 * 128], BF16, kind="Internal").ap()
    xpark = nc.dram_tensor("xpark", [128, 16384], F32, kind="Internal").ap()
    st = ExitStack()
    P = Prog(nc)

    def CP(name):
        if cfg.get("stop") == name:
            raise _Stop()

    def E(eng, fn, r=(), w=()):
        rk, wk = _keys(r), _keys(w)
        wk = wk + [k for k in rk if isinstance(k, tuple) and k[0] == "ps" and k not in wk]
        P.op(eng, fn, reads=rk, writes=wk)

    def DMA(eng, out, in_, r=(), w=()):
        P.op(eng, lambda e: e.dma_start(out=out, in_=in_), reads=_keys(r), writes=_keys(w), dma=True)

    def sb(name, shape, dt=F32):
        return st.enter_context(nc.sbuf_tensor("sb_" + name, list(shape), dt))

    xT = sb("xT", [128, KT, NT])
    hT = sb("hT", [128, KT, NT], BF16)
    mixT = sb("mixT", [128, KT, NT], BF16)
    NWB = 2
    wts = [sb("wt%d" % i, [128, KT, 256], BF16) for i in range(NWB)]
    ident = sb("ident", [128, 128])
    identb = sb("identb", [128, 128], BF16)
    onesb = sb("onesb", [128, 128], BF16)
    permB = sb("permBs", [128, 128], BF16)
    permC = sb("permCs", [64, 64], BF16)
    ropeB = sb("ropeBs", [128, 2, NT], BF16)
    ropeC = sb("ropeCs", [64, 2, NT], BF16)
    scT = sb("scT", [128, KT, 2], BF16)
    modall = sb("modall", [128, 2, 96, 2])
    bmod = sb("bmod", [128, 2, 96])
    gmix = sb("gmix", [128, 2, 16])
    gmlp = sb("gmlp", [128, 2, 16])
    smallp = sb("smallps", [128, 2, 16])
    mods = sb("mods", [128, 6, 16])
    lamr = sb("lamrs", [128, 2, 32])
    lami = sb("lamis", [128, 2, 32])
    ldt = sb("ldts", [128, 2, 32])
    s5c = sb("s5c", [128, 8, 32])
    hfin = sb("hfin", [128, 512])
    SCRN = 12160
    scr = sb("scr", [128, SCRN])

    def xTt(kt, tc, n=512):
        return T(xT[:, kt, tc * 512:tc * 512 + n], [("xT", kt, tc)])

    def hTt(kt, tc):
        return T(hT[:, kt, tc * 512:(tc + 1) * 512], [("hT", kt, tc)])

    def mixTt(kt, t0, n):
        return T(mixT[:, kt, t0:t0 + n], [("mixT", kt, c) for c in range(t0 // 512, (t0 + n - 1) // 512 + 1)])

    hT_f32 = hT[:].rearrange("p a b -> p (a b)").bitcast(F32)

    xT_flat = xT[:].rearrange("p a b -> p (a b)")

    class Scr:
        def __init__(self, backing=None):
            self.off = 0
            self.hT = backing is True
            self.xT = backing == "x"
            self.base = hT_f32 if self.hT else (xT_flat if self.xT else scr)
            self.cap = 8192 if self.hT else (16384 if self.xT else SCRN)

        def take(self, shape, dt=F32, parts=128):
            n = int(np.prod(shape[1:]))
            nf = n if dt == F32 else (n + 1) // 2
            nf = (nf + 31) // 32 * 32
            assert self.off + nf <= self.cap, ("scratch overflow", self.hT, self.off, nf)
            ap = self.base[0:shape[0], self.off:self.off + nf]
            if self.xT:
                keys = sorted({("xT", o // 1024, (o % 1024) // 512) for o in range(self.off, self.off + nf, 32)} |
                              {("xT", (o + 31) // 1024, ((o + 31) % 1024) // 512) for o in range(self.off, self.off + nf, 32)})
            elif self.hT:
                keys = sorted({("hT", (2 * o) // 1024, ((2 * o) % 1024) // 512) for o in range(self.off, self.off + nf, 32)} |
                              {("hT", (2 * o + 62) // 1024, ((2 * o + 62) % 1024) // 512) for o in range(self.off, self.off + nf, 32)})
            else:
                keys = [("scr", c) for c in range(self.off // 128, (self.off + nf - 1) // 128 + 1)]
            self.off += nf
            if dt != F32:
                ap = ap.bitcast(dt)[:, 0:n]
            else:
                ap = ap[:, 0:n]
            if len(shape) == 3:
                ap = ap.rearrange("p (a b) -> p a b", a=shape[1])
            elif len(shape) == 4:
                ap = ap.rearrange("p (a b c) -> p a b c", a=shape[1], b=shape[2])
            return T(ap, keys)

    psall = st.enter_context(nc.psum_tensor("psall", [128, 8, 512], F32))
    pbanks = [psall[:, i, :] for i in range(8)]
    rotset = {"A": [0, 1, 2, 3, 4], "S": [7]}
    rot = {"A": 0, "S": 0}
    PUMPN = {"n": None, "k": 1}

    def psrot(stream="A"):
        bl = rotset[stream]
        i = bl[rot[stream] % len(bl)]
        rot[stream] += 1
        return T(pbanks[i], [("ps", i)])

    psA = T(pbanks[5], [("ps", 5)])
    psB = T(pbanks[6], [("ps", 6)])
    psC = T(pbanks[7], [("ps", 7)])

    evq = {"i": 0}

    def ev_eng():
        evq["i"] += 1
        return "act" if evq["i"] % 2 else "dve"

    def copy(eng, out, in_, r, w):
        if eng == "act":
            E("act", lambda e: e.activation(out=out, in_=in_, func=AF.Copy), r, w)
        else:
            E(eng, lambda e: e.tensor_copy(out, in_), r, w)

    dbg_n = {"i": 0}

    def dump(name, t, shape):
        if DBG is None:
            return
        d = dout("dbg_" + name, shape)
        DBG.append(name)
        DMA("pool", d, t.ap, r=[t])

    wq = {"i": 0}

    def load_w(src2d, nk, ncols):
        i = wq["i"] % NWB
        wq["i"] += 1
        t = T(wts[i][:], [("wt", i)])
        half = (nk + 1) // 2
        for a, b in ((0, half), (half, nk)):
            if b > a:
                DMA("pool", wts[i][:, a:b, 0:ncols], src2d[a * 128:b * 128, :].rearrange("(kt p) n -> p kt n", p=128), w=[t])
        return t

    def lin_fm(src2d, nk, ncols, rhs_fn, tcs, ntc, consume, cchunks=None):
        wt = load_w(src2d, nk, ncols)
        if cchunks is None:
            cchunks = [(c, min(128, ncols - c)) for c in range(0, ncols, 128)]
        for (c0, cs) in cchunks:
            for tc in tcs:
                ps = psrot()
                rl = [rhs_fn(kt, tc) for kt in range(nk)]

                def mm(e, ps=ps, c0=c0, cs=cs, rl=rl):
                    ins = None
                    for kt in range(nk):
                        ins = e.matmul(ps.ap[0:cs, 0:ntc], wt.ap[:, kt, c0:c0 + cs], rl[kt].ap, start=(kt == 0), stop=(kt == nk - 1))
                    return ins
                E("pe", mm, r=[wt] + rl, w=[ps])
                consume(c0, cs, tc, ps)

    I32 = mybir.dt.int32
    C1 = 6.28125
    C2 = TWO_PI - 6.28125

    def sin_rr(x, shift, out_ap, out_keys, ti, tf, ty, n):
        xk = x.keys
        E("dve", lambda e: e.tensor_scalar(out=ti.ap.bitcast(I32), in0=x.ap, scalar1=shift, scalar2=1.0 / TWO_PI, op0=ALU.add, op1=ALU.mult), r=[x], w=[ti])
        E("dve", lambda e: e.tensor_copy(tf.ap, ti.ap.bitcast(I32)), r=[ti], w=[tf])
        E("dve", lambda e: e.tensor_scalar(out=ty.ap, in0=x.ap, scalar1=shift, scalar2=None, op0=ALU.add), r=[x], w=[ty])
        E("dve", lambda e: e.scalar_tensor_tensor(out=ty.ap, in0=tf.ap, scalar=-C1, in1=ty.ap, op0=ALU.mult, op1=ALU.add), r=[tf, ty], w=[ty])
        E("dve", lambda e: e.scalar_tensor_tensor(out=ty.ap, in0=tf.ap, scalar=-C2, in1=ty.ap, op0=ALU.mult, op1=ALU.add), r=[tf, ty], w=[ty])
        E("dve", lambda e: e.tensor_scalar(out=tf.ap, in0=ty.ap, scalar1=math.pi, scalar2=-TWO_PI, op0=ALU.is_gt, op1=ALU.mult), r=[ty], w=[tf])
        E("dve", lambda e: e.tensor_tensor(out=ty.ap, in0=ty.ap, in1=tf.ap, op=ALU.add), r=[ty, tf], w=[ty])
        E("dve", lambda e: e.tensor_scalar(out=tf.ap, in0=ty.ap, scalar1=-math.pi, scalar2=TWO_PI, op0=ALU.is_lt, op1=ALU.mult), r=[ty], w=[tf])
        E("dve", lambda e: e.tensor_tensor(out=ty.ap, in0=ty.ap, in1=tf.ap, op=ALU.add), r=[ty, tf], w=[ty])
        if out_ap is not None:
            E("act", lambda e: e.activation(out=out_ap, in_=ty.ap, func=AF.Sin), r=[ty], w=out_keys)

    try:
        S = Scr()
        DMA("sp", ident[:], ident_d, w=["ident"])
        idT = T(ident[:], ["ident"])
        identbT = T(identb[:], ["identb"])
        onesT = T(onesb[:], ["onesb"])
        E("dve", lambda e: e.tensor_copy(identb[:], ident[:]), r=["ident"], w=["identb"])
        E("dve", lambda e: e.memset(onesb[:], 1.0), w=["onesb"])
        DMA("pool", permB[:], permB_d, w=["permB"])
        DMA("pool", permC[:], permC_d, w=["permC"])
        DMA("pool", ropeB[:], ropeB_d, w=["ropeB"])
        DMA("pool", ropeC[:], ropeC_d, w=["ropeC"])
        DMA("sp", bmod[:], bmod_d, w=["bmod"])
        DMA("sp", gmix[:], gmix_d, w=["gmix"])
        DMA("sp", gmlp[:], gmlp_d, w=["gmlp"])
        DMA("sp", smallp[:], sm_d, w=["smallp"])
        DMA("sp", lamr[:], lamr_d, w=["lamr"])
        DMA("sp", lami[:], lami_d, w=["lami"])
        DMA("sp", ldt[:], ldt_d, w=["ldt"])
        ctmp = S.take([128, 32])
        DMA("sp", ctmp.ap, cT_d.rearrange("p a b -> p (a b)"), w=[ctmp])
        E("act", lambda e: e.activation(out=scT[:].rearrange("p a b -> p (a b)"), in_=ctmp.ap, func=AF.Silu), r=[ctmp], w=["scT"])
        scTt = T(scT[:], ["scT"])
        E("dve", lambda e: e.memset(hfin[:], 0.0), w=["hfin"])

        def adaln_tile(l, ti):
            wt = load_w(w_mod[l, :, ti * 256:(ti + 1) * 256], 16, 256)
            ps = psrot()

            def mm(e, ps=ps, wt=wt):
                ins = None
                for cj in range(2):
                    for kt in range(16):
                        ins = e.matmul(ps.ap[:, cj * 2:cj * 2 + 2], wt.ap[:, kt, cj * 128:(cj + 1) * 128], scT[:, kt, :], start=(kt == 0), stop=(kt == 15))
                return ins
            E("pe", mm, r=[wt, scTt], w=[ps])
            for cj in range(2):
                n = ti * 2 + cj
                E("dve", lambda e, ps=ps, cj=cj, n=n, l=l: e.tensor_scalar(out=modall[:, l, n, :], in0=ps.ap[:, cj * 2:cj * 2 + 2], scalar1=bmod[:, l, n:n + 1], scalar2=None, op0=ALU.add),
                  r=[ps, "bmod"], w=[("modall", l)])
        ada_pend = [(l, ti) for l in range(NL) for ti in range(48)]

        def ada_drain(n=None, upto_layer=None, upto_ti=None):
            k = 0
            while ada_pend and (n is None or k < n):
                if upto_layer is not None and (ada_pend[0][0] > upto_layer or (ada_pend[0][0] == upto_layer and upto_ti is not None and ada_pend[0][1] >= upto_ti)):
                    break
                adaln_tile(*ada_pend.pop(0))
                k += 1

        CP("pre")
        def sumsq_bc(parts, ntc, ps_out):
            sqs = []
            for (src, k) in parts:
                sq = S2.take([128, ntc], BF16)
                E("act", lambda e, sq=sq, src=src, k=k: e.activation(out=sq.ap[0:k, :], in_=src.ap, func=AF.Square), r=[src], w=[sq])
                sqs.append((sq, k))

            def mm(e):
                ins = None
                for i, (sq, k) in enumerate(sqs):
                    ins = e.matmul(ps_out.ap[:, 0:ntc], onesb[0:k, :], sq.ap[0:k, :], start=(i == 0), stop=(i == len(sqs) - 1))
                return ins
            E("pe", mm, r=[onesT] + [s for s, _ in sqs], w=[ps_out])

        def rstd_from(ps_ss, ntc, dim, out):
            E("dve", lambda e: e.tensor_scalar(out=out.ap, in0=ps_ss.ap[:, 0:ntc], scalar1=1.0 / dim, scalar2=EPS, op0=ALU.mult, op1=ALU.add), r=[ps_ss], w=[out])
            E("act", lambda e: e.activation(out=out.ap, in_=out.ap, func=AF.Ln), r=[out], w=[out])
            E("act", lambda e: e.activation(out=out.ap, in_=out.ap, func=AF.Exp, scale=-0.5), r=[out], w=[out])

        trq = {"i": 0, "bufs": None}

        def tr_out(src_fm, k, ntok, dst_dram_rows, colsl):
            for t0 in range(0, ntok, 128):
                ps = psrot()
                E("pe", lambda e, ps=ps, t0=t0: e.transpose(ps.ap[:, 0:k], src_fm.ap[0:k, t0:t0 + 128], ident[0:k, 0:k]), r=[src_fm, idT], w=[ps])
                o = trq["bufs"][trq["i"] % 2]
                trq["i"] += 1
                copy(ev_eng(), o.ap[:, 0:k], ps.ap[:, 0:k], [ps], [o])
                DMA("sp", dst_dram_rows(t0)[:, colsl], o.ap[:, 0:k], r=[o])

        def norm_to_h(l, g, gsl, ssl):
            for tc in range(2):
                S2.off = 0
                sqs = [S2.take([128, 512], BF16) for _ in range(2)]
                rs = S2.take([128, 512])
                tmps = [S2.take([128, 512]) for _ in range(2)]
                ps_ss = psA
                for kt in range(KT):
                    sq = sqs[kt % 2]
                    x_ = xTt(kt, tc)
                    E("act", lambda e, sq=sq, x_=x_: e.activation(out=sq.ap, in_=x_.ap, func=AF.Square), r=[x_], w=[sq])
                    E("pe", lambda e, sq=sq, kt=kt: e.matmul(ps_ss.ap, onesb[:, :], sq.ap, start=(kt == 0), stop=(kt == KT - 1)), r=[sq, onesT], w=[ps_ss])
                rstd_from(ps_ss, 512, D, rs)
                for kt in range(KT):
                    tmp = tmps[kt % 2]
                    x_ = xTt(kt, tc)
                    h_ = hTt(kt, tc)
                    E("dve", lambda e, tmp=tmp, x_=x_, kt=kt: e.scalar_tensor_tensor(out=tmp.ap, in0=x_.ap, scalar=mods[:, gsl, kt:kt + 1], in1=rs.ap, op0=ALU.mult, op1=ALU.mult),
                      r=[x_, rs, "mods"], w=[tmp])
                    E("act", lambda e, tmp=tmp, h_=h_, kt=kt: e.activation(out=h_.ap, in_=tmp.ap, func=AF.Identity, bias=mods[:, ssl, kt:kt + 1], scale=1.0), r=[tmp, "mods"], w=[h_])

        S2 = Scr()
        S3 = Scr(backing=True)
        SX = Scr(backing="x")

        for g in GROUPS:
            lat = (g == 1)
            S2.off = 0
            xtoks = [S2.take([128, D]) for _ in range(2)]
            for tt in range(8):
                xt = xtoks[tt % 2]
                DMA("sp", xt.ap, xin[g, tt * 128:(tt + 1) * 128, :], w=[xt])
                for k4 in range(4):
                    ps = psrot()

                    def trp(e, ps=ps, xt=xt, k4=k4):
                        ins = None
                        for j in range(4):
                            kt = k4 * 4 + j
                            ins = e.transpose(ps.ap[:, j * 128:(j + 1) * 128], xt.ap[:, kt * 128:(kt + 1) * 128], ident[:])
                        return ins
                    E("pe", trp, r=[xt, idT], w=[ps])
                    wkeys = [("xT", k4 * 4 + j, tt // 4) for j in range(4)]
                    copy(ev_eng(), xT[:, k4 * 4:k4 * 4 + 4, tt * 128:(tt + 1) * 128], ps.ap.rearrange("p (a b) -> p a b", a=4), [ps], wkeys)

            CP("load")
            for l in range(NL):
                ada_drain(upto_layer=l, upto_ti=16)
                mk = ("modall", l)
                E("dve", lambda e, l=l, g=g: e.scalar_tensor_tensor(out=mods[:, 0, :], in0=modall[:, l, 16:32, g], scalar=1.0, in1=gmix[:, l, :], op0=ALU.add, op1=ALU.mult), r=[mk, "gmix"], w=["mods"])
                E("dve", lambda e, l=l, g=g: e.tensor_copy(mods[:, 1, :], modall[:, l, 0:16, g]), r=[mk], w=["mods"])

                norm_to_h(l, g, 0, 1)
                rhs_h = lambda kt, tc: hTt(kt, tc)
                W = w_in[l]
                sp_ = lambda c: smallp[:, l, c:c + 1]
                NK = 1536 if lat else 1024
                KOFF = 512 if lat else 0
                if DBG is not None and l == 0:
                    dump("hT_g%d" % g, T(hT[:, 0, 0:512], [("hT", 0, 0)]), [128, 512])

                xflat = xT[:].rearrange("p a b -> p (a b)")
                for hf in range(2):
                    DMA("sp", xpark[:, hf * 8192:(hf + 1) * 8192], xflat[:, hf * 8192:(hf + 1) * 8192],
                        r=[("xT", kt, tc) for kt in range(hf * 8, hf * 8 + 8) for tc in range(2)], w=[("xpark", hf)])
                SX.off = 0
                uT = SX.take([128, 4, NT], BF16)
                yacc = SX.take([128, 4, NT])
                s5base = SX.off

                def cons_u(c0, cs, tc, ps, blk):
                    ct = blk * 2 + c0 // 128
                    if cfg.get("var") != "a":
                        copy("act", uT.ap[:, ct, tc * 512:(tc + 1) * 512], ps.ap, [ps], [uT])
                    if cfg.get("var") != "b":
                        E("dve", lambda e: e.tensor_scalar(out=yacc.ap[:, ct, tc * 512:(tc + 1) * 512], in0=ps.ap, scalar1=sp_(10 + ct), scalar2=None, op0=ALU.mult), r=[ps, "smallp"], w=[yacc])
                for blk in range(2):
                    lin_fm(W[:, blk * 256:(blk + 1) * 256], KT, 256, rhs_h, [0, 1], 512, lambda c0, cs, tc, ps, blk=blk: cons_u(c0, cs, tc, ps, blk))
                c5 = lambda i: s5c[:, i, :]
                K5 = ["s5c"]
                L_ = lambda nm: {"lamr": lamr, "lami": lami, "ldt": ldt}[nm][:, l, :]
                E("act", lambda e: e.activation(out=c5(6), in_=L_("ldt"), func=AF.Exp), r=["ldt"], w=K5)
                E("dve", lambda e: e.tensor_tensor(out=c5(0), in0=L_("lami"), in1=c5(6), op=ALU.mult), r=["lami"] + K5, w=K5)
                E("dve", lambda e: e.tensor_tensor(out=c5(7), in0=L_("lamr"), in1=c5(6), op=ALU.mult), r=["lamr"] + K5, w=K5)
                E("act", lambda e: e.activation(out=c5(1), in_=c5(7), func=AF.Exp), r=K5, w=K5)
                rti, rtf, rty = SX.take([128, 32]), SX.take([128, 32]), SX.take([128, 32])
                th_all = T(c5(0), K5)
                sin_rr(th_all, 0.0, c5(3), K5, rti, rtf, rty, 32)
                sin_rr(th_all, 0.5 * math.pi, c5(2), K5, rti, rtf, rty, 32)
                cf = SX.take([128, 6, 32])
                cfa = lambda i: cf.ap[:, i, :]
                E("dve", lambda e: e.tensor_tensor(out=cfa(0), in0=c5(1), in1=c5(2), op=ALU.mult), r=K5, w=[cf])
                E("dve", lambda e: e.tensor_scalar(out=cfa(0), in0=cfa(0), scalar1=-1.0, scalar2=None, op0=ALU.add), r=[cf], w=[cf])
                E("dve", lambda e: e.tensor_tensor(out=cfa(1), in0=c5(1), in1=c5(3), op=ALU.mult), r=K5, w=[cf])
                E("dve", lambda e: e.tensor_tensor(out=cfa(2), in0=L_("lamr"), in1=L_("lamr"), op=ALU.mult), r=["lamr"], w=[cf])
                E("dve", lambda e: e.tensor_tensor(out=cfa(3), in0=L_("lami"), in1=L_("lami"), op=ALU.mult), r=["lami"], w=[cf])
                E("dve", lambda e: e.tensor_tensor(out=cfa(2), in0=cfa(2), in1=cfa(3), op=ALU.add), r=[cf], w=[cf])
                E("dve", lambda e: e.reciprocal(cfa(2), cfa(2)), r=[cf], w=[cf])
                E("dve", lambda e: e.tensor_tensor(out=cfa(3), in0=cfa(0), in1=L_("lamr"), op=ALU.mult), r=[cf, "lamr"], w=[cf])
                E("dve", lambda e: e.tensor_tensor(out=cfa(4), in0=cfa(1), in1=L_("lami"), op=ALU.mult), r=[cf, "lami"], w=[cf])
                E("dve", lambda e: e.tensor_tensor(out=cfa(3), in0=cfa(3), in1=cfa(4), op=ALU.add), r=[cf], w=[cf])
                E("dve", lambda e: e.tensor_tensor(out=c5(4), in0=cfa(3), in1=cfa(2), op=ALU.mult), r=[cf], w=K5)
                E("dve", lambda e: e.tensor_tensor(out=cfa(3), in0=cfa(1), in1=L_("lamr"), op=ALU.mult), r=[cf, "lamr"], w=[cf])
                E("dve", lambda e: e.tensor_tensor(out=cfa(4), in0=cfa(0), in1=L_("lami"), op=ALU.mult), r=[cf, "lami"], w=[cf])
                E("dve", lambda e: e.tensor_tensor(out=cfa(3), in0=cfa(3), in1=cfa(4), op=ALU.subtract), r=[cf], w=[cf])
                E("dve", lambda e: e.tensor_tensor(out=c5(5), in0=cfa(3), in1=cfa(2), op=ALU.mult), r=[cf], w=K5)
                init0 = SX.take([128, 2, 32])
                if lat:
                    h0 = SX.take([128, 2, 32])
                    DMA("sp", h0.ap[:, 0, :], h0r_d[l], w=[h0])
                    DMA("sp", h0.ap[:, 1, :], h0i_d[l], w=[h0])
                    E("dve", lambda e: e.tensor_tensor(out=cfa(0), in0=c5(2), in1=h0.ap[:, 0, :], op=ALU.mult), r=K5 + [h0], w=[cf])
                    E("dve", lambda e: e.tensor_tensor(out=cfa(1), in0=c5(3), in1=h0.ap[:, 1, :], op=ALU.mult), r=K5 + [h0], w=[cf])
                    E("dve", lambda e: e.tensor_tensor(out=init0.ap[:, 0, :], in0=cfa(0), in1=cfa(1), op=ALU.subtract), r=[cf], w=[init0])
                    E("dve", lambda e: e.tensor_tensor(out=cfa(0), in0=c5(3), in1=h0.ap[:, 0, :], op=ALU.mult), r=K5 + [h0], w=[cf])
                    E("dve", lambda e: e.tensor_tensor(out=cfa(1), in0=c5(2), in1=h0.ap[:, 1, :], op=ALU.mult), r=K5 + [h0], w=[cf])
                    E("dve", lambda e: e.tensor_tensor(out=init0.ap[:, 1, :], in0=cfa(0), in1=cfa(1), op=ALU.add), r=[cf], w=[init0])
                else:
                    E("dve", lambda e: e.memset(init0.ap, 0.0), w=[init0])
                jj = SX.take([128, 2, 256])
                DMA("sp", jj.ap, jj_d, w=[jj])
                def s5_stream():
                    LCH = 256
                    UN = 512
                    tab4s = [SX.take([128, 4, UN])] * 2
                    chain = SX.take([128, 2, 2])
                    if not lat:
                        maskz = SX.take([128, 2 * UN])
                        E("dve", lambda e: e.memset(maskz.ap, 1.0), w=[maskz])
                        for z in range(4):
                            E("dve", lambda e, z=z: e.memset(maskz.ap[:, z * 256:z * 256 + 1], 0.0), w=[maskz])
                    bu2 = T(psall[:, 3:5, :], [("ps", 3), ("ps", 4)])
                    pend = []

                    def flush():
                        while pend:
                            psy, ct_, t0_ = pend.pop(0)
                            E("dve", lambda e: e.tensor_tensor(out=yacc.ap[:, ct_, t0_:t0_ + UN], in0=yacc.ap[:, ct_, t0_:t0_ + UN], in1=psy.ap, op=ALU.add), r=[psy, yacc], w=[yacc])
                    gp_base = SX.off
                    for gp in range(16):
                        flush()
                        SX.off = gp_base
                        ct = gp // 4
                        Pq = SX.take([128, 4, UN])
                        after_pq = SX.off
                        SX.off = gp_base
                        Bst = SX.take([128, 4, 128])
                        if g == GROUPS[0]:
                            DMA("sp", Bst.ap, Bpad_d[l, gp], w=[Bst])
                        Bb = SX.take([128, 4, 128], BF16)
                        tmpB = SX.take([128, 128])
                        a1 = SX.take([128, LCH])
                        a2 = SX.take([128, LCH])
                        a3 = SX.take([128, LCH])
                        a4 = SX.take([128, LCH])
                        assert SX.off <= after_pq
                        SX.off = after_pq
                        Cw = SX.take([128, 4, 128], BF16)
                        DMA("pool", Cw.ap, Cpad_d[l, gp], w=[Cw])
                        BT = SX.take([128, 4, 128], BF16)
                        rTz = SX.take([128, 2 * UN])
                        Xq = SX.take([128, 2, UN])
                        Gq = SX.take([128, 2, UN])
                        Hq = SX.take([128, 2, UN], BF16)
                        he = SX.take([128, 8])
                        first = (g == GROUPS[0])
                        if not first:
                            DMA("sp", BT.ap.rearrange("p a b -> p (a b)"), btc[l, gp], r=[("btc", l, gp)], w=[BT])
                        for dr in (range(2) if first else ()):
                            ci = dr * 16 + gp
                            cr_, ci_ = s5c[:, 4, ci:ci + 1], s5c[:, 5, ci:ci + 1]
                            br, bi = Bst.ap[:, dr * 2, :], Bst.ap[:, dr * 2 + 1, :]
                            E("dve", lambda e, bi=bi, ci_=ci_: e.tensor_scalar(out=tmpB.ap, in0=bi, scalar1=ci_, scalar2=None, op0=ALU.mult), r=[Bst] + K5, w=[tmpB])
                            E("dve", lambda e, br=br, cr_=cr_, dr=dr: e.scalar_tensor_tensor(out=Bb.ap[:, dr * 2, :], in0=br, scalar=cr_, in1=tmpB.ap, op0=ALU.mult, op1=ALU.subtract), r=[Bst, tmpB] + K5, w=[Bb])
                            E("dve", lambda e, br=br, ci_=ci_: e.tensor_scalar(out=tmpB.ap, in0=br, scalar1=ci_, scalar2=None, op0=ALU.mult), r=[Bst] + K5, w=[tmpB])
                            E("dve", lambda e, bi=bi, cr_=cr_, dr=dr: e.scalar_tensor_tensor(out=Bb.ap[:, dr * 2 + 1, :], in0=bi, scalar=cr_, in1=tmpB.ap, op0=ALU.mult, op1=ALU.add), r=[Bst, tmpB] + K5, w=[Bb])
                        if first:
                            ps = psrot("S")

                            def trB(e, ps=ps):
                                ins = None
                                for q in range(4):
                                    ins = e.matmul(ps.ap[:, q * 128:(q + 1) * 128], Bb.ap[:, q, :], identb[:, :], start=True, stop=True)
                                return ins
                            E("pe", trB, r=[Bb, identbT], w=[ps])
                            copy("act", BT.ap, ps.ap.rearrange("p (a b) -> p a b", a=4), [ps], [BT])
                            DMA("sp", btc[l, gp], BT.ap.rearrange("p a b -> p (a b)"), r=[BT], w=[("btc", l, gp)])
                        yield
                        for dr in range(2):
                            ci = dr * 16 + gp
                            tab4 = tab4s[dr]
                            th = s5c[:, 0, ci:ci + 1]
                            rsc = s5c[:, 1, ci:ci + 1]
                            if first:
                                E("dve", lambda e, th=th, dr=dr: e.tensor_scalar(out=a1.ap, in0=jj.ap[:, dr, :], scalar1=th, scalar2=None, op0=ALU.mult), r=[jj] + K5, w=[a1])
                                bc = lambda t_: t_.ap.unsqueeze(1).broadcast_to([128, 2, LCH])
                                halves = lambda pl: tab4.ap[:, pl, :].rearrange("p (a b) -> p a b", a=2)
                                sin_rr(a1, 0.0, None, None, a2, a3, a4, LCH)
                                E("act", lambda e: e.activation(out=halves(1), in_=bc(a4), func=AF.Sin), r=[a4], w=[tab4])
                                E("act", lambda e: e.activation(out=halves(2), in_=bc(a4), func=AF.Sin, scale=-1.0), r=[a4], w=[tab4])
                                E("dve", lambda e: e.tensor_scalar(out=a2.ap, in0=a4.ap, scalar1=0.5 * math.pi, scalar2=None, op0=ALU.add), r=[a4], w=[a2])
                                E("dve", lambda e: e.tensor_scalar(out=a3.ap, in0=a2.ap, scalar1=math.pi, scalar2=-TWO_PI, op0=ALU.is_gt, op1=ALU.mult), r=[a2], w=[a3])
                                E("dve", lambda e: e.tensor_tensor(out=a2.ap, in0=a2.ap, in1=a3.ap, op=ALU.add), r=[a2, a3], w=[a2])
                                E("act", lambda e: e.activation(out=halves(0), in_=bc(a2), func=AF.Sin), r=[a2], w=[tab4])
                                E("act", lambda e: e.activation(out=halves(3), in_=bc(a2), func=AF.Sin), r=[a2], w=[tab4])
                                DMA("sp", tabc[l, ci], tab4.ap.rearrange("p a b -> p (a b)"), r=[tab4], w=[("tabc", l, ci)])
                            else:
                                DMA("sp", tab4.ap.rearrange("p a b -> p (a b)"), tabc[l, ci], r=[("tabc", l, ci)], w=[tab4])
                            if not lat:
                                E("dve", lambda e, rsc=rsc: e.tensor_scalar(out=rTz.ap, in0=maskz.ap, scalar1=rsc, scalar2=None, op0=ALU.mult), r=[maskz] + K5, w=[rTz])
                            else:
                                E("dve", lambda e, rsc=rsc: e.tensor_scalar(out=rTz.ap[:, 0:LCH], in0=jj.ap[:, 0, :], scalar1=0.0, scalar2=rsc, op0=ALU.mult, op1=ALU.add), r=[jj] + K5, w=[rTz])
                                E("dve", lambda e, dr=dr, ci=ci: e.tensor_copy(chain.ap[:, dr, :], init0.ap[:, :, ci]), r=[init0], w=[chain])
                                le_ = LCH - 1 if dr == 0 else 0
                                cl_, sl__ = tab4.ap[:, 0, le_:le_ + 1], tab4.ap[:, 1, le_:le_ + 1]
                                cth_, sth_ = s5c[:, 2, ci:ci + 1], s5c[:, 3, ci:ci + 1]
                                E("dve", lambda e, sl__=sl__, sth_=sth_: e.tensor_tensor(out=he.ap[:, 2:3], in0=sl__, in1=sth_, op=ALU.mult), r=[tab4] + K5, w=[he])
                                E("dve", lambda e, cl_=cl_, cth_=cth_: e.scalar_tensor_tensor(out=he.ap[:, 4:5], in0=cl_, scalar=cth_, in1=he.ap[:, 2:3], op0=ALU.mult, op1=ALU.subtract), r=[tab4, he] + K5, w=[he])
                                E("dve", lambda e, sl__=sl__, cth_=cth_: e.tensor_tensor(out=he.ap[:, 3:4], in0=sl__, in1=cth_, op=ALU.mult), r=[tab4] + K5, w=[he])
                                E("dve", lambda e, cl_=cl_, sth_=sth_: e.scalar_tensor_tensor(out=he.ap[:, 5:6], in0=cl_, scalar=sth_, in1=he.ap[:, 3:4], op0=ALU.mult, op1=ALU.add), r=[tab4, he] + K5, w=[he])
                            tab2 = tab4.ap.rearrange("p (a b) c -> p a (b c)", a=2)
                            for ui in range(2):
                                u = ui if (dr == 0 or not lat) else 1 - ui
                                t0 = u * UN

                                def mmbu(e, dr=dr, t0=t0):
                                    e.matmul(bu2.ap[:, 0, :], BT.ap[:, dr * 2, :], uT.ap[:, ct, t0:t0 + UN], start=True, stop=True)
                                    return e.matmul(bu2.ap[:, 1, :], BT.ap[:, dr * 2 + 1, :], uT.ap[:, ct, t0:t0 + UN], start=True, stop=True)
                                E("pe", mmbu, r=[BT, uT], w=[bu2])
                                buf = bu2.ap.rearrange("p a b -> p (a b)").unsqueeze(1).broadcast_to([128, 2, 2 * UN])
                                P2 = Pq.ap.rearrange("p (a b) c -> p a (b c)", a=2)
                                E("dve", lambda e, buf=buf, P2=P2, tab2=tab2: e.tensor_tensor(out=P2, in0=buf, in1=tab2, op=ALU.mult), r=[bu2, tab4], w=[Pq])
                                P4 = Pq.ap.rearrange("p (a b) c -> p a b c", a=2)
                                E("dve", lambda e, P4=P4: e.tensor_tensor(out=Xq.ap, in0=P4[:, :, 0, :], in1=P4[:, :, 1, :], op=ALU.add), r=[Pq], w=[Xq])
                                flush()
                                Xf = Xq.ap.rearrange("p a b -> p (a b)")
                                Gf = Gq.ap.rearrange("p a b -> p (a b)")
                                if not lat:
                                    sl = slice(None) if dr == 0 else slice(None, None, -1)
                                    E("dve", lambda e, sl=sl, Xf=Xf, Gf=Gf: e.tensor_tensor_scan(out=Gf[:, sl], data0=rTz.ap, data1=Xf[:, sl], initial=0.0, op0=ALU.mult, op1=ALU.add), r=[Xq, rTz], w=[Gq])
                                else:
                                    for cj in range(2):
                                        c = cj if dr == 0 else 1 - cj
                                        cs0 = c * LCH
                                        sl = slice(cs0, cs0 + LCH) if dr == 0 else slice(cs0 + LCH - 1, cs0 - 1 if cs0 > 0 else None, -1)
                                        for pl in range(2):
                                            E("dve", lambda e, sl=sl, pl=pl, dr=dr: e.tensor_tensor_scan(out=Gq.ap[:, pl, sl], data0=rTz.ap[:, 0:LCH], data1=Xq.ap[:, pl, sl], initial=chain.ap[:, dr, pl:pl + 1], op0=ALU.mult, op1=ALU.add), r=[Xq, rTz, chain], w=[Gq])
                                        le = LCH - 1 if dr == 0 else 0
                                        grl, gil = Gq.ap[:, 0, cs0 + le:cs0 + le + 1], Gq.ap[:, 1, cs0 + le:cs0 + le + 1]
                                        wr_, wi_ = he.ap[:, 4:5], he.ap[:, 5:6]
                                        E("dve", lambda e, gil=gil: e.tensor_tensor(out=he.ap[:, 0:1], in0=gil, in1=wi_, op=ALU.mult), r=[Gq, he], w=[he])
                                        E("dve", lambda e, grl=grl, dr=dr: e.scalar_tensor_tensor(out=chain.ap[:, dr, 0:1], in0=grl, scalar=wr_, in1=he.ap[:, 0:1], op0=ALU.mult, op1=ALU.subtract), r=[Gq, he], w=[chain])
                                        E("dve", lambda e, gil=gil: e.tensor_tensor(out=he.ap[:, 1:2], in0=gil, in1=wr_, op=ALU.mult), r=[Gq, he], w=[he])
                                        E("dve", lambda e, grl=grl, dr=dr: e.scalar_tensor_tensor(out=chain.ap[:, dr, 1:2], in0=grl, scalar=wi_, in1=he.ap[:, 1:2], op0=ALU.mult, op1=ALU.add), r=[Gq, he], w=[chain])
                                gbf = Gf.unsqueeze(1).broadcast_to([128, 2, 2 * UN])
                                E("dve", lambda e, gbf=gbf, P2=P2, tab2=tab2: e.tensor_tensor(out=P2, in0=gbf, in1=tab2, op=ALU.mult), r=[Gq, tab4], w=[Pq])
                                E("dve", lambda e, P4=P4: e.tensor_tensor(out=Hq.ap, in0=P4[:, :, 0, :], in1=P4[:, :, 1, :], op=ALU.subtract), r=[Pq], w=[Hq])
                                if not lat:
                                    le = LCH - 1 if dr == 0 else 0
                                    col = ((l * 4 + 2 * u) * 2 + dr) * 32 + gp * 2
                                    E("dve", lambda e, le=le, col=col: e.tensor_tensor(out=hfin[:, col:col + 65:64], in0=Pq.ap[:, 0, le:le + 257:256], in1=Pq.ap[:, 1, le:le + 257:256], op=ALU.subtract), r=[Pq], w=["hfin"])
                                    E("dve", lambda e, le=le, col=col: e.tensor_tensor(out=hfin[:, col + 1:col + 66:64], in0=Pq.ap[:, 3, le:le + 257:256], in1=Pq.ap[:, 2, le:le + 257:256], op=ALU.subtract), r=[Pq], w=["hfin"])
                                psy = psrot("S")

                                def mmy(e, psy=psy, dr=dr):
                                    e.matmul(psy.ap, Cw.ap[:, dr * 2, :], Hq.ap[:, 0, :], start=True, stop=False)
                                    return e.matmul(psy.ap, Cw.ap[:, dr * 2 + 1, :], Hq.ap[:, 1, :], start=False, stop=True)
                                E("pe", mmy, r=[Cw, Hq], w=[psy])
                                pend.append((psy, ct, t0))
                                flush()
                                ada_drain(1)
                                yield
                    flush()
                    yield
                s5gen = s5_stream()
                s5state = {"done": False}

                def pump(n=1):
                    for _ in range(n):
                        if s5state["done"]:
                            return
                        try:
                            next(s5gen)
                        except StopIteration:
                            s5state["done"] = True
                PUMPN["n"] = pump
                PUMPN["k"] = 4 if lat else 2
                rotset["A"] = [0, 1, 2]
                CP("norm1")
                S2.off = 0
                qT = S2.take([128, 6, NT], BF16)
                kT = S2.take([128, 2, NK], BF16)
                vtok = S2.take([128, NK // 128, 256], BF16)
                trq["bufs"] = [S2.take([128, 128]) for _ in range(2)]
                gq_base = S2.off
                if lat:
                    ctmp = S2.take([128, 4, 256])
                    DMA("sp", ctmp.ap, ck[l].rearrange("(a p) n -> p a n", p=128), w=[ctmp])
                    DMA("pool", vtok.ap[:, 0:4, :], cv[l].rearrange("(a p) n -> p a n", p=128), w=[vtok])
                    for hk in range(2):
                        ps = psrot()

                        def trk(e, ps=ps, hk=hk):
                            ins = None
                            for a in range(4):
                                ins = e.transpose(ps.ap[:, a * 128:(a + 1) * 128], ctmp.ap[:, a, hk * 128:(hk + 1) * 128], ident[:])
                            return ins
                        E("pe", trk, r=[ctmp, idT], w=[ps])
                        copy(ev_eng(), kT.ap[:, hk, 0:512], ps.ap, [ps], [kT])

                def gqa_qk(c0, cs, tc, ps, base):
                    S2.off = base
                    hidx = c0 // 128
                    isq = hidx < 6
                    ps_ss = psrot()
                    sumsq_bc([(ps, 128)], 512, ps_ss)
                    rs = S2.take([128, 512])
                    rstd_from(ps_ss, 512, 128, rs)
                    gcol = sp_(0) if isq else sp_(1)
                    if isq:
                        dst = T(qT.ap[:, hidx, tc * 512:(tc + 1) * 512], qT.keys)
                    else:
                        dst = T(kT.ap[:, hidx - 6, KOFF + tc * 512:KOFF + (tc + 1) * 512], kT.keys)
                    if not lat:
                        if isq:
                            E("dve", lambda e: e.scalar_tensor_tensor(out=dst.ap, in0=ps.ap, scalar=gcol, in1=rs.ap, op0=ALU.mult, op1=ALU.mult), r=[ps, rs, "smallp"], w=[dst])
                        else:
                            kn = S2.take([128, 512])
                            E("dve", lambda e: e.scalar_tensor_tensor(out=kn.ap, in0=ps.ap, scalar=gcol, in1=rs.ap, op0=ALU.mult, op1=ALU.mult), r=[ps, rs, "smallp"], w=[kn])
                            copy("act", dst.ap, kn.ap, [kn], [dst])
                            hk = hidx - 6
                            tr_out(kn, 128, 512, lambda t0: nk_d[l, tc * 512 + t0:tc * 512 + t0 + 128, :], slice(hk * 128, (hk + 1) * 128))
                    else:
                        qn = S2.take([128, 512], BF16)
                        E("dve", lambda e: e.scalar_tensor_tensor(out=qn.ap, in0=ps.ap, scalar=gcol, in1=rs.ap, op0=ALU.mult, op1=ALU.mult), r=[ps, rs, "smallp"], w=[qn])
                        ps2 = psrot()
                        E("pe", lambda e: e.matmul(ps2.ap, permB[:, :], qn.ap, start=True, stop=True), r=[qn, "permB"], w=[ps2])
                        t1 = S2.take([128, 512])
                        E("dve", lambda e: e.tensor_tensor(out=t1.ap, in0=ps2.ap, in1=ropeB[:, 1, tc * 512:(tc + 1) * 512], op=ALU.mult), r=[ps2, "ropeB"], w=[t1])
                        t2 = S2.take([128, 512])
                        E("pool", lambda e: e.tensor_tensor(out=t2.ap, in0=qn.ap, in1=ropeB[:, 0, tc * 512:(tc + 1) * 512], op=ALU.mult), r=[qn, "ropeB"], w=[t2])
                        E("dve", lambda e: e.tensor_tensor(out=dst.ap, in0=t1.ap, in1=t2.ap, op=ALU.add), r=[t1, t2], w=[dst])

                for cblk in range(4):
                    lin_fm(W[:, 512 + cblk * 256:512 + (cblk + 1) * 256], KT, 256, rhs_h, [0, 1], 512,
                           lambda c0, cs, tc, ps, cblk=cblk: gqa_qk(cblk * 256 + c0, cs, tc, ps, gq_base))
                wtv = load_w(W[:, 1536:1792], KT, 256)
                for tt in range(8):
                    S2.off = gq_base
                    ps = psrot()

                    def mmv(e, ps=ps, tt=tt):
                        ins = None
                        for kt in range(KT):
                            ins = e.matmul(ps.ap[:, 0:256], hT[:, kt, tt * 128:(tt + 1) * 128], wtv.ap[:, kt, :], start=(kt == 0), stop=(kt == KT - 1))
                        return ins
                    E("pe", mmv, r=[wtv] + [hTt(kt, tt // 4) for kt in range(KT)], w=[ps])
                    copy("act", vtok.ap[:, KOFF // 128 + tt, :], ps.ap[:, 0:256], [ps], [vtok])
                    if not lat:
                        vo = S2.take([128, 256])
                        copy("dve", vo.ap, ps.ap[:, 0:256], [ps], [vo])
                        DMA("sp", nv_d[l, tt * 128:(tt + 1) * 128, :], vo.ap, r=[vo])

                def attention(nheads, qfn, kfn, vfn, scale, mix0):
                    if lat:
                        blocks = [(0, 512, list(range(12))), (512, 512, list(range(12)))]
                    else:
                        blocks = [(s * 256, 256, [2 * s, 2 * s + 1]) for s in range(4)]
                    abase = S2.off
                    for h in range(nheads):
                        for (q0, nq, ktiles) in blocks:
                            S2.off = abase
                            ets = [S2.take([128, nq], BF16) for _ in range(3)]
                            qparts = qfn(h, q0, nq)
                            nk_ = len(ktiles)
                            prev = None

                            def pv(ki, et, vap, nq=nq, nk_=nk_):
                                E("pe", lambda e: e.matmul(psA.ap[:, 0:nq], vap, et.ap, start=(ki == 0), stop=(ki == nk_ - 1)), r=[et, vfn.keys], w=[psA])
                                E("pe", lambda e: e.matmul(psB.ap[:, 0:nq], onesb[:, :], et.ap, start=(ki == 0), stop=(ki == nk_ - 1)), r=[et, onesT], w=[psB])
                            for ki, kt_ in enumerate(ktiles):
                                ps = psrot()
                                kparts = kfn(h, kt_)

                                def mms(e, ps=ps, kparts=kparts, qparts=qparts, nq=nq):
                                    ins = None
                                    for i, ((kap, kk), (qt_, qk)) in enumerate(zip(kparts, qparts)):
                                        ins = e.matmul(ps.ap[:, 0:nq], kap, qt_.ap, start=(i == 0), stop=(i == len(kparts) - 1))
                                    return ins
                                E("pe", mms, r=[kfn.keys] + [q for q, _ in qparts], w=[ps])
                                et = ets[ki % 3]
                                E("act", lambda e, et=et, ps=ps, nq=nq: e.activation(out=et.ap, in_=ps.ap[:, 0:nq], func=AF.Exp, scale=scale), r=[ps], w=[et])
                                if prev is not None:
                                    pv(*prev)
                                prev = (ki, et, vfn(h, kt_))
                            pv(*prev)

                            rd = S2.take([128, nq])
                            E("dve", lambda e, rd=rd, nq=nq: e.reciprocal(rd.ap, psB.ap[:, 0:nq]), r=[psB], w=[rd])
                            mo = mixTt(mix0 + h, q0, nq)
                            E("dve", lambda e, rd=rd, mo=mo, nq=nq: e.tensor_tensor(out=mo.ap, in0=psA.ap[:, 0:nq], in1=rd.ap, op=ALU.mult), r=[psA, rd], w=[mo])
                            if PUMPN["n"] is not None:
                                PUMPN["n"](PUMPN["k"])

                def q_b(h, q0, nq):
                    return [(T(qT.ap[:, h, q0:q0 + nq], qT.keys), 128)]

                def k_b(h, kt_):
                    return [(kT.ap[:, h // 3, kt_ * 128:(kt_ + 1) * 128], 128)]
                k_b.keys = kT

                def v_b(h, kt_):
                    return vtok.ap[:, kt_, (h // 3) * 128:(h // 3 + 1) * 128]
                v_b.keys = vtok
                S2.off = gq_base
                attention(6, q_b, k_b, v_b, 1.0 / math.sqrt(128.0), 4)
                if DBG is not None and l == 0:
                    dump("mixB_g%d" % g, T(mixT[:, 4, 0:512], [("mixT", 4, 0)]), [128, 512])

                CP("gqa")
                S2.off = 0
                ckvT = S2.take([128, 4, NK], BF16)
                krT = S2.take([64, NK])
                krsq = S2.take([64, NK], BF16)
                trq["bufs"] = [S2.take([128, 128]) for _ in range(2)]
                mla_base = S2.off
                if lat:
                    for a in range(4):
                        S2.off = mla_base
                        ctmp = S2.take([128, 512])
                        DMA("sp", ctmp.ap, cckv[l, a * 128:(a + 1) * 128, :], w=[ctmp])
                        ps = psrot()

                        def trc(e, ps=ps, ctmp=ctmp):
                            ins = None
                            for r4 in range(4):
                                ins = e.transpose(ps.ap[:, r4 * 128:(r4 + 1) * 128], ctmp.ap[:, r4 * 128:(r4 + 1) * 128], ident[:])
                            return ins
                        E("pe", trc, r=[ctmp, idT], w=[ps])
                        copy(ev_eng(), ckvT.ap[:, :, a * 128:(a + 1) * 128], ps.ap.rearrange("p (a b) -> p a b", a=4), [ps], [ckvT])
                        ktmp = S2.take([128, 64])
                        DMA("sp", ktmp.ap, ckr[l, a * 128:(a + 1) * 128, :], w=[ktmp])
                        ps = psrot()
                        E("pe", lambda e, ps=ps, ktmp=ktmp: e.transpose(ps.ap[0:64, 0:128], ktmp.ap, ident[:]), r=[ktmp, idT], w=[ps])
                        copy(ev_eng(), krT.ap[:, a * 128:(a + 1) * 128], ps.ap[0:64, 0:128], [ps], [krT])
                for tc in range(2):
                    S2.off = mla_base
                    raws = [S2.take([128, 512]) for _ in range(4)]

                    def cons_ckv(c0, cs, tc_, ps, raws=raws):
                        copy(ev_eng(), raws[c0 // 128].ap, ps.ap, [ps], [raws[c0 // 128]])
                    for half in range(2):
                        lin_fm(W[:, 2944 + half * 256:2944 + (half + 1) * 256], KT, 256, rhs_h, [tc], 512,
                               lambda c0, cs, tc_, ps, half=half: cons_ckv(half * 256 + c0, cs, tc_, ps))
                    ps_ss = psrot()
                    sumsq_bc([(raws[i], 128) for i in range(4)], 512, ps_ss)
                    rs = S2.take([128, 512])
                    rstd_from(ps_ss, 512, 512, rs)
                    for i in range(4):
                        cn = raws[i]
                        E("dve", lambda e, cn=cn, i=i: e.scalar_tensor_tensor(out=cn.ap, in0=cn.ap, scalar=sp_(2 + i), in1=rs.ap, op0=ALU.mult, op1=ALU.mult), r=[cn, rs, "smallp"], w=[cn])
                        copy("act", ckvT.ap[:, i, KOFF + tc * 512:KOFF + (tc + 1) * 512], cn.ap, [cn], [ckvT])
                        if not lat:
                            tr_out(cn, 128, 512, lambda t0, tc=tc: nckv_d[l, tc * 512 + t0:tc * 512 + t0 + 128, :], slice(i * 128, (i + 1) * 128))

                CP("mla_a")

                def cons_kr(c0, cs, tc, ps):
                    copy("act", krT.ap[:, KOFF + tc * 512:KOFF + (tc + 1) * 512], ps.ap[0:64, :], [ps], [krT])
                    if not lat:
                        S2.off = mla_base
                        kro = T(krT.ap[:, tc * 512:(tc + 1) * 512], krT.keys)
                        tr_out(kro, 64, 512, lambda t0: nkr_d[l, tc * 512 + t0:tc * 512 + t0 + 128, :], slice(0, 64))
                lin_fm(W[:, 3456:3520], KT, 64, rhs_h, [0, 1], 512, cons_kr)
                E("act", lambda e: e.activation(out=krsq.ap, in_=krT.ap, func=AF.Square), r=[krT], w=[krsq])
                CP("mla_b")
                head_base = mla_base
                for h in range(6):
                    S2.off = head_base
                    wuk = S2.take([128, 4, 128], BF16)
                    wuv = S2.take([128, 4, 128], BF16)
                    DMA("pool", wuk.ap, w_uk[l][:, h * 128:(h + 1) * 128].rearrange("(a p) n -> p a n", p=128), w=[wuk])
                    DMA("pool", wuv.ap, w_uv[l][:, h * 128:(h + 1) * 128].rearrange("(a p) n -> p a n", p=128), w=[wuv])
                    knT = S2.take([128, NK], BF16)
                    krn = S2.take([64, NK], BF16)
                    vh = S2.take([128, NK // 128, 128], BF16)
                    qn_n = S2.take([128, NT], BF16)
                    qn_r = S2.take([64, NT], BF16)
                    hb2 = S2.off
                    for kc in range(NK // 512):
                        S2.off = hb2
                        ps = psrot()

                        def mmk(e, ps=ps, kc=kc, h=h):
                            ins = None
                            for r4 in range(4):
                                ins = e.matmul(ps.ap, wuk.ap[:, r4, :], ckvT.ap[:, r4, kc * 512:(kc + 1) * 512], start=(r4 == 0), stop=(r4 == 3))
                            return ins
                        E("pe", mmk, r=[wuk, ckvT], w=[ps])
                        sq = S2.take([128, 512], BF16)
                        E("act", lambda e, sq=sq, ps=ps: e.activation(out=sq.ap, in_=ps.ap, func=AF.Square), r=[ps], w=[sq])
                        ps_ss = psrot()

                        def mmss(e, ps_ss=ps_ss, sq=sq, kc=kc):
                            e.matmul(ps_ss.ap, onesb[:, :], sq.ap, start=True, stop=False)
                            return e.matmul(ps_ss.ap, onesb[0:64, :], krsq.ap[:, kc * 512:(kc + 1) * 512], start=False, stop=True)
                        E("pe", mmss, r=[sq, krsq, onesT], w=[ps_ss])
                        rs = S2.take([128, 512])
                        rstd_from(ps_ss, 512, 192, rs)
                        E("dve", lambda e, ps=ps, rs=rs, kc=kc: e.scalar_tensor_tensor(out=knT.ap[:, kc * 512:(kc + 1) * 512], in0=ps.ap, scalar=sp_(8), in1=rs.ap, op0=ALU.mult, op1=ALU.mult),
                          r=[ps, rs, "smallp"], w=[knT])
                        newtok = lat and kc >= 1
                        if not newtok:
                            E("dve", lambda e, rs=rs, kc=kc: e.scalar_tensor_tensor(out=krn.ap[:, kc * 512:(kc + 1) * 512], in0=krT.ap[:, kc * 512:(kc + 1) * 512], scalar=smallp[0:64, l, 9:10], in1=rs.ap[0:64, :], op0=ALU.mult, op1=ALU.mult),
                              r=[krT, rs, "smallp"], w=[krn])
                        else:
                            tcn = kc - 1
                            kq = S2.take([64, 512], BF16)
                            E("dve", lambda e, rs=rs, kc=kc, kq=kq: e.scalar_tensor_tensor(out=kq.ap, in0=krT.ap[:, kc * 512:(kc + 1) * 512], scalar=smallp[0:64, l, 9:10], in1=rs.ap[0:64, :], op0=ALU.mult, op1=ALU.mult),
                              r=[krT, rs, "smallp"], w=[kq])
                            ps2 = psrot()
                            E("pe", lambda e, ps2=ps2, kq=kq: e.matmul(ps2.ap[0:64, :], permC[:, :], kq.ap, start=True, stop=True), r=[kq, "permC"], w=[ps2])
                            t1 = S2.take([64, 512])
                            E("dve", lambda e, t1=t1, ps2=ps2, tcn=tcn: e.tensor_tensor(out=t1.ap, in0=ps2.ap[0:64, :], in1=ropeC[:, 1, tcn * 512:(tcn + 1) * 512], op=ALU.mult), r=[ps2, "ropeC"], w=[t1])
                            t2 = S2.take([64, 512])
                            E("pool", lambda e, t2=t2, kq=kq, tcn=tcn: e.tensor_tensor(out=t2.ap, in0=kq.ap, in1=ropeC[:, 0, tcn * 512:(tcn + 1) * 512], op=ALU.mult), r=[kq, "ropeC"], w=[t2])
                            E("dve", lambda e, t1=t1, t2=t2, kc=kc: e.tensor_tensor(out=krn.ap[:, kc * 512:(kc + 1) * 512], in0=t1.ap, in1=t2.ap, op=ALU.add), r=[t1, t2], w=[krn])
                    CP("mla_c")
                    for kt_ in range(NK // 128):
                        ps = psrot()

                        def mmv2(e, ps=ps, kt_=kt_, h=h):
                            ins = None
                            for r4 in range(4):
                                ins = e.matmul(ps.ap[:, 0:128], ckvT.ap[:, r4, kt_ * 128:(kt_ + 1) * 128], wuv.ap[:, r4, :], start=(r4 == 0), stop=(r4 == 3))
                            return ins
                        E("pe", mmv2, r=[wuv, ckvT], w=[ps])
                        copy(ev_eng(), vh.ap[:, kt_, :], ps.ap[:, 0:128], [ps], [vh])
                    CP("mla_d")
                    wq_ = load_w(W[:, 1792 + h * 192:1792 + (h + 1) * 192], KT, 192)
                    for tc in range(2):
                        S2.off = hb2
                        psn = psrot()
                        psr = psrot()

                        def mmq(e, psn=psn, psr=psr, tc=tc):
                            ins = None
                            for kt in range(KT):
                                ins = e.matmul(psn.ap, wq_.ap[:, kt, 0:128], hT[:, kt, tc * 512:(tc + 1) * 512], start=(kt == 0), stop=(kt == KT - 1))
                            for kt in range(KT):
                                ins = e.matmul(psr.ap[0:64, :], wq_.ap[:, kt, 128:192], hT[:, kt, tc * 512:(tc + 1) * 512], start=(kt == 0), stop=(kt == KT - 1))
                            return ins
                        E("pe", mmq, r=[wq_] + [hTt(kt, tc) for kt in range(KT)], w=[psn, psr])
                        ps_ss = psrot()
                        sumsq_bc([(psn, 128), (T(psr.ap[0:64, :], psr.keys), 64)], 512, ps_ss)
                        rs = S2.take([128, 512])
                        rstd_from(ps_ss, 512, 192, rs)
                        E("dve", lambda e, psn=psn, rs=rs, tc=tc: e.scalar_tensor_tensor(out=qn_n.ap[:, tc * 512:(tc + 1) * 512], in0=psn.ap, scalar=sp_(6), in1=rs.ap, op0=ALU.mult, op1=ALU.mult),
                          r=[psn, rs, "smallp"], w=[qn_n])
                        if not lat:
                            E("dve", lambda e, psr=psr, rs=rs, tc=tc: e.scalar_tensor_tensor(out=qn_r.ap[:, tc * 512:(tc + 1) * 512], in0=psr.ap[0:64, :], scalar=smallp[0:64, l, 7:8], in1=rs.ap[0:64, :], op0=ALU.mult, op1=ALU.mult),
                              r=[psr, rs, "smallp"], w=[qn_r])
                        else:
                            kq = S2.take([64, 512], BF16)
                            E("dve", lambda e, psr=psr, rs=rs, kq=kq: e.scalar_tensor_tensor(out=kq.ap, in0=psr.ap[0:64, :], scalar=smallp[0:64, l, 7:8], in1=rs.ap[0:64, :], op0=ALU.mult, op1=ALU.mult),
                              r=[psr, rs, "smallp"], w=[kq])
                            ps2 = psrot()
                            E("pe", lambda e, ps2=ps2, kq=kq: e.matmul(ps2.ap[0:64, :], permC[:, :], kq.ap, start=True, stop=True), r=[kq, "permC"], w=[ps2])
                            t1 = S2.take([64, 512])
                            E("dve", lambda e, t1=t1, ps2=ps2, tc=tc: e.tensor_tensor(out=t1.ap, in0=ps2.ap[0:64, :], in1=ropeC[:, 1, tc * 512:(tc + 1) * 512], op=ALU.mult), r=[ps2, "ropeC"], w=[t1])
                            t2 = S2.take([64, 512])
                            E("pool", lambda e, t2=t2, kq=kq, tc=tc: e.tensor_tensor(out=t2.ap, in0=kq.ap, in1=ropeC[:, 0, tc * 512:(tc + 1) * 512], op=ALU.mult), r=[kq, "ropeC"], w=[t2])
                            E("dve", lambda e, t1=t1, t2=t2, tc=tc: e.tensor_tensor(out=qn_r.ap[:, tc * 512:(tc + 1) * 512], in0=t1.ap, in1=t2.ap, op=ALU.add), r=[t1, t2], w=[qn_r])
                    S2.off = hb2
                    CP("mla_e")

                    def q_c(h_, q0, nq):
                        return [(T(qn_n.ap[:, q0:q0 + nq], qn_n.keys), 128), (T(qn_r.ap[:, q0:q0 + nq], qn_r.keys), 64)]

                    def k_c(h_, kt_):
                        return [(knT.ap[:, kt_ * 128:(kt_ + 1) * 128], 128), (krn.ap[:, kt_ * 128:(kt_ + 1) * 128], 64)]
                    k_c.keys = T(None, knT.keys + krn.keys)

                    def v_c(h_, kt_):
                        return vh.ap[:, kt_, :]
                    v_c.keys = vh
                    attention(1, q_c, k_c, v_c, 1.0 / math.sqrt(192.0), 10 + h)
                    CP("mla_f%d" % h)
                if DBG is not None and l == 0:
                    dump("mixC_g%d" % g, T(mixT[:, 10, 0:512], [("mixT", 10, 0)]), [128, 512])

                CP("mla")
                PUMPN["n"](10 ** 9)
                PUMPN["n"] = None
                rotset["A"] = [0, 1, 2, 3, 4]
                CP("s5_d")
                S2.off = 0
                yb = S2.take([128, 4, NT], BF16)
                copy("act", yb.ap, yacc.ap, [yacc], [yb])
                if DBG is not None and l == 0:
                    dump("yacc_g%d" % g, T(yacc.ap[:, 0, 0:512], yacc.keys), [128, 512])
                S3.off = 0
                wg = S3.take([128, 4, 1024], BF16)
                sg = S3.take([128, 512])
                DMA("pool", wg.ap, w_glu[l].rearrange("(a p) n -> p a n", p=128), w=[wg])
                for j in range(4):
                    for tc in range(2):
                        psv, psg = psrot(), psrot()

                        def mmg(e, psv=psv, psg=psg, j=j, tc=tc):
                            ins = None
                            for a in range(4):
                                ins = e.matmul(psv.ap, wg.ap[:, a, j * 128:(j + 1) * 128], yb.ap[:, a, tc * 512:(tc + 1) * 512], start=(a == 0), stop=(a == 3))
                            for a in range(4):
                                ins = e.matmul(psg.ap, wg.ap[:, a, 512 + j * 128:512 + (j + 1) * 128], yb.ap[:, a, tc * 512:(tc + 1) * 512], start=(a == 0), stop=(a == 3))
                            return ins
                        E("pe", mmg, r=[wg, yb], w=[psv, psg])
                        E("act", lambda e, psg=psg, sg=sg: e.activation(out=sg.ap, in_=psg.ap, func=AF.Sigmoid), r=[psg], w=[sg])
                        mo = mixTt(j, tc * 512, 512)
                        E("dve", lambda e, psv=psv, sg=sg, mo=mo: e.tensor_tensor(out=mo.ap, in0=psv.ap, in1=sg.ap, op=ALU.mult), r=[psv, sg], w=[mo])
                if DBG is not None and l == 0:
                    dump("mixA_g%d" % g, T(mixT[:, 0, 0:512], [("mixT", 0, 0)]), [128, 512])

                CP("s5")
                for hf in range(2):
                    DMA("sp", xflat[:, hf * 8192:(hf + 1) * 8192], xpark[:, hf * 8192:(hf + 1) * 8192],
                        r=[("xpark", hf)], w=[("xT", kt, tc) for kt in range(hf * 8, hf * 8 + 8) for tc in range(2)])
                ada_drain(upto_layer=l)
                E("dve", lambda e, l=l, g=g: e.tensor_copy(mods[:, 2, :], modall[:, l, 32:48, g]), r=[mk], w=["mods"])
                E("dve", lambda e, l=l, g=g: e.scalar_tensor_tensor(out=mods[:, 3, :], in0=modall[:, l, 64:80, g], scalar=1.0, in1=gmlp[:, l, :], op0=ALU.add, op1=ALU.mult), r=[mk, "gmlp"], w=["mods"])
                E("dve", lambda e, l=l, g=g: e.tensor_copy(mods[:, 4, :], modall[:, l, 48:64, g]), r=[mk], w=["mods"])
                E("dve", lambda e, l=l, g=g: e.tensor_copy(mods[:, 5, :], modall[:, l, 80:96, g]), r=[mk], w=["mods"])

                def resid(gsl, fc, tc, ps):
                    x_ = xTt(fc, tc)
                    E("dve", lambda e: e.scalar_tensor_tensor(out=x_.ap, in0=ps.ap, scalar=mods[:, gsl, fc:fc + 1], in1=x_.ap, op0=ALU.mult, op1=ALU.add), r=[ps, x_, "mods"], w=[x_])
                rhs_m = lambda kt, tc: mixTt(kt, tc * 512, 512)
                for cb in range(8):
                    lin_fm(w_out[l][:, cb * 256:(cb + 1) * 256], KT, 256, rhs_m, [0, 1], 512, lambda c0, cs, tc, ps, cb=cb: resid(2, cb * 2 + c0 // 128, tc, ps))
                if DBG is not None and l == 0:
                    dump("x1_g%d" % g, T(xT[:, 0, 0:512], [("xT", 0, 0)]), [128, 512])

                CP("out")
                norm_to_h(l, g, 3, 4)
                for qd in range(4):
                    S2.off = 0
                    rl = S2.take([128, 512], BF16)
                    rl2 = S2.take([128, 512], BF16)
                    rls = [rl, rl2]
                    cnt = {"i": 0}

                    def cons_ff1(c0, cs, tc, ps, jb):
                        j = jb * 2 + c0 // 128
                        r_ = rls[cnt["i"] % 2]
                        cnt["i"] += 1
                        E("act", lambda e: e.activation(out=r_.ap, in_=ps.ap, func=AF.Relu), r=[ps], w=[r_])
                        a_ = mixTt(j, tc * 512, 512)
                        E("dve", lambda e: e.tensor_tensor(out=a_.ap, in0=r_.ap, in1=r_.ap, op=ALU.mult), r=[r_], w=[a_])
                    for jb in range(8):
                        lin_fm(w_ff1[l][:, qd * 2048 + jb * 256:qd * 2048 + (jb + 1) * 256], KT, 256, rhs_h, [0, 1], 512, lambda c0, cs, tc, ps, jb=jb: cons_ff1(c0, cs, tc, ps, jb))
                    for cb in range(8):
                        lin_fm(w_ff2[l][qd * 2048:(qd + 1) * 2048, cb * 256:(cb + 1) * 256], KT, 256, rhs_m, [0, 1], 512, lambda c0, cs, tc, ps, cb=cb: resid(5, cb * 2 + c0 // 128, tc, ps))

            S2.off = 0
            ybufs = [S2.take([128, D]) for _ in range(2)]
            for tt in range(8):
                yb_ = ybufs[tt % 2]
                for k4 in range(4):
                    ps = psrot()

                    def trp2(e, ps=ps, k4=k4, tt=tt):
                        ins = None
                        for j in range(4):
                            kt = k4 * 4 + j
                            ins = e.transpose(ps.ap[:, j * 128:(j + 1) * 128], xT[:, kt, tt * 128:(tt + 1) * 128], ident[:])
                        return ins
                    E("pe", trp2, r=[idT] + [("xT", k4 * 4 + j, tt // 4) for j in range(4)], w=[ps])
                    copy(ev_eng(), yb_.ap[:, k4 * 512:(k4 + 1) * 512], ps.ap, [ps], [yb_])
                DMA("sp", y_d[g, tt * 128:(tt + 1) * 128, :], yb_.ap, r=[yb_])

        S2.off = 0
        for q in range(4):
            ps = psrot()
            E("pe", lambda e, ps=ps, q=q: e.transpose(ps.ap[:, 0:128], hfin[:, q * 128:(q + 1) * 128], ident[:]), r=["hfin", idT], w=[ps])
            o = S2.take([128, 128])
            copy("dve", o.ap, ps.ap[:, 0:128], [ps], [o])
            DMA("sp", nssm_d[q * 128:(q + 1) * 128, :], o.ap, r=[o])


    except _Stop:
        pass
    stats = P.finalize(st)
    st.close()
    return nc, stats


def _rope_tables(d):
    half, quarter = d // 2, d // 4
    t = np.arange(NT)
    row, col = (t // 64).astype(np.float32), (t % 64).astype(np.float32)
    inv = (10000.0 ** (-(np.arange(quarter, dtype=np.float32) / quarter))).astype(np.float32)
    cos = np.zeros((d, NT), np.float32)
    sin = np.zeros((d, NT), np.float32)
    perm = np.zeros((d, d), np.float32)
    for dd in range(d):
        pos = row if dd < half else col
        i = dd % quarter
        first = (dd % half) < quarter
        ang = pos * inv[i]
        cos[dd] = np.cos(ang)
        sin[dd] = -np.sin(ang) if first else np.sin(ang)
        partner = dd + quarter if first else dd - quarter
        perm[partner, dd] = 1.0
    return np.stack([cos, sin], axis=1).astype(np.float32), perm


def prep_inputs(inp):
    f = lambda a: np.ascontiguousarray(np.asarray(a, dtype=np.float32))
    sh = {}
    fm = lambda v: f(v.reshape(-1, 128).T)
    sh["w_mod"] = f(inp["w_mod"])
    sh["bmodT"] = f(np.stack([fm(inp["b_mod"][l]) for l in range(2)], axis=1))
    sh["gmixT"] = f(np.stack([fm(inp["norm_mix"][l]) for l in range(2)], axis=1))
    sh["gmlpT"] = f(np.stack([fm(inp["norm_mlp"][l]) for l in range(2)], axis=1))
    sh["w_in"] = f(inp["w_in"])
    sm = np.zeros((128, 2, 16), np.float32)
    for l in range(2):
        sm[:, l, 0] = inp["gqa_q_norm"][l]
        sm[:, l, 1] = inp["gqa_k_norm"][l]
        sm[:, l, 2:6] = fm(inp["mla_kv_norm"][l])
        sm[:, l, 6] = inp["mla_q_norm"][l][:128]
        sm[:64, l, 7] = inp["mla_q_norm"][l][128:]
        sm[:, l, 8] = inp["mla_k_norm"][l][:128]
        sm[:64, l, 9] = inp["mla_k_norm"][l][128:]
        sm[:, l, 10:14] = fm(inp["ssm_d"][l])
    sh["smallp"] = sm
    sh["w_uk"] = f(np.asarray(inp["mla_w_uk"]).reshape(2, 512, 768))
    sh["w_uv"] = f(np.asarray(inp["mla_w_uv"]).reshape(2, 512, 768))

    def st_layout(a):
        a = np.asarray(a, np.float32).reshape(2, 2, 16, 2, 64)
        return f(a.transpose(3, 4, 0, 1, 2).reshape(128, 2, 32))
    sh["lamr"] = st_layout(inp["ssm_lam_re"])
    sh["lami"] = st_layout(inp["ssm_lam_im"])
    sh["ldt"] = st_layout(np.broadcast_to(np.asarray(inp["ssm_log_dt"])[..., None], (2, 2, 32, 64)))
    Bpad = np.zeros((2, 16, 128, 4, 128), np.float32)
    Cpad = np.zeros((2, 16, 128, 4, 128), np.float32)
    bre, bim = np.asarray(inp["ssm_b_re"]), np.asarray(inp["ssm_b_im"])
    cre, cim = np.asarray(inp["ssm_c_re"]), np.asarray(inp["ssm_c_im"])
    for gp in range(16):
        for g2 in range(2):
            gg = gp * 2 + g2
            c0 = (gp % 4) * 32 + g2 * 16
            for dr in range(2):
                Bpad[:, gp, g2 * 64:(g2 + 1) * 64, dr * 2, c0:c0 + 16] = bre[:, dr, gg]
                Bpad[:, gp, g2 * 64:(g2 + 1) * 64, dr * 2 + 1, c0:c0 + 16] = bim[:, dr, gg]
                Cpad[:, gp, g2 * 64:(g2 + 1) * 64, dr * 2, c0:c0 + 16] = cre[:, dr, gg].transpose(0, 2, 1)
                Cpad[:, gp, g2 * 64:(g2 + 1) * 64, dr * 2 + 1, c0:c0 + 16] = cim[:, dr, gg].transpose(0, 2, 1)
    sh["Bpad"], sh["Cpad"] = Bpad, Cpad
    sh["w_glu"] = f(inp["ssm_w_glu"])
    sh["w_out"] = f(inp["w_out"])
    sh["w_ff1"] = f(inp["w_ff1"])
    sh["w_ff2"] = f(inp["w_ff2"])
    sh["ident"] = np.eye(128, dtype=np.float32)
    sh["ropeB"], sh["permB"] = _rope_tables(128)
    sh["ropeC"], sh["permC"] = _rope_tables(64)
    j = np.arange(256, dtype=np.float32)
    sh["jj"] = f(np.broadcast_to(np.stack([j, 255.0 - j])[None], (128, 2, 256)))
    percore = []
    xp, xs = np.asarray(inp["x_prompt"]), np.asarray(inp["x_sample"])
    for i in range(NCORES):
        d = dict(sh)
        d["xin"] = f(np.stack([xp[4 * i:4 * i + 4].reshape(NT, D), xs[i]]))
        d["ck"] = f(np.asarray(inp["cache_attn_k"])[i].reshape(2, 512, 256))
        d["cv"] = f(np.asarray(inp["cache_attn_v"])[i].reshape(2, 512, 256))
        d["cckv"] = f(np.asarray(inp["cache_mla_ckv"])[i])
        d["ckr"] = f(np.asarray(inp["cache_mla_krope"])[i])
        s = np.asarray(inp["state_ssm"])[i].reshape(2, 2, 16, 2, 64, 2)
        s = s.transpose(0, 3, 4, 1, 2, 5).reshape(2, 128, 32, 2)
        d["h0r"], d["h0i"] = f(s[..., 0]), f(s[..., 1])
        cv_ = np.stack([np.asarray(inp["c_ctx"]), np.asarray(inp["c"])[i]])
        d["cT"] = f(cv_.reshape(2, 16, 128).transpose(2, 1, 0))
        percore.append(d)
    return percore


_CACHE = {}


def kernel(**inputs):
    if "nc" not in _CACHE:
        _CACHE["nc"] = build()[0]
    nc = _CACHE["nc"]
    in_maps = prep_inputs(inputs)
    res = run_bass_kernel_spmd(nc, in_maps, core_ids=list(range(NCORES)))
    R = res.results
    y_prompt = np.concatenate([r["y"][0].reshape(4, 256, D) for r in R], axis=0)
    y_sample = np.stack([r["y"][1] for r in R], axis=0)

    def seqout(name, tail):
        return np.concatenate([r[name].reshape(2, 4, 256, -1).transpose(1, 0, 2, 3).reshape((4, 2, 256) + tail) for r in R], axis=0)
    new_k = seqout("nk", (2, 128))
    new_v = seqout("nv", (2, 128))
    new_ckv = seqout("nckv", (512,))
    new_kr = seqout("nkr", (64,))
    ss = []
    for r in R:
        a = r["nssm"].reshape(2, 4, 2, 16, 2, 2, 64)
        a = a.transpose(1, 0, 2, 3, 5, 6, 4).reshape(4, 2, 2, 32, 64, 2)
        ss.append(a)
    new_ssm = np.concatenate(ss, axis=0)
    f = lambda a: np.ascontiguousarray(a, dtype=np.float32)
    return (f(y_prompt), f(y_sample), f(new_k), f(new_v), f(new_ckv), f(new_kr), f(new_ssm))
```

```python
import math
import numpy as np
from contextlib import ExitStack
import concourse.bass as bass
import concourse.mybir as mybir
from concourse.bass_utils import run_bass_kernel_spmd

F32 = mybir.dt.float32
BF16 = mybir.dt.bfloat16
ALU = mybir.AluOpType
AF = mybir.ActivationFunctionType
AX = mybir.AxisListType

D = 2048
NT = 1024
KT = 16
DFF = 8192
INW = 3520
EPS = 1e-6
NCORES = 8
TWO_PI = 2.0 * math.pi


class _Rec:
    def __init__(self):
        self.calls = []

    def __getattr__(self, name):
        def f(*a, **k):
            self.calls.append((name, a, k))
            return None
        return f


class Prog:
    NDMA = 40

    def __init__(self, nc):
        self.nc = nc
        self.ops = []

    def op(self, eng, fn, reads=(), writes=(), dma=False):
        rec = _Rec()
        fn(rec)
        calls = rec.calls
        assert calls

        def emit(E, calls=calls):
            ins = None
            for (name, a, k) in calls:
                ins = getattr(E, name)(*a, **k)
            return ins
        self.ops.append((eng, emit, tuple(reads), tuple(writes), dma))

    def finalize(self, stack):
        nc = self.nc
        ops = self.ops
        n = len(ops)
        engs = ["pe", "act", "dve", "pool", "sp"]
        last_w, readers = {}, {}
        deps = [None] * n
        dma_prev, dma_slot, ndma = {}, [None] * n, 0
        for j, (eng, fn, reads, writes, dma) in enumerate(ops):
            d = set()
            for k in reads:
                w = last_w.get(k)
                if w is not None:
                    d.add(w)
            for k in writes:
                w = last_w.get(k)
                if w is not None:
                    d.add(w)
                d.update(readers.get(k, ()))
            if dma:
                slot = ndma % self.NDMA
                ndma += 1
                dma_slot[j] = slot
                if slot in dma_prev:
                    d.add(dma_prev[slot])
                dma_prev[slot] = j
            d.discard(j)
            deps[j] = d
            for k in reads:
                readers.setdefault(k, []).append(j)
            for k in writes:
                last_w[k] = j
                readers[k] = []
        needed = [False] * n
        for j in range(n):
            ej = ops[j][0]
            for i in deps[j]:
                if ops[i][4]:
                    continue
                if ops[i][0] == "pe" and ej == "pe" and not ops[j][4]:
                    continue
                needed[i] = True
        esem = {e: stack.enter_context(nc.semaphore("s_" + e)) for e in engs}
        dsem = [stack.enter_context(nc.semaphore("d%d" % i)) for i in range(self.NDMA)]
        ecount = {e: 0 for e in engs}
        dcount = [0] * self.NDMA
        sig = [None] * n
        seen = {e: {} for e in engs}
        snap = [None] * n
        plan = {e: [] for e in engs}
        nwait = 0
        for j, (eng, fn, reads, writes, dma) in enumerate(ops):
            sj = seen[eng]
            pl = plan[eng]
            for i in sorted(deps[j]):
                if (not ops[i][4]) and ops[i][0] == "pe" and eng == "pe" and not dma:
                    continue
                key, val = sig[i]
                if sj.get(key, 0) >= val:
                    continue
                pl.append((0, esem[key] if isinstance(key, str) else dsem[key], val))
                nwait += 1
                sj[key] = val
                for k2, v2 in snap[i].items():
                    if sj.get(k2, 0) < v2:
                        sj[k2] = v2
            if dma:
                slot = dma_slot[j]
                dcount[slot] += 16
                pl.append((1, fn, dsem[slot], 16))
                sig[j] = (slot, dcount[slot])
                snap[j] = dict(sj)
            elif needed[j]:
                ecount[eng] += 1
                pl.append((1, fn, esem[eng], 1))
                sig[j] = (eng, ecount[eng])
                snap[j] = dict(sj)
            else:
                pl.append((1, fn, None, 0))
        for slot in range(self.NDMA):
            if dcount[slot]:
                plan["sp"].append((0, dsem[slot], dcount[slot]))

        def replay(E, pl):
            for it in pl:
                if it[0] == 0:
                    E.wait_ge(it[1], it[2])
                else:
                    ins = it[1](E)
                    if it[2] is not None:
                        ins.then_inc(it[2], it[3])

        block = stack.enter_context(nc.Block())

        @block.tensor
        def _(e):
            replay(e, plan["pe"])

        @block.scalar
        def _(e):
            replay(e, plan["act"])

        @block.vector
        def _(e):
            replay(e, plan["dve"])

        @block.gpsimd
        def _(e):
            replay(e, plan["pool"])

        @block.sync
        def _(e):
            replay(e, plan["sp"])

        return dict(n_ops=n, n_wait=nwait, counts=ecount)


class _Stop(Exception):
    pass


class T:
    def __init__(self, ap, keys):
        self.ap = ap
        self.keys = list(keys)

    def __getitem__(self, idx):
        return T(self.ap[idx], self.keys)


def _keys(ts):
    out = []
    for t in ts:
        if isinstance(t, T):
            out.extend(t.keys)
        else:
            out.append(t)
    return out


def build(cfg=None):
    cfg = cfg or {}
    GROUPS = cfg.get("groups", [0, 1])
    NL = cfg.get("layers", 2)
    DBG = cfg.get("debug", None)

    nc = bass.Bass("TRN2", target_bir_lowering=False)

    def din(name, shape):
        return nc.dram_tensor(name, list(shape), F32, kind="ExternalInput").ap()

    def dout(name, shape):
        return nc.dram_tensor(name, list(shape), F32, kind="ExternalOutput").ap()

    xin = din("xin", [2, NT, D])
    ck = din("ck", [2, 512, 256])
    cv = din("cv", [2, 512, 256])
    cckv = din("cckv", [2, 512, 512])
    ckr = din("ckr", [2, 512, 64])
    h0r_d = din("h0r", [2, 128, 32])
    h0i_d = din("h0i", [2, 128, 32])
    cT_d = din("cT", [128, 16, 2])
    w_mod = din("w_mod", [2, D, 6 * D])
    bmod_d = din("bmodT", [128, 2, 96])
    gmix_d = din("gmixT", [128, 2, 16])
    gmlp_d = din("gmlpT", [128, 2, 16])
    w_in = din("w_in", [2, D, INW])
    sm_d = din("smallp", [128, 2, 16])
    w_uk = din("w_uk", [2, 512, 768])
    w_uv = din("w_uv", [2, 512, 768])
    lamr_d = din("lamr", [128, 2, 32])
    lami_d = din("lami", [128, 2, 32])
    ldt_d = din("ldt", [128, 2, 32])
    Bpad_d = din("Bpad", [2, 16, 128, 4, 128])
    Cpad_d = din("Cpad", [2, 16, 128, 4, 128])
    w_glu = din("w_glu", [2, 512, 1024])
    w_out = din("w_out", [2, D, D])
    w_ff1 = din("w_ff1", [2, D, DFF])
    w_ff2 = din("w_ff2", [2, DFF, D])
    ident_d = din("ident", [128, 128])
    ropeB_d = din("ropeB", [128, 2, NT])
    ropeC_d = din("ropeC", [64, 2, NT])
    permB_d = din("permB", [128, 128])
    permC_d = din("permC", [64, 64])
    jj_d = din("jj", [128, 2, 256])

    y_d = dout("y", [2, NT, D])
    nk_d = dout("nk", [2, NT, 256])
    nv_d = dout("nv", [2, NT, 256])
    nckv_d = dout("nckv", [2, NT, 512])
    nkr_d = dout("nkr", [2, NT, 64])
    nssm_d = dout("nssm", [512, 128])

    tabc = nc.dram_tensor("tabc", [2, 32, 128, 4 * 512], F32, kind="Internal").ap()
    btc = nc.dram_tensor("btc", [2, 16, 128, 4 * 128], BF16, kind="Internal").ap()
    xpark = nc.dram_tensor("xpark", [128, 16384], F32, kind="Internal").ap()
    st = ExitStack()
    P = Prog(nc)

    def CP(name):
        if cfg.get("stop") == name:
            raise _Stop()

    def E(eng, fn, r=(), w=()):
        rk, wk = _keys(r), _keys(w)
        wk = wk + [k for k in rk if isinstance(k, tuple) and k[0] == "ps" and k not in wk]
        P.op(eng, fn, reads=rk, writes=wk)

    def DMA(eng, out, in_, r=(), w=()):
        P.op(eng, lambda e: e.dma_start(out=out, in_=in_), reads=_keys(r), writes=_keys(w), dma=True)

    def sb(name, shape, dt=F32):
        return st.enter_context(nc.sbuf_tensor("sb_" + name, list(shape), dt))

    xT = sb("xT", [128, KT, NT])
    hT = sb("hT", [128, KT, NT], BF16)
    mixT = sb("mixT", [128, KT, NT], BF16)
    NWB = 2
    wts = [sb("wt%d" % i, [128, KT, 256], BF16) for i in range(NWB)]
    ident = sb("ident", [128, 128])
    identb = sb("identb", [128, 128], BF16)
    onesb = sb("onesb", [128, 128], BF16)
    permB = sb("permBs", [128, 128], BF16)
    permC = sb("permCs", [64, 64], BF16)
    ropeB = sb("ropeBs", [128, 2, NT], BF16)
    ropeC = sb("ropeCs", [64, 2, NT], BF16)
    scT = sb("scT", [128, KT, 2], BF16)
    modall = sb("modall", [128, 2, 96, 2])
    bmod = sb("bmod", [128, 2, 96])
    gmix = sb("gmix", [128, 2, 16])
    gmlp = sb("gmlp", [128, 2, 16])
    smallp = sb("smallps", [128, 2, 16])
    mods = sb("mods", [128, 6, 16])
    lamr = sb("lamrs", [128, 2, 32])
    lami = sb("lamis", [128, 2, 32])
    ldt = sb("ldts", [128, 2, 32])
    s5c = sb("s5c", [128, 8, 32])
    hfin = sb("hfin", [128, 512])
    SCRN = 12160
    scr = sb("scr", [128, SCRN])

    def xTt(kt, tc, n=512):
        return T(xT[:, kt, tc * 512:tc * 512 + n], [("xT", kt, tc)])

    def hTt(kt, tc):
        return T(hT[:, kt, tc * 512:(tc + 1) * 512], [("hT", kt, tc)])

    def mixTt(kt, t0, n):
        return T(mixT[:, kt, t0:t0 + n], [("mixT", kt, c) for c in range(t0 // 512, (t0 + n - 1) // 512 + 1)])

    hT_f32 = hT[:].rearrange("p a b -> p (a b)").bitcast(F32)

    xT_flat = xT[:].rearrange("p a b -> p (a b)")

    class Scr:
        def __init__(self, backing=None):
            self.off = 0
            self.hT = backing is True
            self.xT = backing == "x"
            self.base = hT_f32 if self.hT else (xT_flat if self.xT else scr)
            self.cap = 8192 if self.hT else (16384 if self.xT else SCRN)

        def take(self, shape, dt=F32, parts=128):
            n = int(np.prod(shape[1:]))
            nf = n if dt == F32 else (n + 1) // 2
            nf = (nf + 31) // 32 * 32
            assert self.off + nf <= self.cap, ("scratch overflow", self.hT, self.off, nf)
            ap = self.base[0:shape[0], self.off:self.off + nf]
            if self.xT:
                keys = sorted({("xT", o // 1024, (o % 1024) // 512) for o in range(self.off, self.off + nf, 32)} |
                              {("xT", (o + 31) // 1024, ((o + 31) % 1024) // 512) for o in range(self.off, self.off + nf, 32)})
            elif self.hT:
                keys = sorted({("hT", (2 * o) // 1024, ((2 * o) % 1024) // 512) for o in range(self.off, self.off + nf, 32)} |
                              {("hT", (2 * o + 62) // 1024, ((2 * o + 62) % 1024) // 512) for o in range(self.off, self.off + nf, 32)})
            else:
                keys = [("scr", c) for c in range(self.off // 128, (self.off + nf - 1) // 128 + 1)]
            self.off += nf
            if dt != F32:
                ap = ap.bitcast(dt)[:, 0:n]
            else:
                ap = ap[:, 0:n]
            if len(shape) == 3:
                ap = ap.rearrange("p (a b) -> p a b", a=shape[1])
            elif len(shape) == 4:
                ap = ap.rearrange("p (a b c) -> p a b c", a=shape[1], b=shape[2])
            return T(ap, keys)

    psall = st.enter_context(nc.psum_tensor("psall", [128, 8, 512], F32))
    pbanks = [psall[:, i, :] for i in range(8)]
    rotset = {"A": [0, 1, 2, 3, 4], "S": [7]}
    rot = {"A": 0, "S": 0}
    PUMPN = {"n": None, "k": 1}

    def psrot(stream="A"):
        bl = rotset[stream]
        i = bl[rot[stream] % len(bl)]
        rot[stream] += 1
        return T(pbanks[i], [("ps", i)])

    psA = T(pbanks[5], [("ps", 5)])
    psB = T(pbanks[6], [("ps", 6)])
    psC = T(pbanks[7], [("ps", 7)])

    evq = {"i": 0}

    def ev_eng():
        evq["i"] += 1
        return "act" if evq["i"] % 2 else "dve"

    def copy(eng, out, in_, r, w):
        if eng == "act":
            E("act", lambda e: e.activation(out=out, in_=in_, func=AF.Copy), r, w)
        else:
            E(eng, lambda e: e.tensor_copy(out, in_), r, w)

    dbg_n = {"i": 0}

    def dump(name, t, shape):
        if DBG is None:
            return
        d = dout("dbg_" + name, shape)
        DBG.append(name)
        DMA("pool", d, t.ap, r=[t])

    wq = {"i": 0}

    def load_w(src2d, nk, ncols):
        i = wq["i"] % NWB
        wq["i"] += 1
        t = T(wts[i][:], [("wt", i)])
        half = (nk + 1) // 2
        for a, b in ((0, half), (half, nk)):
            if b > a:
                DMA("pool", wts[i][:, a:b, 0:ncols], src2d[a * 128:b * 128, :].rearrange("(kt p) n -> p kt n", p=128), w=[t])
        return t

    def lin_fm(src2d, nk, ncols, rhs_fn, tcs, ntc, consume, cchunks=None):
        wt = load_w(src2d, nk, ncols)
        if cchunks is None:
            cchunks = [(c, min(128, ncols - c)) for c in range(0, ncols, 128)]
        for (c0, cs) in cchunks:
            for tc in tcs:
                ps = psrot()
                rl = [rhs_fn(kt, tc) for kt in range(nk)]

                def mm(e, ps=ps, c0=c0, cs=cs, rl=rl):
                    ins = None
                    for kt in range(nk):
                        ins = e.matmul(ps.ap[0:cs, 0:ntc], wt.ap[:, kt, c0:c0 + cs], rl[kt].ap, start=(kt == 0), stop=(kt == nk - 1))
                    return ins
                E("pe", mm, r=[wt] + rl, w=[ps])
                consume(c0, cs, tc, ps)

    I32 = mybir.dt.int32
    C1 = 6.28125
    C2 = TWO_PI - 6.28125

    def sin_rr(x, shift, out_ap, out_keys, ti, tf, ty, n):
        xk = x.keys
        E("dve", lambda e: e.tensor_scalar(out=ti.ap.bitcast(I32), in0=x.ap, scalar1=shift, scalar2=1.0 / TWO_PI, op0=ALU.add, op1=ALU.mult), r=[x], w=[ti])
        E("dve", lambda e: e.tensor_copy(tf.ap, ti.ap.bitcast(I32)), r=[ti], w=[tf])
        E("dve", lambda e: e.tensor_scalar(out=ty.ap, in0=x.ap, scalar1=shift, scalar2=None, op0=ALU.add), r=[x], w=[ty])
        E("dve", lambda e: e.scalar_tensor_tensor(out=ty.ap, in0=tf.ap, scalar=-C1, in1=ty.ap, op0=ALU.mult, op1=ALU.add), r=[tf, ty], w=[ty])
        E("dve", lambda e: e.scalar_tensor_tensor(out=ty.ap, in0=tf.ap, scalar=-C2, in1=ty.ap, op0=ALU.mult, op1=ALU.add), r=[tf, ty], w=[ty])
        E("dve", lambda e: e.tensor_scalar(out=tf.ap, in0=ty.ap, scalar1=math.pi, scalar2=-TWO_PI, op0=ALU.is_gt, op1=ALU.mult), r=[ty], w=[tf])
        E("dve", lambda e: e.tensor_tensor(out=ty.ap, in0=ty.ap, in1=tf.ap, op=ALU.add), r=[ty, tf], w=[ty])
        E("dve", lambda e: e.tensor_scalar(out=tf.ap, in0=ty.ap, scalar1=-math.pi, scalar2=TWO_PI, op0=ALU.is_lt, op1=ALU.mult), r=[ty], w=[tf])
        E("dve", lambda e: e.tensor_tensor(out=ty.ap, in0=ty.ap, in1=tf.ap, op=ALU.add), r=[ty, tf], w=[ty])
        if out_ap is not None:
            E("act", lambda e: e.activation(out=out_ap, in_=ty.ap, func=AF.Sin), r=[ty], w=out_keys)

    try:
        S = Scr()
        DMA("sp", ident[:], ident_d, w=["ident"])
        idT = T(ident[:], ["ident"])
        identbT = T(identb[:], ["identb"])
        onesT = T(onesb[:], ["onesb"])
        E("dve", lambda e: e.tensor_copy(identb[:], ident[:]), r=["ident"], w=["identb"])
        E("dve", lambda e: e.memset(onesb[:], 1.0), w=["onesb"])
        DMA("pool", permB[:], permB_d, w=["permB"])
        DMA("pool", permC[:], permC_d, w=["permC"])
        DMA("pool", ropeB[:], ropeB_d, w=["ropeB"])
        DMA("pool", ropeC[:], ropeC_d, w=["ropeC"])
        DMA("sp", bmod[:], bmod_d, w=["bmod"])
        DMA("sp", gmix[:], gmix_d, w=["gmix"])
        DMA("sp", gmlp[:], gmlp_d, w=["gmlp"])
        DMA("sp", smallp[:], sm_d, w=["smallp"])
        DMA("sp", lamr[:], lamr_d, w=["lamr"])
        DMA("sp", lami[:], lami_d, w=["lami"])
        DMA("sp", ldt[:], ldt_d, w=["ldt"])
        ctmp = S.take([128, 32])
        DMA("sp", ctmp.ap, cT_d.rearrange("p a b -> p (a b)"), w=[ctmp])
        E("act", lambda e: e.activation(out=scT[:].rearrange("p a b -> p (a b)"), in_=ctmp.ap, func=AF.Silu), r=[ctmp], w=["scT"])
        scTt = T(scT[:], ["scT"])
        E("dve", lambda e: e.memset(hfin[:], 0.0), w=["hfin"])

        def adaln_tile(l, ti):
            wt = load_w(w_mod[l, :, ti * 256:(ti + 1) * 256], 16, 256)
            ps = psrot()

            def mm(e, ps=ps, wt=wt):
                ins = None
                for cj in range(2):
                    for kt in range(16):
                        ins = e.matmul(ps.ap[:, cj * 2:cj * 2 + 2], wt.ap[:, kt, cj * 128:(cj + 1) * 128], scT[:, kt, :], start=(kt == 0), stop=(kt == 15))
                return ins
            E("pe", mm, r=[wt, scTt], w=[ps])
            for cj in range(2):
                n = ti * 2 + cj
                E("dve", lambda e, ps=ps, cj=cj, n=n, l=l: e.tensor_scalar(out=modall[:, l, n, :], in0=ps.ap[:, cj * 2:cj * 2 + 2], scalar1=bmod[:, l, n:n + 1], scalar2=None, op0=ALU.add),
                  r=[ps, "bmod"], w=[("modall", l)])
        ada_pend = [(l, ti) for l in range(NL) for ti in range(48)]

        def ada_drain(n=None, upto_layer=None, upto_ti=None):
            k = 0
            while ada_pend and (n is None or k < n):
                if upto_layer is not None and (ada_pend[0][0] > upto_layer or (ada_pend[0][0] == upto_layer and upto_ti is not None and ada_pend[0][1] >= upto_ti)):
                    break
                adaln_tile(*ada_pend.pop(0))
                k += 1

        CP("pre")
        def sumsq_bc(parts, ntc, ps_out):
            sqs = []
            for (src, k) in parts:
                sq = S2.take([128, ntc], BF16)
                E("act", lambda e, sq=sq, src=src, k=k: e.activation(out=sq.ap[0:k, :], in_=src.ap, func=AF.Square), r=[src], w=[sq])
                sqs.append((sq, k))

            def mm(e):
                ins = None
                for i, (sq, k) in enumerate(sqs):
                    ins = e.matmul(ps_out.ap[:, 0:ntc], onesb[0:k, :], sq.ap[0:k, :], start=(i == 0), stop=(i == len(sqs) - 1))
                return ins
            E("pe", mm, r=[onesT] + [s for s, _ in sqs], w=[ps_out])

        def rstd_from(ps_ss, ntc, dim, out):
            E("dve", lambda e: e.tensor_scalar(out=out.ap, in0=ps_ss.ap[:, 0:ntc], scalar1=1.0 / dim, scalar2=EPS, op0=ALU.mult, op1=ALU.add), r=[ps_ss], w=[out])
            E("act", lambda e: e.activation(out=out.ap, in_=out.ap, func=AF.Ln), r=[out], w=[out])
            E("act", lambda e: e.activation(out=out.ap, in_=out.ap, func=AF.Exp, scale=-0.5), r=[out], w=[out])

        trq = {"i": 0, "bufs": None}

        def tr_out(src_fm, k, ntok, dst_dram_rows, colsl):
            for t0 in range(0, ntok, 128):
                ps = psrot()
                E("pe", lambda e, ps=ps, t0=t0: e.transpose(ps.ap[:, 0:k], src_fm.ap[0:k, t0:t0 + 128], ident[0:k, 0:k]), r=[src_fm, idT], w=[ps])
                o = trq["bufs"][trq["i"] % 2]
                trq["i"] += 1
                copy(ev_eng(), o.ap[:, 0:k], ps.ap[:, 0:k], [ps], [o])
                DMA("sp", dst_dram_rows(t0)[:, colsl], o.ap[:, 0:k], r=[o])

        def norm_to_h(l, g, gsl, ssl):
            for tc in range(2):
                S2.off = 0
                sqs = [S2.take([128, 512], BF16) for _ in range(2)]
                rs = S2.take([128, 512])
                tmps = [S2.take([128, 512]) for _ in range(2)]
                ps_ss = psA
                for kt in range(KT):
                    sq = sqs[kt % 2]
                    x_ = xTt(kt, tc)
                    E("act", lambda e, sq=sq, x_=x_: e.activation(out=sq.ap, in_=x_.ap, func=AF.Square), r=[x_], w=[sq])
                    E("pe", lambda e, sq=sq, kt=kt: e.matmul(ps_ss.ap, onesb[:, :], sq.ap, start=(kt == 0), stop=(kt == KT - 1)), r=[sq, onesT], w=[ps_ss])
                rstd_from(ps_ss, 512, D, rs)
                for kt in range(KT):
                    tmp = tmps[kt % 2]
                    x_ = xTt(kt, tc)
                    h_ = hTt(kt, tc)
                    E("dve", lambda e, tmp=tmp, x_=x_, kt=kt: e.scalar_tensor_tensor(out=tmp.ap, in0=x_.ap, scalar=mods[:, gsl, kt:kt + 1], in1=rs.ap, op0=ALU.mult, op1=ALU.mult),
                      r=[x_, rs, "mods"], w=[tmp])
                    E("act", lambda e, tmp=tmp, h_=h_, kt=kt: e.activation(out=h_.ap, in_=tmp.ap, func=AF.Identity, bias=mods[:, ssl, kt:kt + 1], scale=1.0), r=[tmp, "mods"], w=[h_])

        S2 = Scr()
        S3 = Scr(backing=True)
        SX = Scr(backing="x")

        for g in GROUPS:
            lat = (g == 1)
            S2.off = 0
            xtoks = [S2.take([128, D]) for _ in range(2)]
            for tt in range(8):
                xt = xtoks[tt % 2]
                DMA("sp", xt.ap, xin[g, tt * 128:(tt + 1) * 128, :], w=[xt])
                for k4 in range(4):
                    ps = psrot()

                    def trp(e, ps=ps, xt=xt, k4=k4):
                        ins = None
                        for j in range(4):
                            kt = k4 * 4 + j
                            ins = e.transpose(ps.ap[:, j * 128:(j + 1) * 128], xt.ap[:, kt * 128:(kt + 1) * 128], ident[:])
                        return ins
                    E("pe", trp, r=[xt, idT], w=[ps])
                    wkeys = [("xT", k4 * 4 + j, tt // 4) for j in range(4)]
                    copy(ev_eng(), xT[:, k4 * 4:k4 * 4 + 4, tt * 128:(tt + 1) * 128], ps.ap.rearrange("p (a b) -> p a b", a=4), [ps], wkeys)

            CP("load")
            for l in range(NL):
                ada_drain(upto_layer=l, upto_ti=16)
                mk = ("modall", l)
                E("dve", lambda e, l=l, g=g: e.scalar_tensor_tensor(out=mods[:, 0, :], in0=modall[:, l, 16:32, g], scalar=1.0, in1=gmix[:, l, :], op0=ALU.add, op1=ALU.mult), r=[mk, "gmix"], w=["mods"])
                E("dve", lambda e, l=l, g=g: e.tensor_copy(mods[:, 1, :], modall[:, l, 0:16, g]), r=[mk], w=["mods"])

                norm_to_h(l, g, 0, 1)
                rhs_h = lambda kt, tc: hTt(kt, tc)
                W = w_in[l]
                sp_ = lambda c: smallp[:, l, c:c + 1]
                NK = 1536 if lat else 1024
                KOFF = 512 if lat else 0
                if DBG is not None and l == 0:
                    dump("hT_g%d" % g, T(hT[:, 0, 0:512], [("hT", 0, 0)]), [128, 512])

                xflat = xT[:].rearrange("p a b -> p (a b)")
                for hf in range(2):
                    DMA("sp", xpark[:, hf * 8192:(hf + 1) * 8192], xflat[:, hf * 8192:(hf + 1) * 8192],
                        r=[("xT", kt, tc) for kt in range(hf * 8, hf * 8 + 8) for tc in range(2)], w=[("xpark", hf)])
                SX.off = 0
                uT = SX.take([128, 4, NT], BF16)
                yacc = SX.take([128, 4, NT])
                s5base = SX.off

                def cons_u(c0, cs, tc, ps, blk):
                    ct = blk * 2 + c0 // 128
                    if cfg.get("var") != "a":
                        copy("act", uT.ap[:, ct, tc * 512:(tc + 1) * 512], ps.ap, [ps], [uT])
                    if cfg.get("var") != "b":
                        E("dve", lambda e: e.tensor_scalar(out=yacc.ap[:, ct, tc * 512:(tc + 1) * 512], in0=ps.ap, scalar1=sp_(10 + ct), scalar2=None, op0=ALU.mult), r=[ps, "smallp"], w=[yacc])
                for blk in range(2):
                    lin_fm(W[:, blk * 256:(blk + 1) * 256], KT, 256, rhs_h, [0, 1], 512, lambda c0, cs, tc, ps, blk=blk: cons_u(c0, cs, tc, ps, blk))
                c5 = lambda i: s5c[:, i, :]
                K5 = ["s5c"]
                L_ = lambda nm: {"lamr": lamr, "lami": lami, "ldt": ldt}[nm][:, l, :]
                E("act", lambda e: e.activation(out=c5(6), in_=L_("ldt"), func=AF.Exp), r=["ldt"], w=K5)
                E("dve", lambda e: e.tensor_tensor(out=c5(0), in0=L_("lami"), in1=c5(6), op=ALU.mult), r=["lami"] + K5, w=K5)
                E("dve", lambda e: e.tensor_tensor(out=c5(7), in0=L_("lamr"), in1=c5(6), op=ALU.mult), r=["lamr"] + K5, w=K5)
                E("act", lambda e: e.activation(out=c5(1), in_=c5(7), func=AF.Exp), r=K5, w=K5)
                rti, rtf, rty = SX.take([128, 32]), SX.take([128, 32]), SX.take([128, 32])
                th_all = T(c5(0), K5)
                sin_rr(th_all, 0.0, c5(3), K5, rti, rtf, rty, 32)
                sin_rr(th_all, 0.5 * math.pi, c5(2), K5, rti, rtf, rty, 32)
                cf = SX.take([128, 6, 32])
                cfa = lambda i: cf.ap[:, i, :]
                E("dve", lambda e: e.tensor_tensor(out=cfa(0), in0=c5(1), in1=c5(2), op=ALU.mult), r=K5, w=[cf])
                E("dve", lambda e: e.tensor_scalar(out=cfa(0), in0=cfa(0), scalar1=-1.0, scalar2=None, op0=ALU.add), r=[cf], w=[cf])
                E("dve", lambda e: e.tensor_tensor(out=cfa(1), in0=c5(1), in1=c5(3), op=ALU.mult), r=K5, w=[cf])
                E("dve", lambda e: e.tensor_tensor(out=cfa(2), in0=L_("lamr"), in1=L_("lamr"), op=ALU.mult), r=["lamr"], w=[cf])
                E("dve", lambda e: e.tensor_tensor(out=cfa(3), in0=L_("lami"), in1=L_("lami"), op=ALU.mult), r=["lami"], w=[cf])
                E("dve", lambda e: e.tensor_tensor(out=cfa(2), in0=cfa(2), in1=cfa(3), op=ALU.add), r=[cf], w=[cf])
                E("dve", lambda e: e.reciprocal(cfa(2), cfa(2)), r=[cf], w=[cf])
                E("dve", lambda e: e.tensor_tensor(out=cfa(3), in0=cfa(0), in1=L_("lamr"), op=ALU.mult), r=[cf, "lamr"], w=[cf])
                E("dve", lambda e: e.tensor_tensor(out=cfa(4), in0=cfa(1), in1=L_("lami"), op=ALU.mult), r=[cf, "lami"], w=[cf])
                E("dve", lambda e: e.tensor_tensor(out=cfa(3), in0=cfa(3), in1=cfa(4), op=ALU.add), r=[cf], w=[cf])
                E("dve", lambda e: e.tensor_tensor(out=c5(4), in0=cfa(3), in1=cfa(2), op=ALU.mult), r=[cf], w=K5)
                E("dve", lambda e: e.tensor_tensor(out=cfa(3), in0=cfa(1), in1=L_("lamr"), op=ALU.mult), r=[cf, "lamr"], w=[cf])
                E("dve", lambda e: e.tensor_tensor(out=cfa(4), in0=cfa(0), in1=L_("lami"), op=ALU.mult), r=[cf, "lami"], w=[cf])
                E("dve", lambda e: e.tensor_tensor(out=cfa(3), in0=cfa(3), in1=cfa(4), op=ALU.subtract), r=[cf], w=[cf])
                E("dve", lambda e: e.tensor_tensor(out=c5(5), in0=cfa(3), in1=cfa(2), op=ALU.mult), r=[cf], w=K5)
                init0 = SX.take([128, 2, 32])
                if lat:
                    h0 = SX.take([128, 2, 32])
                    DMA("sp", h0.ap[:, 0, :], h0r_d[l], w=[h0])
                    DMA("sp", h0.ap[:, 1, :], h0i_d[l], w=[h0])
                    E("dve", lambda e: e.tensor_tensor(out=cfa(0), in0=c5(2), in1=h0.ap[:, 0, :], op=ALU.mult), r=K5 + [h0], w=[cf])
                    E("dve", lambda e: e.tensor_tensor(out=cfa(1), in0=c5(3), in1=h0.ap[:, 1, :], op=ALU.mult), r=K5 + [h0], w=[cf])
                    E("dve", lambda e: e.tensor_tensor(out=init0.ap[:, 0, :], in0=cfa(0), in1=cfa(1), op=ALU.subtract), r=[cf], w=[init0])
                    E("dve", lambda e: e.tensor_tensor(out=cfa(0), in0=c5(3), in1=h0.ap[:, 0, :], op=ALU.mult), r=K5 + [h0], w=[cf])
                    E("dve", lambda e: e.tensor_tensor(out=cfa(1), in0=c5(2), in1=h0.ap[:, 1, :], op=ALU.mult), r=K5 + [h0], w=[cf])
                    E("dve", lambda e: e.tensor_tensor(out=init0.ap[:, 1, :], in0=cfa(0), in1=cfa(1), op=ALU.add), r=[cf], w=[init0])
                else:
                    E("dve", lambda e: e.memset(init0.ap, 0.0), w=[init0])
                jj = SX.take([128, 2, 256])
                DMA("sp", jj.ap, jj_d, w=[jj])
                def s5_stream():
                    LCH = 256
                    UN = 512
                    tab4s = [SX.take([128, 4, UN])] * 2
                    chain = SX.take([128, 2, 2])
                    if not lat:
                        maskz = SX.take([128, 2 * UN])
                        E("dve", lambda e: e.memset(maskz.ap, 1.0), w=[maskz])
                        for z in range(4):
                            E("dve", lambda e, z=z: e.memset(maskz.ap[:, z * 256:z * 256 + 1], 0.0), w=[maskz])
                    bu2 = T(psall[:, 3:5, :], [("ps", 3), ("ps", 4)])
                    pend = []

                    def flush():
                        while pend:
                            psy, ct_, t0_ = pend.pop(0)
                            E("dve", lambda e: e.tensor_tensor(out=yacc.ap[:, ct_, t0_:t0_ + UN], in0=yacc.ap[:, ct_, t0_:t0_ + UN], in1=psy.ap, op=ALU.add), r=[psy, yacc], w=[yacc])
                    gp_base = SX.off
                    for gp in range(16):
                        flush()
                        SX.off = gp_base
                        ct = gp // 4
                        Pq = SX.take([128, 4, UN])
                        after_pq = SX.off
                        SX.off = gp_base
                        Bst = SX.take([128, 4, 128])
                        if g == GROUPS[0]:
                            DMA("sp", Bst.ap, Bpad_d[l, gp], w=[Bst])
                        Bb = SX.take([128, 4, 128], BF16)
                        tmpB = SX.take([128, 128])
                        a1 = SX.take([128, LCH])
                        a2 = SX.take([128, LCH])
                        a3 = SX.take([128, LCH])
                        a4 = SX.take([128, LCH])
                        assert SX.off <= after_pq
                        SX.off = after_pq
                        Cw = SX.take([128, 4, 128], BF16)
                        DMA("pool", Cw.ap, Cpad_d[l, gp], w=[Cw])
                        BT = SX.take([128, 4, 128], BF16)
                        rTz = SX.take([128, 2 * UN])
                        Xq = SX.take([128, 2, UN])
                        Gq = SX.take([128, 2, UN])
                        Hq = SX.take([128, 2, UN], BF16)
                        he = SX.take([128, 8])
                        first = (g == GROUPS[0])
                        if not first:
                            DMA("sp", BT.ap.rearrange("p a b -> p (a b)"), btc[l, gp], r=[("btc", l, gp)], w=[BT])
                        for dr in (range(2) if first else ()):
                            ci = dr * 16 + gp
                            cr_, ci_ = s5c[:, 4, ci:ci + 1], s5c[:, 5, ci:ci + 1]
                            br, bi = Bst.ap[:, dr * 2, :], Bst.ap[:, dr * 2 + 1, :]
                            E("dve", lambda e, bi=bi, ci_=ci_: e.tensor_scalar(out=tmpB.ap, in0=bi, scalar1=ci_, scalar2=None, op0=ALU.mult), r=[Bst] + K5, w=[tmpB])
                            E("dve", lambda e, br=br, cr_=cr_, dr=dr: e.scalar_tensor_tensor(out=Bb.ap[:, dr * 2, :], in0=br, scalar=cr_, in1=tmpB.ap, op0=ALU.mult, op1=ALU.subtract), r=[Bst, tmpB] + K5, w=[Bb])
                            E("dve", lambda e, br=br, ci_=ci_: e.tensor_scalar(out=tmpB.ap, in0=br, scalar1=ci_, scalar2=None, op0=ALU.mult), r=[Bst] + K5, w=[tmpB])
                            E("dve", lambda e, bi=bi, cr_=cr_, dr=dr: e.scalar_tensor_tensor(out=Bb.ap[:, dr * 2 + 1, :], in0=bi, scalar=cr_, in1=tmpB.ap, op0=ALU.mult, op1=ALU.add), r=[Bst, tmpB] + K5, w=[Bb])
                        if first:
                            ps = psrot("S")

                            def trB(e, ps=ps):
                                ins = None
                                for q in range(4):
                                    ins = e.matmul(ps.ap[:, q * 128:(q + 1) * 128], Bb.ap[:, q, :], identb[:, :], start=True, stop=True)
                                return ins
                            E("pe", trB, r=[Bb, identbT], w=[ps])
                            copy("act", BT.ap, ps.ap.rearrange("p (a b) -> p a b", a=4), [ps], [BT])
                            DMA("sp", btc[l, gp], BT.ap.rearrange("p a b -> p (a b)"), r=[BT], w=[("btc", l, gp)])
                        yield
                        for dr in range(2):
                            ci = dr * 16 + gp
                            tab4 = tab4s[dr]
                            th = s5c[:, 0, ci:ci + 1]
                            rsc = s5c[:, 1, ci:ci + 1]
                            if first:
                                E("dve", lambda e, th=th, dr=dr: e.tensor_scalar(out=a1.ap, in0=jj.ap[:, dr, :], scalar1=th, scalar2=None, op0=ALU.mult), r=[jj] + K5, w=[a1])
                                bc = lambda t_: t_.ap.unsqueeze(1).broadcast_to([128, 2, LCH])
                                halves = lambda pl: tab4.ap[:, pl, :].rearrange("p (a b) -> p a b", a=2)
                                sin_rr(a1, 0.0, None, None, a2, a3, a4, LCH)
                                E("act", lambda e: e.activation(out=halves(1), in_=bc(a4), func=AF.Sin), r=[a4], w=[tab4])
                                E("act", lambda e: e.activation(out=halves(2), in_=bc(a4), func=AF.Sin, scale=-1.0), r=[a4], w=[tab4])
                                E("dve", lambda e: e.tensor_scalar(out=a2.ap, in0=a4.ap, scalar1=0.5 * math.pi, scalar2=None, op0=ALU.add), r=[a4], w=[a2])
                                E("dve", lambda e: e.tensor_scalar(out=a3.ap, in0=a2.ap, scalar1=math.pi, scalar2=-TWO_PI, op0=ALU.is_gt, op1=ALU.mult), r=[a2], w=[a3])
                                E("dve", lambda e: e.tensor_tensor(out=a2.ap, in0=a2.ap, in1=a3.ap, op=ALU.add), r=[a2, a3], w=[a2])
                                E("act", lambda e: e.activation(out=halves(0), in_=bc(a2), func=AF.Sin), r=[a2], w=[tab4])
                                E("act", lambda e: e.activation(out=halves(3), in_=bc(a2), func=AF.Sin), r=[a2], w=[tab4])
                                DMA("sp", tabc[l, ci], tab4.ap.rearrange("p a b -> p (a b)"), r=[tab4], w=[("tabc", l, ci)])
                            else:
                                DMA("sp", tab4.ap.rearrange("p a b -> p (a b)"), tabc[l, ci], r=[("tabc", l, ci)], w=[tab4])
                            if not lat:
                                E("dve", lambda e, rsc=rsc: e.tensor_scalar(out=rTz.ap, in0=maskz.ap, scalar1=rsc, scalar2=None, op0=ALU.mult), r=[maskz] + K5, w=[rTz])
                            else:
                                E("dve", lambda e, rsc=rsc: e.tensor_scalar(out=rTz.ap[:, 0:LCH], in0=jj.ap[:, 0, :], scalar1=0.0, scalar2=rsc, op0=ALU.mult, op1=ALU.add), r=[jj] + K5, w=[rTz])
                                E("dve", lambda e, dr=dr, ci=ci: e.tensor_copy(chain.ap[:, dr, :], init0.ap[:, :, ci]), r=[init0], w=[chain])
                                le_ = LCH - 1 if dr == 0 else 0
                                cl_, sl__ = tab4.ap[:, 0, le_:le_ + 1], tab4.ap[:, 1, le_:le_ + 1]
                                cth_, sth_ = s5c[:, 2, ci:ci + 1], s5c[:, 3, ci:ci + 1]
                                E("dve", lambda e, sl__=sl__, sth_=sth_: e.tensor_tensor(out=he.ap[:, 2:3], in0=sl__, in1=sth_, op=ALU.mult), r=[tab4] + K5, w=[he])
                                E("dve", lambda e, cl_=cl_, cth_=cth_: e.scalar_tensor_tensor(out=he.ap[:, 4:5], in0=cl_, scalar=cth_, in1=he.ap[:, 2:3], op0=ALU.mult, op1=ALU.subtract), r=[tab4, he] + K5, w=[he])
                                E("dve", lambda e, sl__=sl__, cth_=cth_: e.tensor_tensor(out=he.ap[:, 3:4], in0=sl__, in1=cth_, op=ALU.mult), r=[tab4] + K5, w=[he])
                                E("dve", lambda e, cl_=cl_, sth_=sth_: e.scalar_tensor_tensor(out=he.ap[:, 5:6], in0=cl_, scalar=sth_, in1=he.ap[:, 3:4], op0=ALU.mult, op1=ALU.add), r=[tab4, he] + K5, w=[he])
                            tab2 = tab4.ap.rearrange("p (a b) c -> p a (b c)", a=2)
                            for ui in range(2):
                                u = ui if (dr == 0 or not lat) else 1 - ui
                                t0 = u * UN

                                def mmbu(e, dr=dr, t0=t0):
                                    e.matmul(bu2.ap[:, 0, :], BT.ap[:, dr * 2, :], uT.ap[:, ct, t0:t0 + UN], start=True, stop=True)
                                    return e.matmul(bu2.ap[:, 1, :], BT.ap[:, dr * 2 + 1, :], uT.ap[:, ct, t0:t0 + UN], start=True, stop=True)
                                E("pe", mmbu, r=[BT, uT], w=[bu2])
                                buf = bu2.ap.rearrange("p a b -> p (a b)").unsqueeze(1).broadcast_to([128, 2, 2 * UN])
                                P2 = Pq.ap.rearrange("p (a b) c -> p a (b c)", a=2)
                                E("dve", lambda e, buf=buf, P2=P2, tab2=tab2: e.tensor_tensor(out=P2, in0=buf, in1=tab2, op=ALU.mult), r=[bu2, tab4], w=[Pq])
                                P4 = Pq.ap.rearrange("p (a b) c -> p a b c", a=2)
                                E("dve", lambda e, P4=P4: e.tensor_tensor(out=Xq.ap, in0=P4[:, :, 0, :], in1=P4[:, :, 1, :], op=ALU.add), r=[Pq], w=[Xq])
                                flush()
                                Xf = Xq.ap.rearrange("p a b -> p (a b)")
                                Gf = Gq.ap.rearrange("p a b -> p (a b)")
                                if not lat:
                                    sl = slice(None) if dr == 0 else slice(None, None, -1)
                                    E("dve", lambda e, sl=sl, Xf=Xf, Gf=Gf: e.tensor_tensor_scan(out=Gf[:, sl], data0=rTz.ap, data1=Xf[:, sl], initial=0.0, op0=ALU.mult, op1=ALU.add), r=[Xq, rTz], w=[Gq])
                                else:
                                    for cj in range(2):
                                        c = cj if dr == 0 else 1 - cj
                                        cs0 = c * LCH
                                        sl = slice(cs0, cs0 + LCH) if dr == 0 else slice(cs0 + LCH - 1, cs0 - 1 if cs0 > 0 else None, -1)
                                        for pl in range(2):
                                            E("dve", lambda e, sl=sl, pl=pl, dr=dr: e.tensor_tensor_scan(out=Gq.ap[:, pl, sl], data0=rTz.ap[:, 0:LCH], data1=Xq.ap[:, pl, sl], initial=chain.ap[:, dr, pl:pl + 1], op0=ALU.mult, op1=ALU.add), r=[Xq, rTz, chain], w=[Gq])
                                        le = LCH - 1 if dr == 0 else 0
                                        grl, gil = Gq.ap[:, 0, cs0 + le:cs0 + le + 1], Gq.ap[:, 1, cs0 + le:cs0 + le + 1]
                                        wr_, wi_ = he.ap[:, 4:5], he.ap[:, 5:6]
                                        E("dve", lambda e, gil=gil: e.tensor_tensor(out=he.ap[:, 0:1], in0=gil, in1=wi_, op=ALU.mult), r=[Gq, he], w=[he])
                                        E("dve", lambda e, grl=grl, dr=dr: e.scalar_tensor_tensor(out=chain.ap[:, dr, 0:1], in0=grl, scalar=wr_, in1=he.ap[:, 0:1], op0=ALU.mult, op1=ALU.subtract), r=[Gq, he], w=[chain])
                                        E("dve", lambda e, gil=gil: e.tensor_tensor(out=he.ap[:, 1:2], in0=gil, in1=wr_, op=ALU.mult), r=[Gq, he], w=[he])
                                        E("dve", lambda e, grl=grl, dr=dr: e.scalar_tensor_tensor(out=chain.ap[:, dr, 1:2], in0=grl, scalar=wi_, in1=he.ap[:, 1:2], op0=ALU.mult, op1=ALU.add), r=[Gq, he], w=[chain])
                                gbf = Gf.unsqueeze(1).broadcast_to([128, 2, 2 * UN])
                                E("dve", lambda e, gbf=gbf, P2=P2, tab2=tab2: e.tensor_tensor(out=P2, in0=gbf, in1=tab2, op=ALU.mult), r=[Gq, tab4], w=[Pq])
                                E("dve", lambda e, P4=P4: e.tensor_tensor(out=Hq.ap, in0=P4[:, :, 0, :], in1=P4[:, :, 1, :], op=ALU.subtract), r=[Pq], w=[Hq])
                                if not lat:
                                    le = LCH - 1 if dr == 0 else 0
                                    col = ((l * 4 + 2 * u) * 2 + dr) * 32 + gp * 2
                                    E("dve", lambda e, le=le, col=col: e.tensor_tensor(out=hfin[:, col:col + 65:64], in0=Pq.ap[:, 0, le:le + 257:256], in1=Pq.ap[:, 1, le:le + 257:256], op=ALU.subtract), r=[Pq], w=["hfin"])
                                    E("dve", lambda e, le=le, col=col: e.tensor_tensor(out=hfin[:, col + 1:col + 66:64], in0=Pq.ap[:, 3, le:le + 257:256], in1=Pq.ap[:, 2, le:le + 257:256], op=ALU.subtract), r=[Pq], w=["hfin"])
                                psy = psrot("S")

                                def mmy(e, psy=psy, dr=dr):
                                    e.matmul(psy.ap, Cw.ap[:, dr * 2, :], Hq.ap[:, 0, :], start=True, stop=False)
                                    return e.matmul(psy.ap, Cw.ap[:, dr * 2 + 1, :], Hq.ap[:, 1, :], start=False, stop=True)
                                E("pe", mmy, r=[Cw, Hq], w=[psy])
                                pend.append((psy, ct, t0))
                                flush()
                                ada_drain(1)
                                yield
                    flush()
                    yield
                s5gen = s5_stream()
                s5state = {"done": False}

                def pump(n=1):
                    for _ in range(n):
                        if s5state["done"]:
                            return
                        try:
                            next(s5gen)
                        except StopIteration:
                            s5state["done"] = True
                PUMPN["n"] = pump
                PUMPN["k"] = 3 if lat else 1
                rotset["A"] = [0, 1, 2]
                CP("norm1")
                S2.off = 0
                qT = S2.take([128, 6, NT], BF16)
                kT = S2.take([128, 2, NK], BF16)
                vtok = S2.take([128, NK // 128, 256], BF16)
                trq["bufs"] = [S2.take([128, 128]) for _ in range(2)]
                gq_base = S2.off
                if lat:
                    ctmp = S2.take([128, 4, 256])
                    DMA("sp", ctmp.ap, ck[l].rearrange("(a p) n -> p a n", p=128), w=[ctmp])
                    DMA("pool", vtok.ap[:, 0:4, :], cv[l].rearrange("(a p) n -> p a n", p=128), w=[vtok])
                    for hk in range(2):
                        ps = psrot()

                        def trk(e, ps=ps, hk=hk):
                            ins = None
                            for a in range(4):
                                ins = e.transpose(ps.ap[:, a * 128:(a + 1) * 128], ctmp.ap[:, a, hk * 128:(hk + 1) * 128], ident[:])
                            return ins
                        E("pe", trk, r=[ctmp, idT], w=[ps])
                        copy(ev_eng(), kT.ap[:, hk, 0:512], ps.ap, [ps], [kT])

                def gqa_qk(c0, cs, tc, ps, base):
                    S2.off = base
                    hidx = c0 // 128
                    isq = hidx < 6
                    ps_ss = psrot()
                    sumsq_bc([(ps, 128)], 512, ps_ss)
                    rs = S2.take([128, 512])
                    rstd_from(ps_ss, 512, 128, rs)
                    gcol = sp_(0) if isq else sp_(1)
                    if isq:
                        dst = T(qT.ap[:, hidx, tc * 512:(tc + 1) * 512], qT.keys)
                    else:
                        dst = T(kT.ap[:, hidx - 6, KOFF + tc * 512:KOFF + (tc + 1) * 512], kT.keys)
                    if not lat:
                        if isq:
                            E("dve", lambda e: e.scalar_tensor_tensor(out=dst.ap, in0=ps.ap, scalar=gcol, in1=rs.ap, op0=ALU.mult, op1=ALU.mult), r=[ps, rs, "smallp"], w=[dst])
                        else:
                            kn = S2.take([128, 512])
                            E("dve", lambda e: e.scalar_tensor_tensor(out=kn.ap, in0=ps.ap, scalar=gcol, in1=rs.ap, op0=ALU.mult, op1=ALU.mult), r=[ps, rs, "smallp"], w=[kn])
                            copy("act", dst.ap, kn.ap, [kn], [dst])
                            hk = hidx - 6
                            tr_out(kn, 128, 512, lambda t0: nk_d[l, tc * 512 + t0:tc * 512 + t0 + 128, :], slice(hk * 128, (hk + 1) * 128))
                    else:
                        qn = S2.take([128, 512], BF16)
                        E("dve", lambda e: e.scalar_tensor_tensor(out=qn.ap, in0=ps.ap, scalar=gcol, in1=rs.ap, op0=ALU.mult, op1=ALU.mult), r=[ps, rs, "smallp"], w=[qn])
                        ps2 = psrot()
                        E("pe", lambda e: e.matmul(ps2.ap, permB[:, :], qn.ap, start=True, stop=True), r=[qn, "permB"], w=[ps2])
                        t1 = S2.take([128, 512])
                        E("dve", lambda e: e.tensor_tensor(out=t1.ap, in0=ps2.ap, in1=ropeB[:, 1, tc * 512:(tc + 1) * 512], op=ALU.mult), r=[ps2, "ropeB"], w=[t1])
                        t2 = S2.take([128, 512])
                        E("pool", lambda e: e.tensor_tensor(out=t2.ap, in0=qn.ap, in1=ropeB[:, 0, tc * 512:(tc + 1) * 512], op=ALU.mult), r=[qn, "ropeB"], w=[t2])
                        E("dve", lambda e: e.tensor_tensor(out=dst.ap, in0=t1.ap, in1=t2.ap, op=ALU.add), r=[t1, t2], w=[dst])

                for cblk in range(4):
                    lin_fm(W[:, 512 + cblk * 256:512 + (cblk + 1) * 256], KT, 256, rhs_h, [0, 1], 512,
                           lambda c0, cs, tc, ps, cblk=cblk: gqa_qk(cblk * 256 + c0, cs, tc, ps, gq_base))
                wtv = load_w(W[:, 1536:1792], KT, 256)
                for tt in range(8):
                    S2.off = gq_base
                    ps = psrot()

                    def mmv(e, ps=ps, tt=tt):
                        ins = None
                        for kt in range(KT):
                            ins = e.matmul(ps.ap[:, 0:256], hT[:, kt, tt * 128:(tt + 1) * 128], wtv.ap[:, kt, :], start=(kt == 0), stop=(kt == KT - 1))
                        return ins
                    E("pe", mmv, r=[wtv] + [hTt(kt, tt // 4) for kt in range(KT)], w=[ps])
                    copy("act", vtok.ap[:, KOFF // 128 + tt, :], ps.ap[:, 0:256], [ps], [vtok])
                    if not lat:
                        vo = S2.take([128, 256])
                        copy("dve", vo.ap, ps.ap[:, 0:256], [ps], [vo])
                        DMA("sp", nv_d[l, tt * 128:(tt + 1) * 128, :], vo.ap, r=[vo])

                def attention(nheads, qfn, kfn, vfn, scale, mix0):
                    if lat:
                        blocks = [(0, 512, list(range(12))), (512, 512, list(range(12)))]
                    else:
                        blocks = [(s * 256, 256, [2 * s, 2 * s + 1]) for s in range(4)]
                    abase = S2.off
                    for h in range(nheads):
                        for (q0, nq, ktiles) in blocks:
                            S2.off = abase
                            ets = [S2.take([128, nq], BF16) for _ in range(3)]
                            qparts = qfn(h, q0, nq)
                            nk_ = len(ktiles)
                            prev = None

                            def pv(ki, et, vap, nq=nq, nk_=nk_):
                                E("pe", lambda e: e.matmul(psA.ap[:, 0:nq], vap, et.ap, start=(ki == 0), stop=(ki == nk_ - 1)), r=[et, vfn.keys], w=[psA])
                                E("pe", lambda e: e.matmul(psB.ap[:, 0:nq], onesb[:, :], et.ap, start=(ki == 0), stop=(ki == nk_ - 1)), r=[et, onesT], w=[psB])
                            for ki, kt_ in enumerate(ktiles):
                                ps = psrot()
                                kparts = kfn(h, kt_)

                                def mms(e, ps=ps, kparts=kparts, qparts=qparts, nq=nq):
                                    ins = None
                                    for i, ((kap, kk), (qt_, qk)) in enumerate(zip(kparts, qparts)):
                                        ins = e.matmul(ps.ap[:, 0:nq], kap, qt_.ap, start=(i == 0), stop=(i == len(kparts) - 1))
                                    return ins
                                E("pe", mms, r=[kfn.keys] + [q for q, _ in qparts], w=[ps])
                                et = ets[ki % 3]
                                E("act", lambda e, et=et, ps=ps, nq=nq: e.activation(out=et.ap, in_=ps.ap[:, 0:nq], func=AF.Exp, scale=scale), r=[ps], w=[et])
                                if prev is not None:
                                    pv(*prev)
                                prev = (ki, et, vfn(h, kt_))
                                PUMPN["c"] = PUMPN.get("c", 0) + 1
                                if PUMPN["n"] is not None and PUMPN["c"] % PUMPN["k"] == 0:
                                    PUMPN["n"](1)
                            pv(*prev)

                            rd = S2.take([128, nq])
                            E("dve", lambda e, rd=rd, nq=nq: e.reciprocal(rd.ap, psB.ap[:, 0:nq]), r=[psB], w=[rd])
                            mo = mixTt(mix0 + h, q0, nq)
                            E("dve", lambda e, rd=rd, mo=mo, nq=nq: e.tensor_tensor(out=mo.ap, in0=psA.ap[:, 0:nq], in1=rd.ap, op=ALU.mult), r=[psA, rd], w=[mo])

                def q_b(h, q0, nq):
                    return [(T(qT.ap[:, h, q0:q0 + nq], qT.keys), 128)]

                def k_b(h, kt_):
                    return [(kT.ap[:, h // 3, kt_ * 128:(kt_ + 1) * 128], 128)]
                k_b.keys = kT

                def v_b(h, kt_):
                    return vtok.ap[:, kt_, (h // 3) * 128:(h // 3 + 1) * 128]
                v_b.keys = vtok
                S2.off = gq_base
                attention(6, q_b, k_b, v_b, 1.0 / math.sqrt(128.0), 4)
                if DBG is not None and l == 0:
                    dump("mixB_g%d" % g, T(mixT[:, 4, 0:512], [("mixT", 4, 0)]), [128, 512])

                CP("gqa")
                S2.off = 0
                ckvT = S2.take([128, 4, NK], BF16)
                krT = S2.take([64, NK])
                krsq = S2.take([64, NK], BF16)
                trq["bufs"] = [S2.take([128, 128]) for _ in range(2)]
                mla_base = S2.off
                if lat:
                    for a in range(4):
                        S2.off = mla_base
                        ctmp = S2.take([128, 512])
                        DMA("sp", ctmp.ap, cckv[l, a * 128:(a + 1) * 128, :], w=[ctmp])
                        ps = psrot()

                        def trc(e, ps=ps, ctmp=ctmp):
                            ins = None
                            for r4 in range(4):
                                ins = e.transpose(ps.ap[:, r4 * 128:(r4 + 1) * 128], ctmp.ap[:, r4 * 128:(r4 + 1) * 128], ident[:])
                            return ins
                        E("pe", trc, r=[ctmp, idT], w=[ps])
                        copy(ev_eng(), ckvT.ap[:, :, a * 128:(a + 1) * 128], ps.ap.rearrange("p (a b) -> p a b", a=4), [ps], [ckvT])
                        ktmp = S2.take([128, 64])
                        DMA("sp", ktmp.ap, ckr[l, a * 128:(a + 1) * 128, :], w=[ktmp])
                        ps = psrot()
                        E("pe", lambda e, ps=ps, ktmp=ktmp: e.transpose(ps.ap[0:64, 0:128], ktmp.ap, ident[:]), r=[ktmp, idT], w=[ps])
                        copy(ev_eng(), krT.ap[:, a * 128:(a + 1) * 128], ps.ap[0:64, 0:128], [ps], [krT])
                for tc in range(2):
                    S2.off = mla_base
                    raws = [S2.take([128, 512]) for _ in range(4)]

                    def cons_ckv(c0, cs, tc_, ps, raws=raws):
                        copy(ev_eng(), raws[c0 // 128].ap, ps.ap, [ps], [raws[c0 // 128]])
                    for half in range(2):
                        lin_fm(W[:, 2944 + half * 256:2944 + (half + 1) * 256], KT, 256, rhs_h, [tc], 512,
                               lambda c0, cs, tc_, ps, half=half: cons_ckv(half * 256 + c0, cs, tc_, ps))
                    ps_ss = psrot()
                    sumsq_bc([(raws[i], 128) for i in range(4)], 512, ps_ss)
                    rs = S2.take([128, 512])
                    rstd_from(ps_ss, 512, 512, rs)
                    for i in range(4):
                        cn = raws[i]
                        E("dve", lambda e, cn=cn, i=i: e.scalar_tensor_tensor(out=cn.ap, in0=cn.ap, scalar=sp_(2 + i), in1=rs.ap, op0=ALU.mult, op1=ALU.mult), r=[cn, rs, "smallp"], w=[cn])
                        copy("act", ckvT.ap[:, i, KOFF + tc * 512:KOFF + (tc + 1) * 512], cn.ap, [cn], [ckvT])
                        if not lat:
                            tr_out(cn, 128, 512, lambda t0, tc=tc: nckv_d[l, tc * 512 + t0:tc * 512 + t0 + 128, :], slice(i * 128, (i + 1) * 128))

                CP("mla_a")

                def cons_kr(c0, cs, tc, ps):
                    copy("act", krT.ap[:, KOFF + tc * 512:KOFF + (tc + 1) * 512], ps.ap[0:64, :], [ps], [krT])
                    if not lat:
                        S2.off = mla_base
                        kro = T(krT.ap[:, tc * 512:(tc + 1) * 512], krT.keys)
                        tr_out(kro, 64, 512, lambda t0: nkr_d[l, tc * 512 + t0:tc * 512 + t0 + 128, :], slice(0, 64))
                lin_fm(W[:, 3456:3520], KT, 64, rhs_h, [0, 1], 512, cons_kr)
                E("act", lambda e: e.activation(out=krsq.ap, in_=krT.ap, func=AF.Square), r=[krT], w=[krsq])
                CP("mla_b")
                head_base = mla_base
                for h in range(6):
                    S2.off = head_base
                    wuk = S2.take([128, 4, 128], BF16)
                    wuv = S2.take([128, 4, 128], BF16)
                    DMA("pool", wuk.ap, w_uk[l][:, h * 128:(h + 1) * 128].rearrange("(a p) n -> p a n", p=128), w=[wuk])
                    DMA("pool", wuv.ap, w_uv[l][:, h * 128:(h + 1) * 128].rearrange("(a p) n -> p a n", p=128), w=[wuv])
                    knT = S2.take([128, NK], BF16)
                    krn = S2.take([64, NK], BF16)
                    vh = S2.take([128, NK // 128, 128], BF16)
                    qn_n = S2.take([128, NT], BF16)
                    qn_r = S2.take([64, NT], BF16)
                    hb2 = S2.off
                    for kc in range(NK // 512):
                        S2.off = hb2
                        ps = psrot()

                        def mmk(e, ps=ps, kc=kc, h=h):
                            ins = None
                            for r4 in range(4):
                                ins = e.matmul(ps.ap, wuk.ap[:, r4, :], ckvT.ap[:, r4, kc * 512:(kc + 1) * 512], start=(r4 == 0), stop=(r4 == 3))
                            return ins
                        E("pe", mmk, r=[wuk, ckvT], w=[ps])
                        sq = S2.take([128, 512], BF16)
                        E("act", lambda e, sq=sq, ps=ps: e.activation(out=sq.ap, in_=ps.ap, func=AF.Square), r=[ps], w=[sq])
                        ps_ss = psrot()

                        def mmss(e, ps_ss=ps_ss, sq=sq, kc=kc):
                            e.matmul(ps_ss.ap, onesb[:, :], sq.ap, start=True, stop=False)
                            return e.matmul(ps_ss.ap, onesb[0:64, :], krsq.ap[:, kc * 512:(kc + 1) * 512], start=False, stop=True)
                        E("pe", mmss, r=[sq, krsq, onesT], w=[ps_ss])
                        rs = S2.take([128, 512])
                        rstd_from(ps_ss, 512, 192, rs)
                        E("dve", lambda e, ps=ps, rs=rs, kc=kc: e.scalar_tensor_tensor(out=knT.ap[:, kc * 512:(kc + 1) * 512], in0=ps.ap, scalar=sp_(8), in1=rs.ap, op0=ALU.mult, op1=ALU.mult),
                          r=[ps, rs, "smallp"], w=[knT])
                        newtok = lat and kc >= 1
                        if not newtok:
                            E("dve", lambda e, rs=rs, kc=kc: e.scalar_tensor_tensor(out=krn.ap[:, kc * 512:(kc + 1) * 512], in0=krT.ap[:, kc * 512:(kc + 1) * 512], scalar=smallp[0:64, l, 9:10], in1=rs.ap[0:64, :], op0=ALU.mult, op1=ALU.mult),
                              r=[krT, rs, "smallp"], w=[krn])
                        else:
                            tcn = kc - 1
                            kq = S2.take([64, 512], BF16)
                            E("dve", lambda e, rs=rs, kc=kc, kq=kq: e.scalar_tensor_tensor(out=kq.ap, in0=krT.ap[:, kc * 512:(kc + 1) * 512], scalar=smallp[0:64, l, 9:10], in1=rs.ap[0:64, :], op0=ALU.mult, op1=ALU.mult),
                              r=[krT, rs, "smallp"], w=[kq])
                            ps2 = psrot()
                            E("pe", lambda e, ps2=ps2, kq=kq: e.matmul(ps2.ap[0:64, :], permC[:, :], kq.ap, start=True, stop=True), r=[kq, "permC"], w=[ps2])
                            t1 = S2.take([64, 512])
                            E("dve", lambda e, t1=t1, ps2=ps2, tcn=tcn: e.tensor_tensor(out=t1.ap, in0=ps2.ap[0:64, :], in1=ropeC[:, 1, tcn * 512:(tcn + 1) * 512], op=ALU.mult), r=[ps2, "ropeC"], w=[t1])
                            t2 = S2.take([64, 512])
                            E("pool", lambda e, t2=t2, kq=kq, tcn=tcn: e.tensor_tensor(out=t2.ap, in0=kq.ap, in1=ropeC[:, 0, tcn * 512:(tcn + 1) * 512], op=ALU.mult), r=[kq, "ropeC"], w=[t2])
                            E("dve", lambda e, t1=t1, t2=t2, kc=kc: e.tensor_tensor(out=krn.ap[:, kc * 512:(kc + 1) * 512], in0=t1.ap, in1=t2.ap, op=ALU.add), r=[t1, t2], w=[krn])
                    CP("mla_c")
                    for kt_ in range(NK // 128):
                        ps = psrot()

                        def mmv2(e, ps=ps, kt_=kt_, h=h):
                            ins = None
                            for r4 in range(4):
                                ins = e.matmul(ps.ap[:, 0:128], ckvT.ap[:, r4, kt_ * 128:(kt_ + 1) * 128], wuv.ap[:, r4, :], start=(r4 == 0), stop=(r4 == 3))
                            return ins
                        E("pe", mmv2, r=[wuv, ckvT], w=[ps])
                        copy(ev_eng(), vh.ap[:, kt_, :], ps.ap[:, 0:128], [ps], [vh])
                    CP("mla_d")
                    wq_ = load_w(W[:, 1792 + h * 192:1792 + (h + 1) * 192], KT, 192)
                    for tc in range(2):
                        S2.off = hb2
                        psn = psrot()
                        psr = psrot()

                        def mmq(e, psn=psn, psr=psr, tc=tc):
                            ins = None
                            for kt in range(KT):
                                ins = e.matmul(psn.ap, wq_.ap[:, kt, 0:128], hT[:, kt, tc * 512:(tc + 1) * 512], start=(kt == 0), stop=(kt == KT - 1))
                            for kt in range(KT):
                                ins = e.matmul(psr.ap[0:64, :], wq_.ap[:, kt, 128:192], hT[:, kt, tc * 512:(tc + 1) * 512], start=(kt == 0), stop=(kt == KT - 1))
                            return ins
                        E("pe", mmq, r=[wq_] + [hTt(kt, tc) for kt in range(KT)], w=[psn, psr])
                        ps_ss = psrot()
                        sumsq_bc([(psn, 128), (T(psr.ap[0:64, :], psr.keys), 64)], 512, ps_ss)
                        rs = S2.take([128, 512])
                        rstd_from(ps_ss, 512, 192, rs)
                        E("dve", lambda e, psn=psn, rs=rs, tc=tc: e.scalar_tensor_tensor(out=qn_n.ap[:, tc * 512:(tc + 1) * 512], in0=psn.ap, scalar=sp_(6), in1=rs.ap, op0=ALU.mult, op1=ALU.mult),
                          r=[psn, rs, "smallp"], w=[qn_n])
                        if not lat:
                            E("dve", lambda e, psr=psr, rs=rs, tc=tc: e.scalar_tensor_tensor(out=qn_r.ap[:, tc * 512:(tc + 1) * 512], in0=psr.ap[0:64, :], scalar=smallp[0:64, l, 7:8], in1=rs.ap[0:64, :], op0=ALU.mult, op1=ALU.mult),
                              r=[psr, rs, "smallp"], w=[qn_r])
                        else:
                            kq = S2.take([64, 512], BF16)
                            E("dve", lambda e, psr=psr, rs=rs, kq=kq: e.scalar_tensor_tensor(out=kq.ap, in0=psr.ap[0:64, :], scalar=smallp[0:64, l, 7:8], in1=rs.ap[0:64, :], op0=ALU.mult, op1=ALU.mult),
                              r=[psr, rs, "smallp"], w=[kq])
                            ps2 = psrot()
                            E("pe", lambda e, ps2=ps2, kq=kq: e.matmul(ps2.ap[0:64, :], permC[:, :], kq.ap, start=True, stop=True), r=[kq, "permC"], w=[ps2])
                            t1 = S2.take([64, 512])
                            E("dve", lambda e, t1=t1, ps2=ps2, tc=tc: e.tensor_tensor(out=t1.ap, in0=ps2.ap[0:64, :], in1=ropeC[:, 1, tc * 512:(tc + 1) * 512], op=ALU.mult), r=[ps2, "ropeC"], w=[t1])
                            t2 = S2.take([64, 512])
                            E("pool", lambda e, t2=t2, kq=kq, tc=tc: e.tensor_tensor(out=t2.ap, in0=kq.ap, in1=ropeC[:, 0, tc * 512:(tc + 1) * 512], op=ALU.mult), r=[kq, "ropeC"], w=[t2])
                            E("dve", lambda e, t1=t1, t2=t2, tc=tc: e.tensor_tensor(out=qn_r.ap[:, tc * 512:(tc + 1) * 512], in0=t1.ap, in1=t2.ap, op=ALU.add), r=[t1, t2], w=[qn_r])
                    S2.off = hb2
                    CP("mla_e")

                    def q_c(h_, q0, nq):
                        return [(T(qn_n.ap[:, q0:q0 + nq], qn_n.keys), 128), (T(qn_r.ap[:, q0:q0 + nq], qn_r.keys), 64)]

                    def k_c(h_, kt_):
                        return [(knT.ap[:, kt_ * 128:(kt_ + 1) * 128], 128), (krn.ap[:, kt_ * 128:(kt_ + 1) * 128], 64)]
                    k_c.keys = T(None, knT.keys + krn.keys)

                    def v_c(h_, kt_):
                        return vh.ap[:, kt_, :]
                    v_c.keys = vh
                    attention(1, q_c, k_c, v_c, 1.0 / math.sqrt(192.0), 10 + h)
                    CP("mla_f%d" % h)
                if DBG is not None and l == 0:
                    dump("mixC_g%d" % g, T(mixT[:, 10, 0:512], [("mixT", 10, 0)]), [128, 512])

                CP("mla")
                PUMPN["n"](10 ** 9)
                PUMPN["n"] = None
                rotset["A"] = [0, 1, 2, 3, 4]
                CP("s5_d")
                S2.off = 0
                yb = S2.take([128, 4, NT], BF16)
                copy("act", yb.ap, yacc.ap, [yacc], [yb])
                if DBG is not None and l == 0:
                    dump("yacc_g%d" % g, T(yacc.ap[:, 0, 0:512], yacc.keys), [128, 512])
                S3.off = 0
                wg = S3.take([128, 4, 1024], BF16)
                sg = S3.take([128, 512])
                DMA("pool", wg.ap, w_glu[l].rearrange("(a p) n -> p a n", p=128), w=[wg])
                for j in range(4):
                    for tc in range(2):
                        psv, psg = psrot(), psrot()

                        def mmg(e, psv=psv, psg=psg, j=j, tc=tc):
                            ins = None
                            for a in range(4):
                                ins = e.matmul(psv.ap, wg.ap[:, a, j * 128:(j + 1) * 128], yb.ap[:, a, tc * 512:(tc + 1) * 512], start=(a == 0), stop=(a == 3))
                            for a in range(4):
                                ins = e.matmul(psg.ap, wg.ap[:, a, 512 + j * 128:512 + (j + 1) * 128], yb.ap[:, a, tc * 512:(tc + 1) * 512], start=(a == 0), stop=(a == 3))
                            return ins
                        E("pe", mmg, r=[wg, yb], w=[psv, psg])
                        E("act", lambda e, psg=psg, sg=sg: e.activation(out=sg.ap, in_=psg.ap, func=AF.Sigmoid), r=[psg], w=[sg])
                        mo = mixTt(j, tc * 512, 512)
                        E("dve", lambda e, psv=psv, sg=sg, mo=mo: e.tensor_tensor(out=mo.ap, in0=psv.ap, in1=sg.ap, op=ALU.mult), r=[psv, sg], w=[mo])
                if DBG is not None and l == 0:
                    dump("mixA_g%d" % g, T(mixT[:, 0, 0:512], [("mixT", 0, 0)]), [128, 512])

                CP("s5")
                for hf in range(2):
                    DMA("sp", xflat[:, hf * 8192:(hf + 1) * 8192], xpark[:, hf * 8192:(hf + 1) * 8192],
                        r=[("xpark", hf)], w=[("xT", kt, tc) for kt in range(hf * 8, hf * 8 + 8) for tc in range(2)])
                ada_drain(upto_layer=l)
                E("dve", lambda e, l=l, g=g: e.tensor_copy(mods[:, 2, :], modall[:, l, 32:48, g]), r=[mk], w=["mods"])
                E("dve", lambda e, l=l, g=g: e.scalar_tensor_tensor(out=mods[:, 3, :], in0=modall[:, l, 64:80, g], scalar=1.0, in1=gmlp[:, l, :], op0=ALU.add, op1=ALU.mult), r=[mk, "gmlp"], w=["mods"])
                E("dve", lambda e, l=l, g=g: e.tensor_copy(mods[:, 4, :], modall[:, l, 48:64, g]), r=[mk], w=["mods"])
                E("dve", lambda e, l=l, g=g: e.tensor_copy(mods[:, 5, :], modall[:, l, 80:96, g]), r=[mk], w=["mods"])

                def resid(gsl, fc, tc, ps):
                    x_ = xTt(fc, tc)
                    E("dve", lambda e: e.scalar_tensor_tensor(out=x_.ap, in0=ps.ap, scalar=mods[:, gsl, fc:fc + 1], in1=x_.ap, op0=ALU.mult, op1=ALU.add), r=[ps, x_, "mods"], w=[x_])
                rhs_m = lambda kt, tc: mixTt(kt, tc * 512, 512)
                for cb in range(8):
                    lin_fm(w_out[l][:, cb * 256:(cb + 1) * 256], KT, 256, rhs_m, [0, 1], 512, lambda c0, cs, tc, ps, cb=cb: resid(2, cb * 2 + c0 // 128, tc, ps))
                if DBG is not None and l == 0:
                    dump("x1_g%d" % g, T(xT[:, 0, 0:512], [("xT", 0, 0)]), [128, 512])

                CP("out")
                norm_to_h(l, g, 3, 4)
                for qd in range(4):
                    S2.off = 0
                    rl = S2.take([128, 512], BF16)
                    rl2 = S2.take([128, 512], BF16)
                    rls = [rl, rl2]
                    cnt = {"i": 0}

                    def cons_ff1(c0, cs, tc, ps, jb):
                        j = jb * 2 + c0 // 128
                        r_ = rls[cnt["i"] % 2]
                        cnt["i"] += 1
                        E("act", lambda e: e.activation(out=r_.ap, in_=ps.ap, func=AF.Relu), r=[ps], w=[r_])
                        a_ = mixTt(j, tc * 512, 512)
                        E("dve", lambda e: e.tensor_tensor(out=a_.ap, in0=r_.ap, in1=r_.ap, op=ALU.mult), r=[r_], w=[a_])
                    for jb in range(8):
                        lin_fm(w_ff1[l][:, qd * 2048 + jb * 256:qd * 2048 + (jb + 1) * 256], KT, 256, rhs_h, [0, 1], 512, lambda c0, cs, tc, ps, jb=jb: cons_ff1(c0, cs, tc, ps, jb))
                    for cb in range(8):
                        lin_fm(w_ff2[l][qd * 2048:(qd + 1) * 2048, cb * 256:(cb + 1) * 256], KT, 256, rhs_m, [0, 1], 512, lambda c0, cs, tc, ps, cb=cb: resid(5, cb * 2 + c0 // 128, tc, ps))

            S2.off = 0
            ybufs = [S2.take([128, D]) for _ in range(2)]
            for tt in range(8):
                yb_ = ybufs[tt % 2]
                for k4 in range(4):
                    ps = psrot()

                    def trp2(e, ps=ps, k4=k4, tt=tt):
                        ins = None
                        for j in range(4):
                            kt = k4 * 4 + j
                            ins = e.transpose(ps.ap[:, j * 128:(j + 1) * 128], xT[:, kt, tt * 128:(tt + 1) * 128], ident[:])
                        return ins
                    E("pe", trp2, r=[idT] + [("xT", k4 * 4 + j, tt // 4) for j in range(4)], w=[ps])
                    copy(ev_eng(), yb_.ap[:, k4 * 512:(k4 + 1) * 512], ps.ap, [ps], [yb_])
                DMA("sp", y_d[g, tt * 128:(tt + 1) * 128, :], yb_.ap, r=[yb_])

        S2.off = 0
        for q in range(4):
            ps = psrot()
            E("pe", lambda e, ps=ps, q=q: e.transpose(ps.ap[:, 0:128], hfin[:, q * 128:(q + 1) * 128], ident[:]), r=["hfin", idT], w=[ps])
            o = S2.take([128, 128])
            copy("dve", o.ap, ps.ap[:, 0:128], [ps], [o])
            DMA("sp", nssm_d[q * 128:(q + 1) * 128, :], o.ap, r=[o])


    except _Stop:
        pass
    stats = P.finalize(st)
    st.close()
    return nc, stats


def _rope_tables(d):
    half, quarter = d // 2, d // 4
    t = np.arange(NT)
    row, col = (t // 64).astype(np.float32), (t % 64).astype(np.float32)
    inv = (10000.0 ** (-(np.arange(quarter, dtype=np.float32) / quarter))).astype(np.float32)
    cos = np.zeros((d, NT), np.float32)
    sin = np.zeros((d, NT), np.float32)
    perm = np.zeros((d, d), np.float32)
    for dd in range(d):
        pos = row if dd < half else col
        i = dd % quarter
        first = (dd % half) < quarter
        ang = pos * inv[i]
        cos[dd] = np.cos(ang)
        sin[dd] = -np.sin(ang) if first else np.sin(ang)
        partner = dd + quarter if first else dd - quarter
        perm[partner, dd] = 1.0
    return np.stack([cos, sin], axis=1).astype(np.float32), perm


def prep_inputs(inp):
    f = lambda a: np.ascontiguousarray(np.asarray(a, dtype=np.float32))
    sh = {}
    fm = lambda v: f(v.reshape(-1, 128).T)
    sh["w_mod"] = f(inp["w_mod"])
    sh["bmodT"] = f(np.stack([fm(inp["b_mod"][l]) for l in range(2)], axis=1))
    sh["gmixT"] = f(np.stack([fm(inp["norm_mix"][l]) for l in range(2)], axis=1))
    sh["gmlpT"] = f(np.stack([fm(inp["norm_mlp"][l]) for l in range(2)], axis=1))
    sh["w_in"] = f(inp["w_in"])
    sm = np.zeros((128, 2, 16), np.float32)
    for l in range(2):
        sm[:, l, 0] = inp["gqa_q_norm"][l]
        sm[:, l, 1] = inp["gqa_k_norm"][l]
        sm[:, l, 2:6] = fm(inp["mla_kv_norm"][l])
        sm[:, l, 6] = inp["mla_q_norm"][l][:128]
        sm[:64, l, 7] = inp["mla_q_norm"][l][128:]
        sm[:, l, 8] = inp["mla_k_norm"][l][:128]
        sm[:64, l, 9] = inp["mla_k_norm"][l][128:]
        sm[:, l, 10:14] = fm(inp["ssm_d"][l])
    sh["smallp"] = sm
    sh["w_uk"] = f(np.asarray(inp["mla_w_uk"]).reshape(2, 512, 768))
    sh["w_uv"] = f(np.asarray(inp["mla_w_uv"]).reshape(2, 512, 768))

    def st_layout(a):
        a = np.asarray(a, np.float32).reshape(2, 2, 16, 2, 64)
        return f(a.transpose(3, 4, 0, 1, 2).reshape(128, 2, 32))
    sh["lamr"] = st_layout(inp["ssm_lam_re"])
    sh["lami"] = st_layout(inp["ssm_lam_im"])
    sh["ldt"] = st_layout(np.broadcast_to(np.asarray(inp["ssm_log_dt"])[..., None], (2, 2, 32, 64)))
    Bpad = np.zeros((2, 16, 128, 4, 128), np.float32)
    Cpad = np.zeros((2, 16, 128, 4, 128), np.float32)
    bre, bim = np.asarray(inp["ssm_b_re"]), np.asarray(inp["ssm_b_im"])
    cre, cim = np.asarray(inp["ssm_c_re"]), np.asarray(inp["ssm_c_im"])
    for gp in range(16):
        for g2 in range(2):
            gg = gp * 2 + g2
            c0 = (gp % 4) * 32 + g2 * 16
            for dr in range(2):
                Bpad[:, gp, g2 * 64:(g2 + 1) * 64, dr * 2, c0:c0 + 16] = bre[:, dr, gg]
                Bpad[:, gp, g2 * 64:(g2 + 1) * 64, dr * 2 + 1, c0:c0 + 16] = bim[:, dr, gg]
                Cpad[:, gp, g2 * 64:(g2 + 1) * 64, dr * 2, c0:c0 + 16] = cre[:, dr, gg].transpose(0, 2, 1)
                Cpad[:, gp, g2 * 64:(g2 + 1) * 64, dr * 2 + 1, c0:c0 + 16] = cim[:, dr, gg].transpose(0, 2, 1)
    sh["Bpad"], sh["Cpad"] = Bpad, Cpad
    sh["w_glu"] = f(inp["ssm_w_glu"])
    sh["w_out"] = f(inp["w_out"])
    sh["w_ff1"] = f(inp["w_ff1"])
    sh["w_ff2"] = f(inp["w_ff2"])
    sh["ident"] = np.eye(128, dtype=np.float32)
    sh["ropeB"], sh["permB"] = _rope_tables(128)
    sh["ropeC"], sh["permC"] = _rope_tables(64)
    j = np.arange(256, dtype=np.float32)
    sh["jj"] = f(np.broadcast_to(np.stack([j, 255.0 - j])[None], (128, 2, 256)))
    percore = []
    xp, xs = np.asarray(inp["x_prompt"]), np.asarray(inp["x_sample"])
    for i in range(NCORES):
        d = dict(sh)
        d["xin"] = f(np.stack([xp[4 * i:4 * i + 4].reshape(NT, D), xs[i]]))
        d["ck"] = f(np.asarray(inp["cache_attn_k"])[i].reshape(2, 512, 256))
        d["cv"] = f(np.asarray(inp["cache_attn_v"])[i].reshape(2, 512, 256))
        d["cckv"] = f(np.asarray(inp["cache_mla_ckv"])[i])
        d["ckr"] = f(np.asarray(inp["cache_mla_krope"])[i])
        s = np.asarray(inp["state_ssm"])[i].reshape(2, 2, 16, 2, 64, 2)
        s = s.transpose(0, 3, 4, 1, 2, 5).reshape(2, 128, 32, 2)
        d["h0r"], d["h0i"] = f(s[..., 0]), f(s[..., 1])
        cv_ = np.stack([np.asarray(inp["c_ctx"]), np.asarray(inp["c"])[i]])
        d["cT"] = f(cv_.reshape(2, 16, 128).transpose(2, 1, 0))
        percore.append(d)
    return percore


_CACHE = {}


def kernel(**inputs):
    if "nc" not in _CACHE:
        _CACHE["nc"] = build()[0]
    nc = _CACHE["nc"]
    in_maps = prep_inputs(inputs)
    res = run_bass_kernel_spmd(nc, in_maps, core_ids=list(range(NCORES)))
    R = res.results
    y_prompt = np.concatenate([r["y"][0].reshape(4, 256, D) for r in R], axis=0)
    y_sample = np.stack([r["y"][1] for r in R], axis=0)

    def seqout(name, tail):
        return np.concatenate([r[name].reshape(2, 4, 256, -1).transpose(1, 0, 2, 3).reshape((4, 2, 256) + tail) for r in R], axis=0)
    new_k = seqout("nk", (2, 128))
    new_v = seqout("nv", (2, 128))
    new_ckv = seqout("nckv", (512,))
    new_kr = seqout("nkr", (64,))
    ss = []
    for r in R:
        a = r["nssm"].reshape(2, 4, 2, 16, 2, 2, 64)
        a = a.transpose(1, 0, 2, 3, 5, 6, 4).reshape(4, 2, 2, 32, 64, 2)
        ss.append(a)
    new_ssm = np.concatenate(ss, axis=0)
    f = lambda a: np.ascontiguousarray(a, dtype=np.float32)
    return (f(y_prompt), f(y_sample), f(new_k), f(new_v), f(new_ckv), f(new_kr), f(new_ssm))
```

```python
import math
import numpy as np
from contextlib import ExitStack
import concourse.bass as bass
import concourse.mybir as mybir
from concourse.bass_utils import run_bass_kernel_spmd

F32 = mybir.dt.float32
BF16 = mybir.dt.bfloat16
ALU = mybir.AluOpType
AF = mybir.ActivationFunctionType
AX = mybir.AxisListType

D = 2048
NT = 1024
KT = 16
DFF = 8192
INW = 3520
EPS = 1e-6
NCORES = 8
TWO_PI = 2.0 * math.pi


class _Rec:
    def __init__(self):
        self.calls = []

    def __getattr__(self, name):
        def f(*a, **k):
            self.calls.append((name, a, k))
            return None
        return f


class Prog:
    NDMA = 40

    def __init__(self, nc):
        self.nc = nc
        self.ops = []

    def op(self, eng, fn, reads=(), writes=(), dma=False):
        rec = _Rec()
        fn(rec)
        calls = rec.calls
        assert calls

        def emit(E, calls=calls):
            ins = None
            for (name, a, k) in calls:
                ins = getattr(E, name)(*a, **k)
            return ins
        self.ops.append((eng, emit, tuple(reads), tuple(writes), dma))

    def finalize(self, stack):
        nc = self.nc
        ops = self.ops
        n = len(ops)
        engs = ["pe", "act", "dve", "pool", "sp"]
        last_w, readers = {}, {}
        deps = [None] * n
        dma_prev, dma_slot, ndma = {}, [None] * n, 0
        for j, (eng, fn, reads, writes, dma) in enumerate(ops):
            d = set()
            for k in reads:
                w = last_w.get(k)
                if w is not None:
                    d.add(w)
            for k in writes:
                w = last_w.get(k)
                if w is not None:
                    d.add(w)
                d.update(readers.get(k, ()))
            if dma:
                slot = ndma % self.NDMA
                ndma += 1
                dma_slot[j] = slot
                if slot in dma_prev:
                    d.add(dma_prev[slot])
                dma_prev[slot] = j
            d.discard(j)
            deps[j] = d
            for k in reads:
                readers.setdefault(k, []).append(j)
            for k in writes:
                last_w[k] = j
                readers[k] = []
        needed = [False] * n
        for j in range(n):
            ej = ops[j][0]
            for i in deps[j]:
                if ops[i][4]:
                    continue
                if ops[i][0] == "pe" and ej == "pe" and not ops[j][4]:
                    continue
                needed[i] = True
        esem = {e: stack.enter_context(nc.semaphore("s_" + e)) for e in engs}
        dsem = [stack.enter_context(nc.semaphore("d%d" % i)) for i in range(self.NDMA)]
        ecount = {e: 0 for e in engs}
        dcount = [0] * self.NDMA
        sig = [None] * n
        seen = {e: {} for e in engs}
        snap = [None] * n
        plan = {e: [] for e in engs}
        nwait = 0
        for j, (eng, fn, reads, writes, dma) in enumerate(ops):
            sj = seen[eng]
            pl = plan[eng]
            for i in sorted(deps[j]):
                if (not ops[i][4]) and ops[i][0] == "pe" and eng == "pe" and not dma:
                    continue
                key, val = sig[i]
                if sj.get(key, 0) >= val:
                    continue
                pl.append((0, esem[key] if isinstance(key, str) else dsem[key], val))
                nwait += 1
                sj[key] = val
                for k2, v2 in snap[i].items():
                    if sj.get(k2, 0) < v2:
                        sj[k2] = v2
            if dma:
                slot = dma_slot[j]
                dcount[slot] += 16
                pl.append((1, fn, dsem[slot], 16))
                sig[j] = (slot, dcount[slot])
                snap[j] = dict(sj)
            elif needed[j]:
                ecount[eng] += 1
                pl.append((1, fn, esem[eng], 1))
                sig[j] = (eng, ecount[eng])
                snap[j] = dict(sj)
            else:
                pl.append((1, fn, None, 0))
        for slot in range(self.NDMA):
            if dcount[slot]:
                plan["sp"].append((0, dsem[slot], dcount[slot]))

        def replay(E, pl):
            for it in pl:
                if it[0] == 0:
                    E.wait_ge(it[1], it[2])
                else:
                    ins = it[1](E)
                    if it[2] is not None:
                        ins.then_inc(it[2], it[3])

        block = stack.enter_context(nc.Block())

        @block.tensor
        def _(e):
            replay(e, plan["pe"])

        @block.scalar
        def _(e):
            replay(e, plan["act"])

        @block.vector
        def _(e):
            replay(e, plan["dve"])

        @block.gpsimd
        def _(e):
            replay(e, plan["pool"])

        @block.sync
        def _(e):
            replay(e, plan["sp"])

        return dict(n_ops=n, n_wait=nwait, counts=ecount)


class _Stop(Exception):
    pass


class T:
    def __init__(self, ap, keys):
        self.ap = ap
        self.keys = list(keys)

    def __getitem__(self, idx):
        return T(self.ap[idx], self.keys)


def _keys(ts):
    out = []
    for t in ts:
        if isinstance(t, T):
            out.extend(t.keys)
        else:
            out.append(t)
    return out


def build(cfg=None):
    cfg = cfg or {}
    GROUPS = cfg.get("groups", [0, 1])
    NL = cfg.get("layers", 2)
    DBG = cfg.get("debug", None)

    nc = bass.Bass("TRN2", target_bir_lowering=False)

    def din(name, shape):
        return nc.dram_tensor(name, list(shape), F32, kind="ExternalInput").ap()

    def dout(name, shape):
        return nc.dram_tensor(name, list(shape), F32, kind="ExternalOutput").ap()

    xin = din("xin", [2, NT, D])
    ck = din("ck", [2, 512, 256])
    cv = din("cv", [2, 512, 256])
    cckv = din("cckv", [2, 512, 512])
    ckr = din("ckr", [2, 512, 64])
    h0r_d = din("h0r", [2, 128, 32])
    h0i_d = din("h0i", [2, 128, 32])
    cT_d = din("cT", [128, 16, 2])
    w_mod = din("w_mod", [2, D, 6 * D])
    bmod_d = din("bmodT", [128, 2, 96])
    gmix_d = din("gmixT", [128, 2, 16])
    gmlp_d = din("gmlpT", [128, 2, 16])
    w_in = din("w_in", [2, D, INW])
    sm_d = din("smallp", [128, 2, 16])
    w_uk = din("w_uk", [2, 512, 768])
    w_uv = din("w_uv", [2, 512, 768])
    lamr_d = din("lamr", [128, 2, 32])
    lami_d = din("lami", [128, 2, 32])
    ldt_d = din("ldt", [128, 2, 32])
    Bpad_d = din("Bpad", [2, 16, 128, 4, 128])
    Cpad_d = din("Cpad", [2, 16, 128, 4, 128])
    w_glu = din("w_glu", [2, 512, 1024])
    w_out = din("w_out", [2, D, D])
    w_ff1 = din("w_ff1", [2, D, DFF])
    w_ff2 = din("w_ff2", [2, DFF, D])
    ident_d = din("ident", [128, 128])
    ropeB_d = din("ropeB", [128, 2, NT])
    ropeC_d = din("ropeC", [64, 2, NT])
    permB_d = din("permB", [128, 128])
    permC_d = din("permC", [64, 64])
    jj_d = din("jj", [128, 2, 256])

    y_d = dout("y", [2, NT, D])
    nk_d = dout("nk", [2, NT, 256])
    nv_d = dout("nv", [2, NT, 256])
    nckv_d = dout("nckv", [2, NT, 512])
    nkr_d = dout("nkr", [2, NT, 64])
    nssm_d = dout("nssm", [512, 128])

    tabc = nc.dram_tensor("tabc", [2, 32, 128, 4 * 512], F32, kind="Internal").ap()
    btc = nc.dram_tensor("btc", [2, 16, 128, 4 * 128], BF16, kind="Internal").ap()
    st = ExitStack()
    P = Prog(nc)

    def CP(name):
        if cfg.get("stop") == name:
            raise _Stop()

    def E(eng, fn, r=(), w=()):
        rk, wk = _keys(r), _keys(w)
        wk = wk + [k for k in rk if isinstance(k, tuple) and k[0] == "ps" and k not in wk]
        P.op(eng, fn, reads=rk, writes=wk)

    def DMA(eng, out, in_, r=(), w=()):
        P.op(eng, lambda e: e.dma_start(out=out, in_=in_), reads=_keys(r), writes=_keys(w), dma=True)

    def sb(name, shape, dt=F32):
        return st.enter_context(nc.sbuf_tensor("sb_" + name, list(shape), dt))

    xT = sb("xT", [128, KT, NT])
    hT = sb("hT", [128, KT, NT], BF16)
    mixT = sb("mixT", [128, KT, NT], BF16)
    NWB = 2
    wts = [sb("wt%d" % i, [128, KT, 256], BF16) for i in range(NWB)]
    ident = sb("ident", [128, 128])
    identb = sb("identb", [128, 128], BF16)
    onesb = sb("onesb", [128, 128], BF16)
    permB = sb("permBs", [128, 128], BF16)
    permC = sb("permCs", [64, 64], BF16)
    ropeB = sb("ropeBs", [128, 2, NT], BF16)
    ropeC = sb("ropeCs", [64, 2, NT], BF16)
    scT = sb("scT", [128, KT, 2], BF16)
    modall = sb("modall", [128, 2, 96, 2])
    bmod = sb("bmod", [128, 2, 96])
    gmix = sb("gmix", [128, 2, 16])
    gmlp = sb("gmlp", [128, 2, 16])
    smallp = sb("smallps", [128, 2, 16])
    mods = sb("mods", [128, 6, 16])
    lamr = sb("lamrs", [128, 2, 32])
    lami = sb("lamis", [128, 2, 32])
    ldt = sb("ldts", [128, 2, 32])
    s5c = sb("s5c", [128, 8, 32])
    hfin = sb("hfin", [128, 512])
    SCRN = 12160
    scr = sb("scr", [128, SCRN])

    def xTt(kt, tc, n=512):
        return T(xT[:, kt, tc * 512:tc * 512 + n], [("xT", kt, tc)])

    def hTt(kt, tc):
        return T(hT[:, kt, tc * 512:(tc + 1) * 512], [("hT", kt, tc)])

    def mixTt(kt, t0, n):
        return T(mixT[:, kt, t0:t0 + n], [("mixT", kt, c) for c in range(t0 // 512, (t0 + n - 1) // 512 + 1)])

    hT_f32 = hT[:].rearrange("p a b -> p (a b)").bitcast(F32)

    class Scr:
        def __init__(self, backing=None):
            self.off = 0
            self.hT = backing is not None
            self.base = hT_f32 if self.hT else scr
            self.cap = 8192 if self.hT else SCRN

        def take(self, shape, dt=F32, parts=128):
            n = int(np.prod(shape[1:]))
            nf = n if dt == F32 else (n + 1) // 2
            nf = (nf + 31) // 32 * 32
            assert self.off + nf <= self.cap, ("scratch overflow", self.hT, self.off, nf)
            ap = self.base[0:shape[0], self.off:self.off + nf]
            if self.hT:
                keys = sorted({("hT", (2 * o) // 1024, ((2 * o) % 1024) // 512) for o in range(self.off, self.off + nf, 32)} |
                              {("hT", (2 * o + 62) // 1024, ((2 * o + 62) % 1024) // 512) for o in range(self.off, self.off + nf, 32)})
            else:
                keys = [("scr", c) for c in range(self.off // 128, (self.off + nf - 1) // 128 + 1)]
            self.off += nf
            if dt != F32:
                ap = ap.bitcast(dt)[:, 0:n]
            else:
                ap = ap[:, 0:n]
            if len(shape) == 3:
                ap = ap.rearrange("p (a b) -> p a b", a=shape[1])
            elif len(shape) == 4:
                ap = ap.rearrange("p (a b c) -> p a b c", a=shape[1], b=shape[2])
            return T(ap, keys)

    psall = st.enter_context(nc.psum_tensor("psall", [128, 8, 512], F32))
    pbanks = [psall[:, i, :] for i in range(8)]
    rot = {"i": 0}

    def psrot():
        i = rot["i"] % 5
        rot["i"] += 1
        return T(pbanks[i], [("ps", i)])

    psA = T(pbanks[5], [("ps", 5)])
    psB = T(pbanks[6], [("ps", 6)])
    psC = T(pbanks[7], [("ps", 7)])

    evq = {"i": 0}

    def ev_eng():
        evq["i"] += 1
        return "act" if evq["i"] % 2 else "dve"

    def copy(eng, out, in_, r, w):
        if eng == "act":
            E("act", lambda e: e.activation(out=out, in_=in_, func=AF.Copy), r, w)
        else:
            E(eng, lambda e: e.tensor_copy(out, in_), r, w)

    dbg_n = {"i": 0}

    def dump(name, t, shape):
        if DBG is None:
            return
        d = dout("dbg_" + name, shape)
        DBG.append(name)
        DMA("pool", d, t.ap, r=[t])

    wq = {"i": 0}

    def load_w(src2d, nk, ncols):
        i = wq["i"] % NWB
        wq["i"] += 1
        t = T(wts[i][:], [("wt", i)])
        half = (nk + 1) // 2
        for a, b in ((0, half), (half, nk)):
            if b > a:
                DMA("pool", wts[i][:, a:b, 0:ncols], src2d[a * 128:b * 128, :].rearrange("(kt p) n -> p kt n", p=128), w=[t])
        return t

    def lin_fm(src2d, nk, ncols, rhs_fn, tcs, ntc, consume, cchunks=None):
        wt = load_w(src2d, nk, ncols)
        if cchunks is None:
            cchunks = [(c, min(128, ncols - c)) for c in range(0, ncols, 128)]
        prevc = None
        for (c0, cs) in cchunks:
            for tc in tcs:
                ps = psrot()
                rl = [rhs_fn(kt, tc) for kt in range(nk)]

                def mm(e, ps=ps, c0=c0, cs=cs, rl=rl):
                    ins = None
                    for kt in range(nk):
                        ins = e.matmul(ps.ap[0:cs, 0:ntc], wt.ap[:, kt, c0:c0 + cs], rl[kt].ap, start=(kt == 0), stop=(kt == nk - 1))
                    return ins
                E("pe", mm, r=[wt] + rl, w=[ps])
                if prevc is not None:
                    consume(*prevc)
                prevc = (c0, cs, tc, ps)
        if prevc is not None:
            consume(*prevc)

    I32 = mybir.dt.int32
    C1 = 6.28125
    C2 = TWO_PI - 6.28125

    def sin_rr(x, shift, out_ap, out_keys, ti, tf, ty, n):
        xk = x.keys
        E("dve", lambda e: e.tensor_scalar(out=ti.ap.bitcast(I32), in0=x.ap, scalar1=shift, scalar2=1.0 / TWO_PI, op0=ALU.add, op1=ALU.mult), r=[x], w=[ti])
        E("dve", lambda e: e.tensor_copy(tf.ap, ti.ap.bitcast(I32)), r=[ti], w=[tf])
        E("dve", lambda e: e.tensor_scalar(out=ty.ap, in0=x.ap, scalar1=shift, scalar2=None, op0=ALU.add), r=[x], w=[ty])
        E("dve", lambda e: e.scalar_tensor_tensor(out=ty.ap, in0=tf.ap, scalar=-C1, in1=ty.ap, op0=ALU.mult, op1=ALU.add), r=[tf, ty], w=[ty])
        E("dve", lambda e: e.scalar_tensor_tensor(out=ty.ap, in0=tf.ap, scalar=-C2, in1=ty.ap, op0=ALU.mult, op1=ALU.add), r=[tf, ty], w=[ty])
        E("dve", lambda e: e.tensor_scalar(out=tf.ap, in0=ty.ap, scalar1=math.pi, scalar2=-TWO_PI, op0=ALU.is_gt, op1=ALU.mult), r=[ty], w=[tf])
        E("dve", lambda e: e.tensor_tensor(out=ty.ap, in0=ty.ap, in1=tf.ap, op=ALU.add), r=[ty, tf], w=[ty])
        E("dve", lambda e: e.tensor_scalar(out=tf.ap, in0=ty.ap, scalar1=-math.pi, scalar2=TWO_PI, op0=ALU.is_lt, op1=ALU.mult), r=[ty], w=[tf])
        E("dve", lambda e: e.tensor_tensor(out=ty.ap, in0=ty.ap, in1=tf.ap, op=ALU.add), r=[ty, tf], w=[ty])
        if out_ap is not None:
            E("act", lambda e: e.activation(out=out_ap, in_=ty.ap, func=AF.Sin), r=[ty], w=out_keys)

    try:
        S = Scr()
        DMA("sp", ident[:], ident_d, w=["ident"])
        idT = T(ident[:], ["ident"])
        identbT = T(identb[:], ["identb"])
        onesT = T(onesb[:], ["onesb"])
        E("dve", lambda e: e.tensor_copy(identb[:], ident[:]), r=["ident"], w=["identb"])
        E("dve", lambda e: e.memset(onesb[:], 1.0), w=["onesb"])
        DMA("pool", permB[:], permB_d, w=["permB"])
        DMA("pool", permC[:], permC_d, w=["permC"])
        DMA("pool", ropeB[:], ropeB_d, w=["ropeB"])
        DMA("pool", ropeC[:], ropeC_d, w=["ropeC"])
        DMA("sp", bmod[:], bmod_d, w=["bmod"])
        DMA("sp", gmix[:], gmix_d, w=["gmix"])
        DMA("sp", gmlp[:], gmlp_d, w=["gmlp"])
        DMA("sp", smallp[:], sm_d, w=["smallp"])
        DMA("sp", lamr[:], lamr_d, w=["lamr"])
        DMA("sp", lami[:], lami_d, w=["lami"])
        DMA("sp", ldt[:], ldt_d, w=["ldt"])
        ctmp = S.take([128, 32])
        DMA("sp", ctmp.ap, cT_d.rearrange("p a b -> p (a b)"), w=[ctmp])
        E("act", lambda e: e.activation(out=scT[:].rearrange("p a b -> p (a b)"), in_=ctmp.ap, func=AF.Silu), r=[ctmp], w=["scT"])
        scTt = T(scT[:], ["scT"])
        E("dve", lambda e: e.memset(hfin[:], 0.0), w=["hfin"])

        def adaln_tile(l, ti):
            wt = load_w(w_mod[l, :, ti * 256:(ti + 1) * 256], 16, 256)
            ps = psrot()

            def mm(e, ps=ps, wt=wt):
                ins = None
                for cj in range(2):
                    for kt in range(16):
                        ins = e.matmul(ps.ap[:, cj * 2:cj * 2 + 2], wt.ap[:, kt, cj * 128:(cj + 1) * 128], scT[:, kt, :], start=(kt == 0), stop=(kt == 15))
                return ins
            E("pe", mm, r=[wt, scTt], w=[ps])
            for cj in range(2):
                n = ti * 2 + cj
                E("dve", lambda e, ps=ps, cj=cj, n=n, l=l: e.tensor_scalar(out=modall[:, l, n, :], in0=ps.ap[:, cj * 2:cj * 2 + 2], scalar1=bmod[:, l, n:n + 1], scalar2=None, op0=ALU.add),
                  r=[ps, "bmod"], w=[("modall", l)])
        ada_pend = [(l, ti) for l in range(NL) for ti in range(48)]

        def ada_drain(n=None, upto_layer=None, upto_ti=None):
            k = 0
            while ada_pend and (n is None or k < n):
                if upto_layer is not None and (ada_pend[0][0] > upto_layer or (ada_pend[0][0] == upto_layer and upto_ti is not None and ada_pend[0][1] >= upto_ti)):
                    break
                adaln_tile(*ada_pend.pop(0))
                k += 1

        CP("pre")
        def sumsq_bc(parts, ntc, ps_out):
            sqs = []
            for (src, k) in parts:
                sq = S2.take([128, ntc], BF16)
                E("act", lambda e, sq=sq, src=src, k=k: e.activation(out=sq.ap[0:k, :], in_=src.ap, func=AF.Square), r=[src], w=[sq])
                sqs.append((sq, k))

            def mm(e):
                ins = None
                for i, (sq, k) in enumerate(sqs):
                    ins = e.matmul(ps_out.ap[:, 0:ntc], onesb[0:k, :], sq.ap[0:k, :], start=(i == 0), stop=(i == len(sqs) - 1))
                return ins
            E("pe", mm, r=[onesT] + [s for s, _ in sqs], w=[ps_out])

        def rstd_from(ps_ss, ntc, dim, out):
            E("dve", lambda e: e.tensor_scalar(out=out.ap, in0=ps_ss.ap[:, 0:ntc], scalar1=1.0 / dim, scalar2=EPS, op0=ALU.mult, op1=ALU.add), r=[ps_ss], w=[out])
            E("act", lambda e: e.activation(out=out.ap, in_=out.ap, func=AF.Ln), r=[out], w=[out])
            E("act", lambda e: e.activation(out=out.ap, in_=out.ap, func=AF.Exp, scale=-0.5), r=[out], w=[out])

        trq = {"i": 0, "bufs": None}

        def tr_out(src_fm, k, ntok, dst_dram_rows, colsl):
            for t0 in range(0, ntok, 128):
                ps = psC
                E("pe", lambda e, ps=ps, t0=t0: e.transpose(ps.ap[:, 0:k], src_fm.ap[0:k, t0:t0 + 128], ident[0:k, 0:k]), r=[src_fm, idT], w=[ps])
                o = trq["bufs"][trq["i"] % 2]
                trq["i"] += 1
                copy(ev_eng(), o.ap[:, 0:k], ps.ap[:, 0:k], [ps], [o])
                DMA("sp", dst_dram_rows(t0)[:, colsl], o.ap[:, 0:k], r=[o])

        def norm_to_h(l, g, gsl, ssl):
            for tc in range(2):
                S2.off = 0
                sqs = [S2.take([128, 512], BF16) for _ in range(2)]
                rs = S2.take([128, 512])
                tmps = [S2.take([128, 512]) for _ in range(2)]
                ps_ss = psA
                for kt in range(KT):
                    sq = sqs[kt % 2]
                    x_ = xTt(kt, tc)
                    E("act", lambda e, sq=sq, x_=x_: e.activation(out=sq.ap, in_=x_.ap, func=AF.Square), r=[x_], w=[sq])
                    E("pe", lambda e, sq=sq, kt=kt: e.matmul(ps_ss.ap, onesb[:, :], sq.ap, start=(kt == 0), stop=(kt == KT - 1)), r=[sq, onesT], w=[ps_ss])
                rstd_from(ps_ss, 512, D, rs)
                for kt in range(KT):
                    tmp = tmps[kt % 2]
                    x_ = xTt(kt, tc)
                    h_ = hTt(kt, tc)
                    E("dve", lambda e, tmp=tmp, x_=x_, kt=kt: e.scalar_tensor_tensor(out=tmp.ap, in0=x_.ap, scalar=mods[:, gsl, kt:kt + 1], in1=rs.ap, op0=ALU.mult, op1=ALU.mult),
                      r=[x_, rs, "mods"], w=[tmp])
                    E("act", lambda e, tmp=tmp, h_=h_, kt=kt: e.activation(out=h_.ap, in_=tmp.ap, func=AF.Identity, bias=mods[:, ssl, kt:kt + 1], scale=1.0), r=[tmp, "mods"], w=[h_])

        S2 = Scr()
        S3 = Scr(backing=True)

        for g in GROUPS:
            lat = (g == 1)
            S2.off = 0
            xtoks = [S2.take([128, D]) for _ in range(2)]
            for tt in range(8):
                xt = xtoks[tt % 2]
                DMA("sp", xt.ap, xin[g, tt * 128:(tt + 1) * 128, :], w=[xt])
                for k4 in range(4):
                    ps = psrot()

                    def trp(e, ps=ps, xt=xt, k4=k4):
                        ins = None
                        for j in range(4):
                            kt = k4 * 4 + j
                            ins = e.transpose(ps.ap[:, j * 128:(j + 1) * 128], xt.ap[:, kt * 128:(kt + 1) * 128], ident[:])
                        return ins
                    E("pe", trp, r=[xt, idT], w=[ps])
                    wkeys = [("xT", k4 * 4 + j, tt // 4) for j in range(4)]
                    copy(ev_eng(), xT[:, k4 * 4:k4 * 4 + 4, tt * 128:(tt + 1) * 128], ps.ap.rearrange("p (a b) -> p a b", a=4), [ps], wkeys)

            CP("load")
            for l in range(NL):
                ada_drain(upto_layer=l, upto_ti=16)
                mk = ("modall", l)
                E("dve", lambda e, l=l, g=g: e.scalar_tensor_tensor(out=mods[:, 0, :], in0=modall[:, l, 16:32, g], scalar=1.0, in1=gmix[:, l, :], op0=ALU.add, op1=ALU.mult), r=[mk, "gmix"], w=["mods"])
                E("dve", lambda e, l=l, g=g: e.tensor_copy(mods[:, 1, :], modall[:, l, 0:16, g]), r=[mk], w=["mods"])

                norm_to_h(l, g, 0, 1)
                rhs_h = lambda kt, tc: hTt(kt, tc)
                W = w_in[l]
                sp_ = lambda c: smallp[:, l, c:c + 1]
                NK = 1536 if lat else 1024
                KOFF = 512 if lat else 0
                if DBG is not None and l == 0:
                    dump("hT_g%d" % g, T(hT[:, 0, 0:512], [("hT", 0, 0)]), [128, 512])

                CP("norm1")
                S2.off = 0
                qT = S2.take([128, 6, NT], BF16)
                kT = S2.take([128, 2, NK], BF16)
                vtok = S2.take([128, NK // 128, 256], BF16)
                trq["bufs"] = [S2.take([128, 128]) for _ in range(2)]
                gq_base = S2.off
                if lat:
                    ctmp = S2.take([128, 4, 256])
                    DMA("sp", ctmp.ap, ck[l].rearrange("(a p) n -> p a n", p=128), w=[ctmp])
                    DMA("pool", vtok.ap[:, 0:4, :], cv[l].rearrange("(a p) n -> p a n", p=128), w=[vtok])
                    for hk in range(2):
                        ps = psrot()

                        def trk(e, ps=ps, hk=hk):
                            ins = None
                            for a in range(4):
                                ins = e.transpose(ps.ap[:, a * 128:(a + 1) * 128], ctmp.ap[:, a, hk * 128:(hk + 1) * 128], ident[:])
                            return ins
                        E("pe", trk, r=[ctmp, idT], w=[ps])
                        copy(ev_eng(), kT.ap[:, hk, 0:512], ps.ap, [ps], [kT])

                def gqa_qk(c0, cs, tc, ps, base):
                    S2.off = base
                    hidx = c0 // 128
                    isq = hidx < 6
                    ps_ss = psrot()
                    sumsq_bc([(ps, 128)], 512, ps_ss)
                    rs = S2.take([128, 512])
                    rstd_from(ps_ss, 512, 128, rs)
                    gcol = sp_(0) if isq else sp_(1)
                    if isq:
                        dst = T(qT.ap[:, hidx, tc * 512:(tc + 1) * 512], qT.keys)
                    else:
                        dst = T(kT.ap[:, hidx - 6, KOFF + tc * 512:KOFF + (tc + 1) * 512], kT.keys)
                    if not lat:
                        if isq:
                            E("dve", lambda e: e.scalar_tensor_tensor(out=dst.ap, in0=ps.ap, scalar=gcol, in1=rs.ap, op0=ALU.mult, op1=ALU.mult), r=[ps, rs, "smallp"], w=[dst])
                        else:
                            kn = S2.take([128, 512])
                            E("dve", lambda e: e.scalar_tensor_tensor(out=kn.ap, in0=ps.ap, scalar=gcol, in1=rs.ap, op0=ALU.mult, op1=ALU.mult), r=[ps, rs, "smallp"], w=[kn])
                            copy("act", dst.ap, kn.ap, [kn], [dst])
                            hk = hidx - 6
                            tr_out(kn, 128, 512, lambda t0: nk_d[l, tc * 512 + t0:tc * 512 + t0 + 128, :], slice(hk * 128, (hk + 1) * 128))
                    else:
                        qn = S2.take([128, 512], BF16)
                        E("dve", lambda e: e.scalar_tensor_tensor(out=qn.ap, in0=ps.ap, scalar=gcol, in1=rs.ap, op0=ALU.mult, op1=ALU.mult), r=[ps, rs, "smallp"], w=[qn])
                        ps2 = psrot()
                        E("pe", lambda e: e.matmul(ps2.ap, permB[:, :], qn.ap, start=True, stop=True), r=[qn, "permB"], w=[ps2])
                        t1 = S2.take([128, 512])
                        E("dve", lambda e: e.tensor_tensor(out=t1.ap, in0=ps2.ap, in1=ropeB[:, 1, tc * 512:(tc + 1) * 512], op=ALU.mult), r=[ps2, "ropeB"], w=[t1])
                        t2 = S2.take([128, 512])
                        E("pool", lambda e: e.tensor_tensor(out=t2.ap, in0=qn.ap, in1=ropeB[:, 0, tc * 512:(tc + 1) * 512], op=ALU.mult), r=[qn, "ropeB"], w=[t2])
                        E("dve", lambda e: e.tensor_tensor(out=dst.ap, in0=t1.ap, in1=t2.ap, op=ALU.add), r=[t1, t2], w=[dst])

                for cblk in range(4):
                    lin_fm(W[:, 512 + cblk * 256:512 + (cblk + 1) * 256], KT, 256, rhs_h, [0, 1], 512,
                           lambda c0, cs, tc, ps, cblk=cblk: gqa_qk(cblk * 256 + c0, cs, tc, ps, gq_base))
                wtv = load_w(W[:, 1536:1792], KT, 256)
                for tt in range(8):
                    S2.off = gq_base
                    ps = psrot()

                    def mmv(e, ps=ps, tt=tt):
                        ins = None
                        for kt in range(KT):
                            ins = e.matmul(ps.ap[:, 0:256], hT[:, kt, tt * 128:(tt + 1) * 128], wtv.ap[:, kt, :], start=(kt == 0), stop=(kt == KT - 1))
                        return ins
                    E("pe", mmv, r=[wtv] + [hTt(kt, tt // 4) for kt in range(KT)], w=[ps])
                    copy("act", vtok.ap[:, KOFF // 128 + tt, :], ps.ap[:, 0:256], [ps], [vtok])
                    if not lat:
                        vo = S2.take([128, 256])
                        copy("dve", vo.ap, ps.ap[:, 0:256], [ps], [vo])
                        DMA("sp", nv_d[l, tt * 128:(tt + 1) * 128, :], vo.ap, r=[vo])

                def attention(nheads, qfn, kfn, vfn, scale, mix0):
                    if lat:
                        blocks = [(0, 512, list(range(12))), (512, 512, list(range(12)))]
                    else:
                        blocks = [(s * 256, 256, [2 * s, 2 * s + 1]) for s in range(4)]
                    abase = S2.off
                    for h in range(nheads):
                        for (q0, nq, ktiles) in blocks:
                            S2.off = abase
                            ets = [S2.take([128, nq], BF16) for _ in range(3)]
                            qparts = qfn(h, q0, nq)
                            nk_ = len(ktiles)
                            prev = None

                            def pv(ki, et, vap, nq=nq, nk_=nk_):
                                E("pe", lambda e: e.matmul(psA.ap[:, 0:nq], vap, et.ap, start=(ki == 0), stop=(ki == nk_ - 1)), r=[et, vfn.keys], w=[psA])
                                E("pe", lambda e: e.matmul(psB.ap[:, 0:nq], onesb[:, :], et.ap, start=(ki == 0), stop=(ki == nk_ - 1)), r=[et, onesT], w=[psB])
                            for ki, kt_ in enumerate(ktiles):
                                ps = psrot()
                                kparts = kfn(h, kt_)

                                def mms(e, ps=ps, kparts=kparts, qparts=qparts, nq=nq):
                                    ins = None
                                    for i, ((kap, kk), (qt_, qk)) in enumerate(zip(kparts, qparts)):
                                        ins = e.matmul(ps.ap[:, 0:nq], kap, qt_.ap, start=(i == 0), stop=(i == len(kparts) - 1))
                                    return ins
                                E("pe", mms, r=[kfn.keys] + [q for q, _ in qparts], w=[ps])
                                et = ets[ki % 3]
                                E("act", lambda e, et=et, ps=ps, nq=nq: e.activation(out=et.ap, in_=ps.ap[:, 0:nq], func=AF.Exp, scale=scale), r=[ps], w=[et])
                                if prev is not None:
                                    pv(*prev)
                                prev = (ki, et, vfn(h, kt_))
                            pv(*prev)

                            rd = S2.take([128, nq])
                            E("dve", lambda e, rd=rd, nq=nq: e.reciprocal(rd.ap, psB.ap[:, 0:nq]), r=[psB], w=[rd])
                            mo = mixTt(mix0 + h, q0, nq)
                            E("dve", lambda e, rd=rd, mo=mo, nq=nq: e.tensor_tensor(out=mo.ap, in0=psA.ap[:, 0:nq], in1=rd.ap, op=ALU.mult), r=[psA, rd], w=[mo])

                def q_b(h, q0, nq):
                    return [(T(qT.ap[:, h, q0:q0 + nq], qT.keys), 128)]

                def k_b(h, kt_):
                    return [(kT.ap[:, h // 3, kt_ * 128:(kt_ + 1) * 128], 128)]
                k_b.keys = kT

                def v_b(h, kt_):
                    return vtok.ap[:, kt_, (h // 3) * 128:(h // 3 + 1) * 128]
                v_b.keys = vtok
                S2.off = gq_base
                attention(6, q_b, k_b, v_b, 1.0 / math.sqrt(128.0), 4)
                if DBG is not None and l == 0:
                    dump("mixB_g%d" % g, T(mixT[:, 4, 0:512], [("mixT", 4, 0)]), [128, 512])

                CP("gqa")
                S2.off = 0
                ckvT = S2.take([128, 4, NK], BF16)
                krT = S2.take([64, NK])
                krsq = S2.take([64, NK], BF16)
                trq["bufs"] = [S2.take([128, 128]) for _ in range(2)]
                mla_base = S2.off
                if lat:
                    for a in range(4):
                        S2.off = mla_base
                        ctmp = S2.take([128, 512])
                        DMA("sp", ctmp.ap, cckv[l, a * 128:(a + 1) * 128, :], w=[ctmp])
                        ps = psrot()

                        def trc(e, ps=ps, ctmp=ctmp):
                            ins = None
                            for r4 in range(4):
                                ins = e.transpose(ps.ap[:, r4 * 128:(r4 + 1) * 128], ctmp.ap[:, r4 * 128:(r4 + 1) * 128], ident[:])
                            return ins
                        E("pe", trc, r=[ctmp, idT], w=[ps])
                        copy(ev_eng(), ckvT.ap[:, :, a * 128:(a + 1) * 128], ps.ap.rearrange("p (a b) -> p a b", a=4), [ps], [ckvT])
                        ktmp = S2.take([128, 64])
                        DMA("sp", ktmp.ap, ckr[l, a * 128:(a + 1) * 128, :], w=[ktmp])
                        ps = psrot()
                        E("pe", lambda e, ps=ps, ktmp=ktmp: e.transpose(ps.ap[0:64, 0:128], ktmp.ap, ident[:]), r=[ktmp, idT], w=[ps])
                        copy(ev_eng(), krT.ap[:, a * 128:(a + 1) * 128], ps.ap[0:64, 0:128], [ps], [krT])
                for tc in range(2):
                    S2.off = mla_base
                    raws = [S2.take([128, 512]) for _ in range(4)]

                    def cons_ckv(c0, cs, tc_, ps, raws=raws):
                        copy(ev_eng(), raws[c0 // 128].ap, ps.ap, [ps], [raws[c0 // 128]])
                    for half in range(2):
                        lin_fm(W[:, 2944 + half * 256:2944 + (half + 1) * 256], KT, 256, rhs_h, [tc], 512,
                               lambda c0, cs, tc_, ps, half=half: cons_ckv(half * 256 + c0, cs, tc_, ps))
                    ps_ss = psrot()
                    sumsq_bc([(raws[i], 128) for i in range(4)], 512, ps_ss)
                    rs = S2.take([128, 512])
                    rstd_from(ps_ss, 512, 512, rs)
                    for i in range(4):
                        cn = raws[i]
                        E("dve", lambda e, cn=cn, i=i: e.scalar_tensor_tensor(out=cn.ap, in0=cn.ap, scalar=sp_(2 + i), in1=rs.ap, op0=ALU.mult, op1=ALU.mult), r=[cn, rs, "smallp"], w=[cn])
                        copy("act", ckvT.ap[:, i, KOFF + tc * 512:KOFF + (tc + 1) * 512], cn.ap, [cn], [ckvT])
                        if not lat:
                            tr_out(cn, 128, 512, lambda t0, tc=tc: nckv_d[l, tc * 512 + t0:tc * 512 + t0 + 128, :], slice(i * 128, (i + 1) * 128))

                CP("mla_a")

                def cons_kr(c0, cs, tc, ps):
                    copy("act", krT.ap[:, KOFF + tc * 512:KOFF + (tc + 1) * 512], ps.ap[0:64, :], [ps], [krT])
                    if not lat:
                        S2.off = mla_base
                        kro = T(krT.ap[:, tc * 512:(tc + 1) * 512], krT.keys)
                        tr_out(kro, 64, 512, lambda t0: nkr_d[l, tc * 512 + t0:tc * 512 + t0 + 128, :], slice(0, 64))
                lin_fm(W[:, 3456:3520], KT, 64, rhs_h, [0, 1], 512, cons_kr)
                E("act", lambda e: e.activation(out=krsq.ap, in_=krT.ap, func=AF.Square), r=[krT], w=[krsq])
                CP("mla_b")
                head_base = mla_base
                for h in range(6):
                    S2.off = head_base
                    wuk = S2.take([128, 4, 128], BF16)
                    wuv = S2.take([128, 4, 128], BF16)
                    DMA("pool", wuk.ap, w_uk[l][:, h * 128:(h + 1) * 128].rearrange("(a p) n -> p a n", p=128), w=[wuk])
                    DMA("pool", wuv.ap, w_uv[l][:, h * 128:(h + 1) * 128].rearrange("(a p) n -> p a n", p=128), w=[wuv])
                    knT = S2.take([128, NK], BF16)
                    krn = S2.take([64, NK], BF16)
                    vh = S2.take([128, NK // 128, 128], BF16)
                    qn_n = S2.take([128, NT], BF16)
                    qn_r = S2.take([64, NT], BF16)
                    hb2 = S2.off
                    for kc in range(NK // 512):
                        S2.off = hb2
                        ps = psrot()

                        def mmk(e, ps=ps, kc=kc, h=h):
                            ins = None
                            for r4 in range(4):
                                ins = e.matmul(ps.ap, wuk.ap[:, r4, :], ckvT.ap[:, r4, kc * 512:(kc + 1) * 512], start=(r4 == 0), stop=(r4 == 3))
                            return ins
                        E("pe", mmk, r=[wuk, ckvT], w=[ps])
                        sq = S2.take([128, 512], BF16)
                        E("act", lambda e, sq=sq, ps=ps: e.activation(out=sq.ap, in_=ps.ap, func=AF.Square), r=[ps], w=[sq])
                        ps_ss = psrot()

                        def mmss(e, ps_ss=ps_ss, sq=sq, kc=kc):
                            e.matmul(ps_ss.ap, onesb[:, :], sq.ap, start=True, stop=False)
                            return e.matmul(ps_ss.ap, onesb[0:64, :], krsq.ap[:, kc * 512:(kc + 1) * 512], start=False, stop=True)
                        E("pe", mmss, r=[sq, krsq, onesT], w=[ps_ss])
                        rs = S2.take([128, 512])
                        rstd_from(ps_ss, 512, 192, rs)
                        E("dve", lambda e, ps=ps, rs=rs, kc=kc: e.scalar_tensor_tensor(out=knT.ap[:, kc * 512:(kc + 1) * 512], in0=ps.ap, scalar=sp_(8), in1=rs.ap, op0=ALU.mult, op1=ALU.mult),
                          r=[ps, rs, "smallp"], w=[knT])
                        newtok = lat and kc >= 1
                        if not newtok:
                            E("dve", lambda e, rs=rs, kc=kc: e.scalar_tensor_tensor(out=krn.ap[:, kc * 512:(kc + 1) * 512], in0=krT.ap[:, kc * 512:(kc + 1) * 512], scalar=smallp[0:64, l, 9:10], in1=rs.ap[0:64, :], op0=ALU.mult, op1=ALU.mult),
                              r=[krT, rs, "smallp"], w=[krn])
                        else:
                            tcn = kc - 1
                            kq = S2.take([64, 512], BF16)
                            E("dve", lambda e, rs=rs, kc=kc, kq=kq: e.scalar_tensor_tensor(out=kq.ap, in0=krT.ap[:, kc * 512:(kc + 1) * 512], scalar=smallp[0:64, l, 9:10], in1=rs.ap[0:64, :], op0=ALU.mult, op1=ALU.mult),
                              r=[krT, rs, "smallp"], w=[kq])
                            ps2 = psrot()
                            E("pe", lambda e, ps2=ps2, kq=kq: e.matmul(ps2.ap[0:64, :], permC[:, :], kq.ap, start=True, stop=True), r=[kq, "permC"], w=[ps2])
                            t1 = S2.take([64, 512])
                            E("dve", lambda e, t1=t1, ps2=ps2, tcn=tcn: e.tensor_tensor(out=t1.ap, in0=ps2.ap[0:64, :], in1=ropeC[:, 1, tcn * 512:(tcn + 1) * 512], op=ALU.mult), r=[ps2, "ropeC"], w=[t1])
                            t2 = S2.take([64, 512])
                            E("pool", lambda e, t2=t2, kq=kq, tcn=tcn: e.tensor_tensor(out=t2.ap, in0=kq.ap, in1=ropeC[:, 0, tcn * 512:(tcn + 1) * 512], op=ALU.mult), r=[kq, "ropeC"], w=[t2])
                            E("dve", lambda e, t1=t1, t2=t2, kc=kc: e.tensor_tensor(out=krn.ap[:, kc * 512:(kc + 1) * 512], in0=t1.ap, in1=t2.ap, op=ALU.add), r=[t1, t2], w=[krn])
                    CP("mla_c")
                    for kt_ in range(NK // 128):
                        ps = psrot()

                        def mmv2(e, ps=ps, kt_=kt_, h=h):
                            ins = None
                            for r4 in range(4):
                                ins = e.matmul(ps.ap[:, 0:128], ckvT.ap[:, r4, kt_ * 128:(kt_ + 1) * 128], wuv.ap[:, r4, :], start=(r4 == 0), stop=(r4 == 3))
                            return ins
                        E("pe", mmv2, r=[wuv, ckvT], w=[ps])
                        copy(ev_eng(), vh.ap[:, kt_, :], ps.ap[:, 0:128], [ps], [vh])
                    CP("mla_d")
                    wq_ = load_w(W[:, 1792 + h * 192:1792 + (h + 1) * 192], KT, 192)
                    for tc in range(2):
                        S2.off = hb2
                        psn = psrot()
                        psr = psrot()

                        def mmq(e, psn=psn, psr=psr, tc=tc):
                            ins = None
                            for kt in range(KT):
                                ins = e.matmul(psn.ap, wq_.ap[:, kt, 0:128], hT[:, kt, tc * 512:(tc + 1) * 512], start=(kt == 0), stop=(kt == KT - 1))
                            for kt in range(KT):
                                ins = e.matmul(psr.ap[0:64, :], wq_.ap[:, kt, 128:192], hT[:, kt, tc * 512:(tc + 1) * 512], start=(kt == 0), stop=(kt == KT - 1))
                            return ins
                        E("pe", mmq, r=[wq_] + [hTt(kt, tc) for kt in range(KT)], w=[psn, psr])
                        ps_ss = psrot()
                        sumsq_bc([(psn, 128), (T(psr.ap[0:64, :], psr.keys), 64)], 512, ps_ss)
                        rs = S2.take([128, 512])
                        rstd_from(ps_ss, 512, 192, rs)
                        E("dve", lambda e, psn=psn, rs=rs, tc=tc: e.scalar_tensor_tensor(out=qn_n.ap[:, tc * 512:(tc + 1) * 512], in0=psn.ap, scalar=sp_(6), in1=rs.ap, op0=ALU.mult, op1=ALU.mult),
                          r=[psn, rs, "smallp"], w=[qn_n])
                        if not lat:
                            E("dve", lambda e, psr=psr, rs=rs, tc=tc: e.scalar_tensor_tensor(out=qn_r.ap[:, tc * 512:(tc + 1) * 512], in0=psr.ap[0:64, :], scalar=smallp[0:64, l, 7:8], in1=rs.ap[0:64, :], op0=ALU.mult, op1=ALU.mult),
                              r=[psr, rs, "smallp"], w=[qn_r])
                        else:
                            kq = S2.take([64, 512], BF16)
                            E("dve", lambda e, psr=psr, rs=rs, kq=kq: e.scalar_tensor_tensor(out=kq.ap, in0=psr.ap[0:64, :], scalar=smallp[0:64, l, 7:8], in1=rs.ap[0:64, :], op0=ALU.mult, op1=ALU.mult),
                              r=[psr, rs, "smallp"], w=[kq])
                            ps2 = psrot()
                            E("pe", lambda e, ps2=ps2, kq=kq: e.matmul(ps2.ap[0:64, :], permC[:, :], kq.ap, start=True, stop=True), r=[kq, "permC"], w=[ps2])
                            t1 = S2.take([64, 512])
                            E("dve", lambda e, t1=t1, ps2=ps2, tc=tc: e.tensor_tensor(out=t1.ap, in0=ps2.ap[0:64, :], in1=ropeC[:, 1, tc * 512:(tc + 1) * 512], op=ALU.mult), r=[ps2, "ropeC"], w=[t1])
                            t2 = S2.take([64, 512])
                            E("pool", lambda e, t2=t2, kq=kq, tc=tc: e.tensor_tensor(out=t2.ap, in0=kq.ap, in1=ropeC[:, 0, tc * 512:(tc + 1) * 512], op=ALU.mult), r=[kq, "ropeC"], w=[t2])
                            E("dve", lambda e, t1=t1, t2=t2, tc=tc: e.tensor_tensor(out=qn_r.ap[:, tc * 512:(tc + 1) * 512], in0=t1.ap, in1=t2.ap, op=ALU.add), r=[t1, t2], w=[qn_r])
                    S2.off = hb2
                    CP("mla_e")

                    def q_c(h_, q0, nq):
                        return [(T(qn_n.ap[:, q0:q0 + nq], qn_n.keys), 128), (T(qn_r.ap[:, q0:q0 + nq], qn_r.keys), 64)]

                    def k_c(h_, kt_):
                        return [(knT.ap[:, kt_ * 128:(kt_ + 1) * 128], 128), (krn.ap[:, kt_ * 128:(kt_ + 1) * 128], 64)]
                    k_c.keys = T(None, knT.keys + krn.keys)

                    def v_c(h_, kt_):
                        return vh.ap[:, kt_, :]
                    v_c.keys = vh
                    attention(1, q_c, k_c, v_c, 1.0 / math.sqrt(192.0), 10 + h)
                    CP("mla_f%d" % h)
                if DBG is not None and l == 0:
                    dump("mixC_g%d" % g, T(mixT[:, 10, 0:512], [("mixT", 10, 0)]), [128, 512])

                CP("mla")
                S2.off = 0
                uT = S2.take([128, 4, NT], BF16)
                yacc = S2.take([128, 4, NT])
                s5base = S2.off

                def cons_u(c0, cs, tc, ps, blk):
                    ct = blk * 2 + c0 // 128
                    if cfg.get("var") != "a":
                        copy("act", uT.ap[:, ct, tc * 512:(tc + 1) * 512], ps.ap, [ps], [uT])
                    if cfg.get("var") != "b":
                        E("dve", lambda e: e.tensor_scalar(out=yacc.ap[:, ct, tc * 512:(tc + 1) * 512], in0=ps.ap, scalar1=sp_(10 + ct), scalar2=None, op0=ALU.mult), r=[ps, "smallp"], w=[yacc])
                for blk in range(2):
                    lin_fm(W[:, blk * 256:(blk + 1) * 256], KT, 256, rhs_h, [0, 1], 512, lambda c0, cs, tc, ps, blk=blk: cons_u(c0, cs, tc, ps, blk))
                CP("s5_0")
                c5 = lambda i: s5c[:, i, :]
                K5 = ["s5c"]
                L_ = lambda nm: {"lamr": lamr, "lami": lami, "ldt": ldt}[nm][:, l, :]
                E("act", lambda e: e.activation(out=c5(6), in_=L_("ldt"), func=AF.Exp), r=["ldt"], w=K5)
                E("dve", lambda e: e.tensor_tensor(out=c5(0), in0=L_("lami"), in1=c5(6), op=ALU.mult), r=["lami"] + K5, w=K5)
                E("dve", lambda e: e.tensor_tensor(out=c5(7), in0=L_("lamr"), in1=c5(6), op=ALU.mult), r=["lamr"] + K5, w=K5)
                E("act", lambda e: e.activation(out=c5(1), in_=c5(7), func=AF.Exp), r=K5, w=K5)
                CP("s5_1")
                S3.off = 0
                rti, rtf, rty = S3.take([128, 32]), S3.take([128, 32]), S3.take([128, 32])
                th_all = T(c5(0), K5)
                sin_rr(th_all, 0.0, c5(3), K5, rti, rtf, rty, 32)
                sin_rr(th_all, 0.5 * math.pi, c5(2), K5, rti, rtf, rty, 32)
                CP("s5_2")
                cf = S2.take([128, 6, 32])
                cfa = lambda i: cf.ap[:, i, :]
                E("dve", lambda e: e.tensor_tensor(out=cfa(0), in0=c5(1), in1=c5(2), op=ALU.mult), r=K5, w=[cf])
                E("dve", lambda e: e.tensor_scalar(out=cfa(0), in0=cfa(0), scalar1=-1.0, scalar2=None, op0=ALU.add), r=[cf], w=[cf])
                E("dve", lambda e: e.tensor_tensor(out=cfa(1), in0=c5(1), in1=c5(3), op=ALU.mult), r=K5, w=[cf])
                E("dve", lambda e: e.tensor_tensor(out=cfa(2), in0=L_("lamr"), in1=L_("lamr"), op=ALU.mult), r=["lamr"], w=[cf])
                E("dve", lambda e: e.tensor_tensor(out=cfa(3), in0=L_("lami"), in1=L_("lami"), op=ALU.mult), r=["lami"], w=[cf])
                E("dve", lambda e: e.tensor_tensor(out=cfa(2), in0=cfa(2), in1=cfa(3), op=ALU.add), r=[cf], w=[cf])
                E("dve", lambda e: e.reciprocal(cfa(2), cfa(2)), r=[cf], w=[cf])
                E("dve", lambda e: e.tensor_tensor(out=cfa(3), in0=cfa(0), in1=L_("lamr"), op=ALU.mult), r=[cf, "lamr"], w=[cf])
                E("dve", lambda e: e.tensor_tensor(out=cfa(4), in0=cfa(1), in1=L_("lami"), op=ALU.mult), r=[cf, "lami"], w=[cf])
                E("dve", lambda e: e.tensor_tensor(out=cfa(3), in0=cfa(3), in1=cfa(4), op=ALU.add), r=[cf], w=[cf])
                E("dve", lambda e: e.tensor_tensor(out=c5(4), in0=cfa(3), in1=cfa(2), op=ALU.mult), r=[cf], w=K5)
                E("dve", lambda e: e.tensor_tensor(out=cfa(3), in0=cfa(1), in1=L_("lamr"), op=ALU.mult), r=[cf, "lamr"], w=[cf])
                E("dve", lambda e: e.tensor_tensor(out=cfa(4), in0=cfa(0), in1=L_("lami"), op=ALU.mult), r=[cf, "lami"], w=[cf])
                E("dve", lambda e: e.tensor_tensor(out=cfa(3), in0=cfa(3), in1=cfa(4), op=ALU.subtract), r=[cf], w=[cf])
                E("dve", lambda e: e.tensor_tensor(out=c5(5), in0=cfa(3), in1=cfa(2), op=ALU.mult), r=[cf], w=K5)
                init0 = S2.take([128, 2, 32])
                if lat:
                    h0 = S2.take([128, 2, 32])
                    DMA("sp", h0.ap[:, 0, :], h0r_d[l], w=[h0])
                    DMA("sp", h0.ap[:, 1, :], h0i_d[l], w=[h0])
                    E("dve", lambda e: e.tensor_tensor(out=cfa(0), in0=c5(2), in1=h0.ap[:, 0, :], op=ALU.mult), r=K5 + [h0], w=[cf])
                    E("dve", lambda e: e.tensor_tensor(out=cfa(1), in0=c5(3), in1=h0.ap[:, 1, :], op=ALU.mult), r=K5 + [h0], w=[cf])
                    E("dve", lambda e: e.tensor_tensor(out=init0.ap[:, 0, :], in0=cfa(0), in1=cfa(1), op=ALU.subtract), r=[cf], w=[init0])
                    E("dve", lambda e: e.tensor_tensor(out=cfa(0), in0=c5(3), in1=h0.ap[:, 0, :], op=ALU.mult), r=K5 + [h0], w=[cf])
                    E("dve", lambda e: e.tensor_tensor(out=cfa(1), in0=c5(2), in1=h0.ap[:, 1, :], op=ALU.mult), r=K5 + [h0], w=[cf])
                    E("dve", lambda e: e.tensor_tensor(out=init0.ap[:, 1, :], in0=cfa(0), in1=cfa(1), op=ALU.add), r=[cf], w=[init0])
                else:
                    E("dve", lambda e: e.memset(init0.ap, 0.0), w=[init0])
                jj = S2.take([128, 2, 256])
                DMA("sp", jj.ap, jj_d, w=[jj])
                CP("s5_a")
                LCH = 256
                UN = 512
                tab4s = [S2.take([128, 4, UN]) for _ in range(2)]
                chain = S2.take([128, 2, 2])
                if not lat:
                    maskz = S2.take([128, 2 * UN])
                    E("dve", lambda e: e.memset(maskz.ap, 1.0), w=[maskz])
                    for z in range(4):
                        E("dve", lambda e, z=z: e.memset(maskz.ap[:, z * 256:z * 256 + 1], 0.0), w=[maskz])
                bu2 = T(psall[:, 5:7, :], [("ps", 5), ("ps", 6)])
                pend = []

                def flush():
                    while pend:
                        psy, ct_, t0_ = pend.pop(0)
                        E("dve", lambda e: e.tensor_tensor(out=yacc.ap[:, ct_, t0_:t0_ + UN], in0=yacc.ap[:, ct_, t0_:t0_ + UN], in1=psy.ap, op=ALU.add), r=[psy, yacc], w=[yacc])
                for gp in range(16):
                    flush()
                    S3.off = 0
                    ct = gp // 4
                    Bst = S3.take([128, 4, 128])
                    if g == GROUPS[0]:
                        DMA("sp", Bst.ap, Bpad_d[l, gp], w=[Bst])
                    Cw = S3.take([128, 4, 128], BF16)
                    DMA("pool", Cw.ap, Cpad_d[l, gp], w=[Cw])
                    Bb = S3.take([128, 4, 128], BF16)
                    tmpB = S3.take([128, 128])
                    BT = S3.take([128, 4, 128], BF16)
                    a1 = S3.take([128, LCH])
                    a2 = S3.take([128, LCH])
                    a3 = S3.take([128, LCH])
                    a4 = S3.take([128, LCH])
                    rTz = S3.take([128, 2 * UN])
                    Pq = S3.take([128, 4, UN])
                    Xq = S3.take([128, 2, UN])
                    Gq = S3.take([128, 2, UN])
                    Hq = S3.take([128, 2, UN], BF16)
                    he = S3.take([128, 8])
                    first = (g == GROUPS[0])
                    if not first:
                        DMA("sp", BT.ap.rearrange("p a b -> p (a b)"), btc[l, gp], r=[("btc", l, gp)], w=[BT])
                    for dr in (range(2) if first else ()):
                        ci = dr * 16 + gp
                        cr_, ci_ = s5c[:, 4, ci:ci + 1], s5c[:, 5, ci:ci + 1]
                        br, bi = Bst.ap[:, dr * 2, :], Bst.ap[:, dr * 2 + 1, :]
                        E("dve", lambda e, bi=bi, ci_=ci_: e.tensor_scalar(out=tmpB.ap, in0=bi, scalar1=ci_, scalar2=None, op0=ALU.mult), r=[Bst] + K5, w=[tmpB])
                        E("dve", lambda e, br=br, cr_=cr_, dr=dr: e.scalar_tensor_tensor(out=Bb.ap[:, dr * 2, :], in0=br, scalar=cr_, in1=tmpB.ap, op0=ALU.mult, op1=ALU.subtract), r=[Bst, tmpB] + K5, w=[Bb])
                        E("dve", lambda e, br=br, ci_=ci_: e.tensor_scalar(out=tmpB.ap, in0=br, scalar1=ci_, scalar2=None, op0=ALU.mult), r=[Bst] + K5, w=[tmpB])
                        E("dve", lambda e, bi=bi, cr_=cr_, dr=dr: e.scalar_tensor_tensor(out=Bb.ap[:, dr * 2 + 1, :], in0=bi, scalar=cr_, in1=tmpB.ap, op0=ALU.mult, op1=ALU.add), r=[Bst, tmpB] + K5, w=[Bb])
                    if first:
                        ps = psrot()

                        def trB(e, ps=ps):
                            ins = None
                            for q in range(4):
                                ins = e.matmul(ps.ap[:, q * 128:(q + 1) * 128], Bb.ap[:, q, :], identb[:, :], start=True, stop=True)
                            return ins
                        E("pe", trB, r=[Bb, identbT], w=[ps])
                        copy("act", BT.ap, ps.ap.rearrange("p (a b) -> p a b", a=4), [ps], [BT])
                        DMA("sp", btc[l, gp], BT.ap.rearrange("p a b -> p (a b)"), r=[BT], w=[("btc", l, gp)])
                    for dr in range(2):
                        ci = dr * 16 + gp
                        tab4 = tab4s[dr]
                        th = s5c[:, 0, ci:ci + 1]
                        rsc = s5c[:, 1, ci:ci + 1]
                        if first:
                            E("dve", lambda e, th=th, dr=dr: e.tensor_scalar(out=a1.ap, in0=jj.ap[:, dr, :], scalar1=th, scalar2=None, op0=ALU.mult), r=[jj] + K5, w=[a1])
                            bc = lambda t_: t_.ap.unsqueeze(1).broadcast_to([128, 2, LCH])
                            halves = lambda pl: tab4.ap[:, pl, :].rearrange("p (a b) -> p a b", a=2)
                            sin_rr(a1, 0.0, None, None, a2, a3, a4, LCH)
                            E("act", lambda e: e.activation(out=halves(1), in_=bc(a4), func=AF.Sin), r=[a4], w=[tab4])
                            E("act", lambda e: e.activation(out=halves(2), in_=bc(a4), func=AF.Sin, scale=-1.0), r=[a4], w=[tab4])
                            E("dve", lambda e: e.tensor_scalar(out=a2.ap, in0=a4.ap, scalar1=0.5 * math.pi, scalar2=None, op0=ALU.add), r=[a4], w=[a2])
                            E("dve", lambda e: e.tensor_scalar(out=a3.ap, in0=a2.ap, scalar1=math.pi, scalar2=-TWO_PI, op0=ALU.is_gt, op1=ALU.mult), r=[a2], w=[a3])
                            E("dve", lambda e: e.tensor_tensor(out=a2.ap, in0=a2.ap, in1=a3.ap, op=ALU.add), r=[a2, a3], w=[a2])
                            E("act", lambda e: e.activation(out=halves(0), in_=bc(a2), func=AF.Sin), r=[a2], w=[tab4])
                            E("act", lambda e: e.activation(out=halves(3), in_=bc(a2), func=AF.Sin), r=[a2], w=[tab4])
                            DMA("sp", tabc[l, ci], tab4.ap.rearrange("p a b -> p (a b)"), r=[tab4], w=[("tabc", l, ci)])
                        else:
                            DMA("sp", tab4.ap.rearrange("p a b -> p (a b)"), tabc[l, ci], r=[("tabc", l, ci)], w=[tab4])
                        if not lat:
                            E("dve", lambda e, rsc=rsc: e.tensor_scalar(out=rTz.ap, in0=maskz.ap, scalar1=rsc, scalar2=None, op0=ALU.mult), r=[maskz] + K5, w=[rTz])
                        else:
                            E("dve", lambda e, rsc=rsc: e.tensor_scalar(out=rTz.ap[:, 0:LCH], in0=jj.ap[:, 0, :], scalar1=0.0, scalar2=rsc, op0=ALU.mult, op1=ALU.add), r=[jj] + K5, w=[rTz])
                            E("dve", lambda e, dr=dr, ci=ci: e.tensor_copy(chain.ap[:, dr, :], init0.ap[:, :, ci]), r=[init0], w=[chain])
                            le_ = LCH - 1 if dr == 0 else 0
                            cl_, sl__ = tab4.ap[:, 0, le_:le_ + 1], tab4.ap[:, 1, le_:le_ + 1]
                            cth_, sth_ = s5c[:, 2, ci:ci + 1], s5c[:, 3, ci:ci + 1]
                            E("dve", lambda e, sl__=sl__, sth_=sth_: e.tensor_tensor(out=he.ap[:, 2:3], in0=sl__, in1=sth_, op=ALU.mult), r=[tab4] + K5, w=[he])
                            E("dve", lambda e, cl_=cl_, cth_=cth_: e.scalar_tensor_tensor(out=he.ap[:, 4:5], in0=cl_, scalar=cth_, in1=he.ap[:, 2:3], op0=ALU.mult, op1=ALU.subtract), r=[tab4, he] + K5, w=[he])
                            E("dve", lambda e, sl__=sl__, cth_=cth_: e.tensor_tensor(out=he.ap[:, 3:4], in0=sl__, in1=cth_, op=ALU.mult), r=[tab4] + K5, w=[he])
                            E("dve", lambda e, cl_=cl_, sth_=sth_: e.scalar_tensor_tensor(out=he.ap[:, 5:6], in0=cl_, scalar=sth_, in1=he.ap[:, 3:4], op0=ALU.mult, op1=ALU.add), r=[tab4, he] + K5, w=[he])
                        tab2 = tab4.ap.rearrange("p (a b) c -> p a (b c)", a=2)
                        for ui in range(2):
                            u = ui if (dr == 0 or not lat) else 1 - ui
                            t0 = u * UN

                            def mmbu(e, dr=dr, t0=t0):
                                e.matmul(bu2.ap[:, 0, :], BT.ap[:, dr * 2, :], uT.ap[:, ct, t0:t0 + UN], start=True, stop=True)
                                return e.matmul(bu2.ap[:, 1, :], BT.ap[:, dr * 2 + 1, :], uT.ap[:, ct, t0:t0 + UN], start=True, stop=True)
                            E("pe", mmbu, r=[BT, uT], w=[bu2])
                            buf = bu2.ap.rearrange("p a b -> p (a b)").unsqueeze(1).broadcast_to([128, 2, 2 * UN])
                            P2 = Pq.ap.rearrange("p (a b) c -> p a (b c)", a=2)
                            E("dve", lambda e, buf=buf, P2=P2, tab2=tab2: e.tensor_tensor(out=P2, in0=buf, in1=tab2, op=ALU.mult), r=[bu2, tab4], w=[Pq])
                            P4 = Pq.ap.rearrange("p (a b) c -> p a b c", a=2)
                            E("dve", lambda e, P4=P4: e.tensor_tensor(out=Xq.ap, in0=P4[:, :, 0, :], in1=P4[:, :, 1, :], op=ALU.add), r=[Pq], w=[Xq])
                            flush()
                            Xf = Xq.ap.rearrange("p a b -> p (a b)")
                            Gf = Gq.ap.rearrange("p a b -> p (a b)")
                            if not lat:
                                sl = slice(None) if dr == 0 else slice(None, None, -1)
                                E("dve", lambda e, sl=sl, Xf=Xf, Gf=Gf: e.tensor_tensor_scan(out=Gf[:, sl], data0=rTz.ap, data1=Xf[:, sl], initial=0.0, op0=ALU.mult, op1=ALU.add), r=[Xq, rTz], w=[Gq])
                            else:
                                for cj in range(2):
                                    c = cj if dr == 0 else 1 - cj
                                    cs0 = c * LCH
                                    sl = slice(cs0, cs0 + LCH) if dr == 0 else slice(cs0 + LCH - 1, cs0 - 1 if cs0 > 0 else None, -1)
                                    for pl in range(2):
                                        E("dve", lambda e, sl=sl, pl=pl, dr=dr: e.tensor_tensor_scan(out=Gq.ap[:, pl, sl], data0=rTz.ap[:, 0:LCH], data1=Xq.ap[:, pl, sl], initial=chain.ap[:, dr, pl:pl + 1], op0=ALU.mult, op1=ALU.add), r=[Xq, rTz, chain], w=[Gq])
                                    le = LCH - 1 if dr == 0 else 0
                                    grl, gil = Gq.ap[:, 0, cs0 + le:cs0 + le + 1], Gq.ap[:, 1, cs0 + le:cs0 + le + 1]
                                    wr_, wi_ = he.ap[:, 4:5], he.ap[:, 5:6]
                                    E("dve", lambda e, gil=gil: e.tensor_tensor(out=he.ap[:, 0:1], in0=gil, in1=wi_, op=ALU.mult), r=[Gq, he], w=[he])
                                    E("dve", lambda e, grl=grl, dr=dr: e.scalar_tensor_tensor(out=chain.ap[:, dr, 0:1], in0=grl, scalar=wr_, in1=he.ap[:, 0:1], op0=ALU.mult, op1=ALU.subtract), r=[Gq, he], w=[chain])
                                    E("dve", lambda e, gil=gil: e.tensor_tensor(out=he.ap[:, 1:2], in0=gil, in1=wr_, op=ALU.mult), r=[Gq, he], w=[he])
                                    E("dve", lambda e, grl=grl, dr=dr: e.scalar_tensor_tensor(out=chain.ap[:, dr, 1:2], in0=grl, scalar=wi_, in1=he.ap[:, 1:2], op0=ALU.mult, op1=ALU.add), r=[Gq, he], w=[chain])
                            gbf = Gf.unsqueeze(1).broadcast_to([128, 2, 2 * UN])
                            E("dve", lambda e, gbf=gbf, P2=P2, tab2=tab2: e.tensor_tensor(out=P2, in0=gbf, in1=tab2, op=ALU.mult), r=[Gq, tab4], w=[Pq])
                            E("dve", lambda e, P4=P4: e.tensor_tensor(out=Hq.ap, in0=P4[:, :, 0, :], in1=P4[:, :, 1, :], op=ALU.subtract), r=[Pq], w=[Hq])
                            if not lat:
                                le = LCH - 1 if dr == 0 else 0
                                col = ((l * 4 + 2 * u) * 2 + dr) * 32 + gp * 2
                                E("dve", lambda e, le=le, col=col: e.tensor_tensor(out=hfin[:, col:col + 65:64], in0=Pq.ap[:, 0, le:le + 257:256], in1=Pq.ap[:, 1, le:le + 257:256], op=ALU.subtract), r=[Pq], w=["hfin"])
                                E("dve", lambda e, le=le, col=col: e.tensor_tensor(out=hfin[:, col + 1:col + 66:64], in0=Pq.ap[:, 3, le:le + 257:256], in1=Pq.ap[:, 2, le:le + 257:256], op=ALU.subtract), r=[Pq], w=["hfin"])
                            psy = psrot()

                            def mmy(e, psy=psy, dr=dr):
                                e.matmul(psy.ap, Cw.ap[:, dr * 2, :], Hq.ap[:, 0, :], start=True, stop=False)
                                return e.matmul(psy.ap, Cw.ap[:, dr * 2 + 1, :], Hq.ap[:, 1, :], start=False, stop=True)
                            E("pe", mmy, r=[Cw, Hq], w=[psy])
                            pend.append((psy, ct, t0))
                            ada_drain(1)
                flush()
                CP("s5_d")
                S2.off = s5base
                yb = S2.take([128, 4, NT], BF16)
                copy("act", yb.ap, yacc.ap, [yacc], [yb])
                if DBG is not None and l == 0:
                    dump("yacc_g%d" % g, T(yacc.ap[:, 0, 0:512], yacc.keys), [128, 512])
                S3.off = 0
                wg = S3.take([128, 4, 1024], BF16)
                sg = S3.take([128, 512])
                DMA("pool", wg.ap, w_glu[l].rearrange("(a p) n -> p a n", p=128), w=[wg])
                for j in range(4):
                    for tc in range(2):
                        psv, psg = psrot(), psrot()

                        def mmg(e, psv=psv, psg=psg, j=j, tc=tc):
                            ins = None
                            for a in range(4):
                                ins = e.matmul(psv.ap, wg.ap[:, a, j * 128:(j + 1) * 128], yb.ap[:, a, tc * 512:(tc + 1) * 512], start=(a == 0), stop=(a == 3))
                            for a in range(4):
                                ins = e.matmul(psg.ap, wg.ap[:, a, 512 + j * 128:512 + (j + 1) * 128], yb.ap[:, a, tc * 512:(tc + 1) * 512], start=(a == 0), stop=(a == 3))
                            return ins
                        E("pe", mmg, r=[wg, yb], w=[psv, psg])
                        E("act", lambda e, psg=psg, sg=sg: e.activation(out=sg.ap, in_=psg.ap, func=AF.Sigmoid), r=[psg], w=[sg])
                        mo = mixTt(j, tc * 512, 512)
                        E("dve", lambda e, psv=psv, sg=sg, mo=mo: e.tensor_tensor(out=mo.ap, in0=psv.ap, in1=sg.ap, op=ALU.mult), r=[psv, sg], w=[mo])
                if DBG is not None and l == 0:
                    dump("mixA_g%d" % g, T(mixT[:, 0, 0:512], [("mixT", 0, 0)]), [128, 512])

                CP("s5")
                ada_drain(upto_layer=l)
                E("dve", lambda e, l=l, g=g: e.tensor_copy(mods[:, 2, :], modall[:, l, 32:48, g]), r=[mk], w=["mods"])
                E("dve", lambda e, l=l, g=g: e.scalar_tensor_tensor(out=mods[:, 3, :], in0=modall[:, l, 64:80, g], scalar=1.0, in1=gmlp[:, l, :], op0=ALU.add, op1=ALU.mult), r=[mk, "gmlp"], w=["mods"])
                E("dve", lambda e, l=l, g=g: e.tensor_copy(mods[:, 4, :], modall[:, l, 48:64, g]), r=[mk], w=["mods"])
                E("dve", lambda e, l=l, g=g: e.tensor_copy(mods[:, 5, :], modall[:, l, 80:96, g]), r=[mk], w=["mods"])

                def resid(gsl, fc, tc, ps):
                    x_ = xTt(fc, tc)
                    E("dve", lambda e: e.scalar_tensor_tensor(out=x_.ap, in0=ps.ap, scalar=mods[:, gsl, fc:fc + 1], in1=x_.ap, op0=ALU.mult, op1=ALU.add), r=[ps, x_, "mods"], w=[x_])
                rhs_m = lambda kt, tc: mixTt(kt, tc * 512, 512)
                for cb in range(8):
                    lin_fm(w_out[l][:, cb * 256:(cb + 1) * 256], KT, 256, rhs_m, [0, 1], 512, lambda c0, cs, tc, ps, cb=cb: resid(2, cb * 2 + c0 // 128, tc, ps))
                if DBG is not None and l == 0:
                    dump("x1_g%d" % g, T(xT[:, 0, 0:512], [("xT", 0, 0)]), [128, 512])

                CP("out")
                norm_to_h(l, g, 3, 4)
                for qd in range(4):
                    S2.off = 0
                    rl = S2.take([128, 512], BF16)
                    rl2 = S2.take([128, 512], BF16)
                    rls = [rl, rl2]
                    cnt = {"i": 0}

                    def cons_ff1(c0, cs, tc, ps, jb):
                        j = jb * 2 + c0 // 128
                        r_ = rls[cnt["i"] % 2]
                        cnt["i"] += 1
                        E("act", lambda e: e.activation(out=r_.ap, in_=ps.ap, func=AF.Relu), r=[ps], w=[r_])
                        a_ = mixTt(j, tc * 512, 512)
                        E("dve", lambda e: e.tensor_tensor(out=a_.ap, in0=r_.ap, in1=r_.ap, op=ALU.mult), r=[r_], w=[a_])
                    for jb in range(8):
                        lin_fm(w_ff1[l][:, qd * 2048 + jb * 256:qd * 2048 + (jb + 1) * 256], KT, 256, rhs_h, [0, 1], 512, lambda c0, cs, tc, ps, jb=jb: cons_ff1(c0, cs, tc, ps, jb))
                    for cb in range(8):
                        lin_fm(w_ff2[l][qd * 2048:(qd + 1) * 2048, cb * 256:(cb + 1) * 256], KT, 256, rhs_m, [0, 1], 512, lambda c0, cs, tc, ps, cb=cb: resid(5, cb * 2 + c0 // 128, tc, ps))

            S2.off = 0
            ybufs = [S2.take([128, D]) for _ in range(2)]
            for tt in range(8):
                yb_ = ybufs[tt % 2]
                for k4 in range(4):
                    ps = psrot()

                    def trp2(e, ps=ps, k4=k4, tt=tt):
                        ins = None
                        for j in range(4):
                            kt = k4 * 4 + j
                            ins = e.transpose(ps.ap[:, j * 128:(j + 1) * 128], xT[:, kt, tt * 128:(tt + 1) * 128], ident[:])
                        return ins
                    E("pe", trp2, r=[idT] + [("xT", k4 * 4 + j, tt // 4) for j in range(4)], w=[ps])
                    copy(ev_eng(), yb_.ap[:, k4 * 512:(k4 + 1) * 512], ps.ap, [ps], [yb_])
                DMA("sp", y_d[g, tt * 128:(tt + 1) * 128, :], yb_.ap, r=[yb_])

        S2.off = 0
        for q in range(4):
            ps = psrot()
            E("pe", lambda e, ps=ps, q=q: e.transpose(ps.ap[:, 0:128], hfin[:, q * 128:(q + 1) * 128], ident[:]), r=["hfin", idT], w=[ps])
            o = S2.take([128, 128])
            copy("dve", o.ap, ps.ap[:, 0:128], [ps], [o])
            DMA("sp", nssm_d[q * 128:(q + 1) * 128, :], o.ap, r=[o])


    except _Stop:
        pass
    stats = P.finalize(st)
    st.close()
    return nc, stats


def _rope_tables(d):
    half, quarter = d // 2, d // 4
    t = np.arange(NT)
    row, col = (t // 64).astype(np.float32), (t % 64).astype(np.float32)
    inv = (10000.0 ** (-(np.arange(quarter, dtype=np.float32) / quarter))).astype(np.float32)
    cos = np.zeros((d, NT), np.float32)
    sin = np.zeros((d, NT), np.float32)
    perm = np.zeros((d, d), np.float32)
    for dd in range(d):
        pos = row if dd < half else col
        i = dd % quarter
        first = (dd % half) < quarter
        ang = pos * inv[i]
        cos[dd] = np.cos(ang)
        sin[dd] = -np.sin(ang) if first else np.sin(ang)
        partner = dd + quarter if first else dd - quarter
        perm[partner, dd] = 1.0
    return np.stack([cos, sin], axis=1).astype(np.float32), perm


def prep_inputs(inp):
    f = lambda a: np.ascontiguousarray(np.asarray(a, dtype=np.float32))
    sh = {}
    fm = lambda v: f(v.reshape(-1, 128).T)
    sh["w_mod"] = f(inp["w_mod"])
    sh["bmodT"] = f(np.stack([fm(inp["b_mod"][l]) for l in range(2)], axis=1))
    sh["gmixT"] = f(np.stack([fm(inp["norm_mix"][l]) for l in range(2)], axis=1))
    sh["gmlpT"] = f(np.stack([fm(inp["norm_mlp"][l]) for l in range(2)], axis=1))
    sh["w_in"] = f(inp["w_in"])
    sm = np.zeros((128, 2, 16), np.float32)
    for l in range(2):
        sm[:, l, 0] = inp["gqa_q_norm"][l]
        sm[:, l, 1] = inp["gqa_k_norm"][l]
        sm[:, l, 2:6] = fm(inp["mla_kv_norm"][l])
        sm[:, l, 6] = inp["mla_q_norm"][l][:128]
        sm[:64, l, 7] = inp["mla_q_norm"][l][128:]
        sm[:, l, 8] = inp["mla_k_norm"][l][:128]
        sm[:64, l, 9] = inp["mla_k_norm"][l][128:]
        sm[:, l, 10:14] = fm(inp["ssm_d"][l])
    sh["smallp"] = sm
    sh["w_uk"] = f(np.asarray(inp["mla_w_uk"]).reshape(2, 512, 768))
    sh["w_uv"] = f(np.asarray(inp["mla_w_uv"]).reshape(2, 512, 768))

    def st_layout(a):
        a = np.asarray(a, np.float32).reshape(2, 2, 16, 2, 64)
        return f(a.transpose(3, 4, 0, 1, 2).reshape(128, 2, 32))
    sh["lamr"] = st_layout(inp["ssm_lam_re"])
    sh["lami"] = st_layout(inp["ssm_lam_im"])
    sh["ldt"] = st_layout(np.broadcast_to(np.asarray(inp["ssm_log_dt"])[..., None], (2, 2, 32, 64)))
    Bpad = np.zeros((2, 16, 128, 4, 128), np.float32)
    Cpad = np.zeros((2, 16, 128, 4, 128), np.float32)
    bre, bim = np.asarray(inp["ssm_b_re"]), np.asarray(inp["ssm_b_im"])
    cre, cim = np.asarray(inp["ssm_c_re"]), np.asarray(inp["ssm_c_im"])
    for gp in range(16):
        for g2 in range(2):
            gg = gp * 2 + g2
            c0 = (gp % 4) * 32 + g2 * 16
            for dr in range(2):
                Bpad[:, gp, g2 * 64:(g2 + 1) * 64, dr * 2, c0:c0 + 16] = bre[:, dr, gg]
                Bpad[:, gp, g2 * 64:(g2 + 1) * 64, dr * 2 + 1, c0:c0 + 16] = bim[:, dr, gg]
                Cpad[:, gp, g2 * 64:(g2 + 1) * 64, dr * 2, c0:c0 + 16] = cre[:, dr, gg].transpose(0, 2, 1)
                Cpad[:, gp, g2 * 64:(g2 + 1) * 64, dr * 2 + 1, c0:c0 + 16] = cim[:, dr, gg].transpose(0, 2, 1)
    sh["Bpad"], sh["Cpad"] = Bpad, Cpad
    sh["w_glu"] = f(inp["ssm_w_glu"])
    sh["w_out"] = f(inp["w_out"])
    sh["w_ff1"] = f(inp["w_ff1"])
    sh["w_ff2"] = f(inp["w_ff2"])
    sh["ident"] = np.eye(128, dtype=np.float32)
    sh["ropeB"], sh["permB"] = _rope_tables(128)
    sh["ropeC"], sh["permC"] = _rope_tables(64)
    j = np.arange(256, dtype=np.float32)
    sh["jj"] = f(np.broadcast_to(np.stack([j, 255.0 - j])[None], (128, 2, 256)))
    percore = []
    xp, xs = np.asarray(inp["x_prompt"]), np.asarray(inp["x_sample"])
    for i in range(NCORES):
        d = dict(sh)
        d["xin"] = f(np.stack([xp[4 * i:4 * i + 4].reshape(NT, D), xs[i]]))
        d["ck"] = f(np.asarray(inp["cache_attn_k"])[i].reshape(2, 512, 256))
        d["cv"] = f(np.asarray(inp["cache_attn_v"])[i].reshape(2, 512, 256))
        d["cckv"] = f(np.asarray(inp["cache_mla_ckv"])[i])
        d["ckr"] = f(np.asarray(inp["cache_mla_krope"])[i])
        s = np.asarray(inp["state_ssm"])[i].reshape(2, 2, 16, 2, 64, 2)
        s = s.transpose(0, 3, 4, 1, 2, 5).reshape(2, 128, 32, 2)
        d["h0r"], d["h0i"] = f(s[..., 0]), f(s[..., 1])
        cv_ = np.stack([np.asarray(inp["c_ctx"]), np.asarray(inp["c"])[i]])
        d["cT"] = f(cv_.reshape(2, 16, 128).transpose(2, 1, 0))
        percore.append(d)
    return percore


_CACHE = {}


def kernel(**inputs):
    if "nc" not in _CACHE:
        _CACHE["nc"] = build()[0]
    nc = _CACHE["nc"]
    in_maps = prep_inputs(inputs)
    res = run_bass_kernel_spmd(nc, in_maps, core_ids=list(range(NCORES)))
    R = res.results
    y_prompt = np.concatenate([r["y"][0].reshape(4, 256, D) for r in R], axis=0)
    y_sample = np.stack([r["y"][1] for r in R], axis=0)

    def seqout(name, tail):
        return np.concatenate([r[name].reshape(2, 4, 256, -1).transpose(1, 0, 2, 3).reshape((4, 2, 256) + tail) for r in R], axis=0)
    new_k = seqout("nk", (2, 128))
    new_v = seqout("nv", (2, 128))
    new_ckv = seqout("nckv", (512,))
    new_kr = seqout("nkr", (64,))
    ss = []
    for r in R:
        a = r["nssm"].reshape(2, 4, 2, 16, 2, 2, 64)
        a = a.transpose(1, 0, 2, 3, 5, 6, 4).reshape(4, 2, 2, 32, 64, 2)
        ss.append(a)
    new_ssm = np.concatenate(ss, axis=0)
    f = lambda a: np.ascontiguousarray(a, dtype=np.float32)
    return (f(y_prompt), f(y_sample), f(new_k), f(new_v), f(new_ckv), f(new_kr), f(new_ssm))
```

```python
import math
import numpy as np
from contextlib import ExitStack
import concourse.bass as bass
import concourse.mybir as mybir
from concourse.bass_utils import run_bass_kernel_spmd

F32 = mybir.dt.float32
BF16 = mybir.dt.bfloat16
ALU = mybir.AluOpType
AF = mybir.ActivationFunctionType
AX = mybir.AxisListType

D = 2048
NT = 1024
KT = 16
DFF = 8192
INW = 3520
EPS = 1e-6
NCORES = 8
TWO_PI = 2.0 * math.pi


class _Rec:
    def __init__(self):
        self.calls = []

    def __getattr__(self, name):
        def f(*a, **k):
            self.calls.append((name, a, k))
            return None
        return f


class Prog:
    NDMA = 40

    def __init__(self, nc):
        self.nc = nc
        self.ops = []

    def op(self, eng, fn, reads=(), writes=(), dma=False):
        rec = _Rec()
        fn(rec)
        calls = rec.calls
        assert calls

        def emit(E, calls=calls):
            ins = None
            for (name, a, k) in calls:
                ins = getattr(E, name)(*a, **k)
            return ins
        self.ops.append((eng, emit, tuple(reads), tuple(writes), dma))

    def finalize(self, stack):
        nc = self.nc
        ops = self.ops
        n = len(ops)
        engs = ["pe", "act", "dve", "pool", "sp"]
        last_w, readers = {}, {}
        deps = [None] * n
        dma_prev, dma_slot, ndma = {}, [None] * n, 0
        for j, (eng, fn, reads, writes, dma) in enumerate(ops):
            d = set()
            for k in reads:
                w = last_w.get(k)
                if w is not None:
                    d.add(w)
            for k in writes:
                w = last_w.get(k)
                if w is not None:
                    d.add(w)
                d.update(readers.get(k, ()))
            if dma:
                slot = ndma % self.NDMA
                ndma += 1
                dma_slot[j] = slot
                if slot in dma_prev:
                    d.add(dma_prev[slot])
                dma_prev[slot] = j
            d.discard(j)
            deps[j] = d
            for k in reads:
                readers.setdefault(k, []).append(j)
            for k in writes:
                last_w[k] = j
                readers[k] = []
        needed = [False] * n
        for j in range(n):
            ej = ops[j][0]
            for i in deps[j]:
                if ops[i][4]:
                    continue
                if ops[i][0] == "pe" and ej == "pe" and not ops[j][4]:
                    continue
                needed[i] = True
        esem = {e: stack.enter_context(nc.semaphore("s_" + e)) for e in engs}
        dsem = [stack.enter_context(nc.semaphore("d%d" % i)) for i in range(self.NDMA)]
        ecount = {e: 0 for e in engs}
        dcount = [0] * self.NDMA
        sig = [None] * n
        seen = {e: {} for e in engs}
        snap = [None] * n
        plan = {e: [] for e in engs}
        nwait = 0
        for j, (eng, fn, reads, writes, dma) in enumerate(ops):
            sj = seen[eng]
            pl = plan[eng]
            for i in sorted(deps[j]):
                if (not ops[i][4]) and ops[i][0] == "pe" and eng == "pe" and not dma:
                    continue
                key, val = sig[i]
                if sj.get(key, 0) >= val:
                    continue
                pl.append((0, esem[key] if isinstance(key, str) else dsem[key], val))
                nwait += 1
                sj[key] = val
                for k2, v2 in snap[i].items():
                    if sj.get(k2, 0) < v2:
                        sj[k2] = v2
            if dma:
                slot = dma_slot[j]
                dcount[slot] += 16
                pl.append((1, fn, dsem[slot], 16))
                sig[j] = (slot, dcount[slot])
                snap[j] = dict(sj)
            elif needed[j]:
                ecount[eng] += 1
                pl.append((1, fn, esem[eng], 1))
                sig[j] = (eng, ecount[eng])
                snap[j] = dict(sj)
            else:
                pl.append((1, fn, None, 0))
        for slot in range(self.NDMA):
            if dcount[slot]:
                plan["sp"].append((0, dsem[slot], dcount[slot]))

        def replay(E, pl):
            for it in pl:
                if it[0] == 0:
                    E.wait_ge(it[1], it[2])
                else:
                    ins = it[1](E)
                    if it[2] is not None:
                        ins.then_inc(it[2], it[3])

        block = stack.enter_context(nc.Block())

        @block.tensor
        def _(e):
            replay(e, plan["pe"])

        @block.scalar
        def _(e):
            replay(e, plan["act"])

        @block.vector
        def _(e):
            replay(e, plan["dve"])

        @block.gpsimd
        def _(e):
            replay(e, plan["pool"])

        @block.sync
        def _(e):
            replay(e, plan["sp"])

        return dict(n_ops=n, n_wait=nwait, counts=ecount)


class _Stop(Exception):
    pass


class T:
    def __init__(self, ap, keys):
        self.ap = ap
        self.keys = list(keys)

    def __getitem__(self, idx):
        return T(self.ap[idx], self.keys)


def _keys(ts):
    out = []
    for t in ts:
        if isinstance(t, T):
            out.extend(t.keys)
        else:
            out.append(t)
    return out


def build(cfg=None):
    cfg = cfg or {}
    GROUPS = cfg.get("groups", [0, 1])
    NL = cfg.get("layers", 2)
    DBG = cfg.get("debug", None)

    nc = bass.Bass("TRN2", target_bir_lowering=False)

    def din(name, shape):
        return nc.dram_tensor(name, list(shape), F32, kind="ExternalInput").ap()

    def dout(name, shape):
        return nc.dram_tensor(name, list(shape), F32, kind="ExternalOutput").ap()

    xin = din("xin", [2, NT, D])
    ck = din("ck", [2, 512, 256])
    cv = din("cv", [2, 512, 256])
    cckv = din("cckv", [2, 512, 512])
    ckr = din("ckr", [2, 512, 64])
    h0r_d = din("h0r", [2, 128, 32])
    h0i_d = din("h0i", [2, 128, 32])
    cT_d = din("cT", [128, 16, 2])
    w_mod = din("w_mod", [2, D, 6 * D])
    bmod_d = din("bmodT", [128, 2, 96])
    gmix_d = din("gmixT", [128, 2, 16])
    gmlp_d = din("gmlpT", [128, 2, 16])
    w_in = din("w_in", [2, D, INW])
    sm_d = din("smallp", [128, 2, 16])
    w_uk = din("w_uk", [2, 512, 768])
    w_uv = din("w_uv", [2, 512, 768])
    lamr_d = din("lamr", [128, 2, 32])
    lami_d = din("lami", [128, 2, 32])
    ldt_d = din("ldt", [128, 2, 32])
    Bpad_d = din("Bpad", [2, 16, 128, 4, 128])
    Cpad_d = din("Cpad", [2, 16, 128, 4, 128])
    w_glu = din("w_glu", [2, 512, 1024])
    w_out = din("w_out", [2, D, D])
    w_ff1 = din("w_ff1", [2, D, DFF])
    w_ff2 = din("w_ff2", [2, DFF, D])
    ident_d = din("ident", [128, 128])
    ropeB_d = din("ropeB", [128, 2, NT])
    ropeC_d = din("ropeC", [64, 2, NT])
    permB_d = din("permB", [128, 128])
    permC_d = din("permC", [64, 64])
    jj_d = din("jj", [128, 2, 256])

    y_d = dout("y", [2, NT, D])
    nk_d = dout("nk", [2, NT, 256])
    nv_d = dout("nv", [2, NT, 256])
    nckv_d = dout("nckv", [2, NT, 512])
    nkr_d = dout("nkr", [2, NT, 64])
    nssm_d = dout("nssm", [512, 128])

    tabc = nc.dram_tensor("tabc", [2, 32, 128, 4 * 512], F32, kind="Internal").ap()
    btc = nc.dram_tensor("btc", [2, 16, 128, 4 * 128], BF16, kind="Internal").ap()
    st = ExitStack()
    P = Prog(nc)

    def CP(name):
        if cfg.get("stop") == name:
            raise _Stop()

    def E(eng, fn, r=(), w=()):
        rk, wk = _keys(r), _keys(w)
        wk = wk + [k for k in rk if isinstance(k, tuple) and k[0] == "ps" and k not in wk]
        P.op(eng, fn, reads=rk, writes=wk)

    def DMA(eng, out, in_, r=(), w=()):
        P.op(eng, lambda e: e.dma_start(out=out, in_=in_), reads=_keys(r), writes=_keys(w), dma=True)

    def sb(name, shape, dt=F32):
        return st.enter_context(nc.sbuf_tensor("sb_" + name, list(shape), dt))

    xT = sb("xT", [128, KT, NT])
    hT = sb("hT", [128, KT, NT], BF16)
    mixT = sb("mixT", [128, KT, NT], BF16)
    NWB = 2
    wts = [sb("wt%d" % i, [128, KT, 256], BF16) for i in range(NWB)]
    ident = sb("ident", [128, 128])
    identb = sb("identb", [128, 128], BF16)
    onesb = sb("onesb", [128, 128], BF16)
    permB = sb("permBs", [128, 128], BF16)
    permC = sb("permCs", [64, 64], BF16)
    ropeB = sb("ropeBs", [128, 2, NT], BF16)
    ropeC = sb("ropeCs", [64, 2, NT], BF16)
    scT = sb("scT", [128, KT, 2], BF16)
    modall = sb("modall", [128, 2, 96, 2])
    bmod = sb("bmod", [128, 2, 96])
    gmix = sb("gmix", [128, 2, 16])
    gmlp = sb("gmlp", [128, 2, 16])
    smallp = sb("smallps", [128, 2, 16])
    mods = sb("mods", [128, 6, 16])
    lamr = sb("lamrs", [128, 2, 32])
    lami = sb("lamis", [128, 2, 32])
    ldt = sb("ldts", [128, 2, 32])
    s5c = sb("s5c", [128, 8, 32])
    hfin = sb("hfin", [128, 512])
    SCRN = 12160
    scr = sb("scr", [128, SCRN])

    def xTt(kt, tc, n=512):
        return T(xT[:, kt, tc * 512:tc * 512 + n], [("xT", kt, tc)])

    def hTt(kt, tc):
        return T(hT[:, kt, tc * 512:(tc + 1) * 512], [("hT", kt, tc)])

    def mixTt(kt, t0, n):
        return T(mixT[:, kt, t0:t0 + n], [("mixT", kt, c) for c in range(t0 // 512, (t0 + n - 1) // 512 + 1)])

    hT_f32 = hT[:].rearrange("p a b -> p (a b)").bitcast(F32)

    class Scr:
        def __init__(self, backing=None):
            self.off = 0
            self.hT = backing is not None
            self.base = hT_f32 if self.hT else scr
            self.cap = 8192 if self.hT else SCRN

        def take(self, shape, dt=F32, parts=128):
            n = int(np.prod(shape[1:]))
            nf = n if dt == F32 else (n + 1) // 2
            nf = (nf + 31) // 32 * 32
            assert self.off + nf <= self.cap, ("scratch overflow", self.hT, self.off, nf)
            ap = self.base[0:shape[0], self.off:self.off + nf]
            if self.hT:
                keys = sorted({("hT", (2 * o) // 1024, ((2 * o) % 1024) // 512) for o in range(self.off, self.off + nf, 32)} |
                              {("hT", (2 * o + 62) // 1024, ((2 * o + 62) % 1024) // 512) for o in range(self.off, self.off + nf, 32)})
            else:
                keys = [("scr", c) for c in range(self.off // 128, (self.off + nf - 1) // 128 + 1)]
            self.off += nf
            if dt != F32:
                ap = ap.bitcast(dt)[:, 0:n]
            else:
                ap = ap[:, 0:n]
            if len(shape) == 3:
                ap = ap.rearrange("p (a b) -> p a b", a=shape[1])
            elif len(shape) == 4:
                ap = ap.rearrange("p (a b c) -> p a b c", a=shape[1], b=shape[2])
            return T(ap, keys)

    psall = st.enter_context(nc.psum_tensor("psall", [128, 8, 512], F32))
    pbanks = [psall[:, i, :] for i in range(8)]
    rot = {"i": 0}

    def psrot():
        i = rot["i"] % 5
        rot["i"] += 1
        return T(pbanks[i], [("ps", i)])

    psA = T(pbanks[5], [("ps", 5)])
    psB = T(pbanks[6], [("ps", 6)])
    psC = T(pbanks[7], [("ps", 7)])

    evq = {"i": 0}

    def ev_eng():
        evq["i"] += 1
        return "act" if evq["i"] % 2 else "dve"

    def copy(eng, out, in_, r, w):
        if eng == "act":
            E("act", lambda e: e.activation(out=out, in_=in_, func=AF.Copy), r, w)
        else:
            E(eng, lambda e: e.tensor_copy(out, in_), r, w)

    dbg_n = {"i": 0}

    def dump(name, t, shape):
        if DBG is None:
            return
        d = dout("dbg_" + name, shape)
        DBG.append(name)
        DMA("pool", d, t.ap, r=[t])

    wq = {"i": 0}

    def load_w(src2d, nk, ncols):
        i = wq["i"] % NWB
        wq["i"] += 1
        t = T(wts[i][:], [("wt", i)])
        half = (nk + 1) // 2
        for a, b in ((0, half), (half, nk)):
            if b > a:
                DMA("pool", wts[i][:, a:b, 0:ncols], src2d[a * 128:b * 128, :].rearrange("(kt p) n -> p kt n", p=128), w=[t])
        return t

    def lin_fm(src2d, nk, ncols, rhs_fn, tcs, ntc, consume, cchunks=None):
        wt = load_w(src2d, nk, ncols)
        if cchunks is None:
            cchunks = [(c, min(128, ncols - c)) for c in range(0, ncols, 128)]
        prevc = None
        for (c0, cs) in cchunks:
            for tc in tcs:
                ps = psrot()
                rl = [rhs_fn(kt, tc) for kt in range(nk)]

                def mm(e, ps=ps, c0=c0, cs=cs, rl=rl):
                    ins = None
                    for kt in range(nk):
                        ins = e.matmul(ps.ap[0:cs, 0:ntc], wt.ap[:, kt, c0:c0 + cs], rl[kt].ap, start=(kt == 0), stop=(kt == nk - 1))
                    return ins
                E("pe", mm, r=[wt] + rl, w=[ps])
                if prevc is not None:
                    consume(*prevc)
                prevc = (c0, cs, tc, ps)
        if prevc is not None:
            consume(*prevc)

    I32 = mybir.dt.int32
    C1 = 6.28125
    C2 = TWO_PI - 6.28125

    def sin_rr(x, shift, out_ap, out_keys, ti, tf, ty, n):
        xk = x.keys
        E("dve", lambda e: e.tensor_scalar(out=ti.ap.bitcast(I32), in0=x.ap, scalar1=shift, scalar2=1.0 / TWO_PI, op0=ALU.add, op1=ALU.mult), r=[x], w=[ti])
        E("dve", lambda e: e.tensor_copy(tf.ap, ti.ap.bitcast(I32)), r=[ti], w=[tf])
        E("dve", lambda e: e.tensor_scalar(out=ty.ap, in0=x.ap, scalar1=shift, scalar2=None, op0=ALU.add), r=[x], w=[ty])
        E("dve", lambda e: e.scalar_tensor_tensor(out=ty.ap, in0=tf.ap, scalar=-C1, in1=ty.ap, op0=ALU.mult, op1=ALU.add), r=[tf, ty], w=[ty])
        E("dve", lambda e: e.scalar_tensor_tensor(out=ty.ap, in0=tf.ap, scalar=-C2, in1=ty.ap, op0=ALU.mult, op1=ALU.add), r=[tf, ty], w=[ty])
        E("dve", lambda e: e.tensor_scalar(out=tf.ap, in0=ty.ap, scalar1=math.pi, scalar2=-TWO_PI, op0=ALU.is_gt, op1=ALU.mult), r=[ty], w=[tf])
        E("dve", lambda e: e.tensor_tensor(out=ty.ap, in0=ty.ap, in1=tf.ap, op=ALU.add), r=[ty, tf], w=[ty])
        E("dve", lambda e: e.tensor_scalar(out=tf.ap, in0=ty.ap, scalar1=-math.pi, scalar2=TWO_PI, op0=ALU.is_lt, op1=ALU.mult), r=[ty], w=[tf])
        E("dve", lambda e: e.tensor_tensor(out=ty.ap, in0=ty.ap, in1=tf.ap, op=ALU.add), r=[ty, tf], w=[ty])
        if out_ap is not None:
            E("act", lambda e: e.activation(out=out_ap, in_=ty.ap, func=AF.Sin), r=[ty], w=out_keys)

    try:
        S = Scr()
        DMA("sp", ident[:], ident_d, w=["ident"])
        idT = T(ident[:], ["ident"])
        identbT = T(identb[:], ["identb"])
        onesT = T(onesb[:], ["onesb"])
        E("dve", lambda e: e.tensor_copy(identb[:], ident[:]), r=["ident"], w=["identb"])
        E("dve", lambda e: e.memset(onesb[:], 1.0), w=["onesb"])
        DMA("pool", permB[:], permB_d, w=["permB"])
        DMA("pool", permC[:], permC_d, w=["permC"])
        DMA("pool", ropeB[:], ropeB_d, w=["ropeB"])
        DMA("pool", ropeC[:], ropeC_d, w=["ropeC"])
        DMA("sp", bmod[:], bmod_d, w=["bmod"])
        DMA("sp", gmix[:], gmix_d, w=["gmix"])
        DMA("sp", gmlp[:], gmlp_d, w=["gmlp"])
        DMA("sp", smallp[:], sm_d, w=["smallp"])
        DMA("sp", lamr[:], lamr_d, w=["lamr"])
        DMA("sp", lami[:], lami_d, w=["lami"])
        DMA("sp", ldt[:], ldt_d, w=["ldt"])
        ctmp = S.take([128, 32])
        DMA("sp", ctmp.ap, cT_d.rearrange("p a b -> p (a b)"), w=[ctmp])
        E("act", lambda e: e.activation(out=scT[:].rearrange("p a b -> p (a b)"), in_=ctmp.ap, func=AF.Silu), r=[ctmp], w=["scT"])
        scTt = T(scT[:], ["scT"])
        E("dve", lambda e: e.memset(hfin[:], 0.0), w=["hfin"])

        def adaln_tile(l, ti):
            wt = load_w(w_mod[l, :, ti * 256:(ti + 1) * 256], 16, 256)
            ps = psrot()

            def mm(e, ps=ps, wt=wt):
                ins = None
                for cj in range(2):
                    for kt in range(16):
                        ins = e.matmul(ps.ap[:, cj * 2:cj * 2 + 2], wt.ap[:, kt, cj * 128:(cj + 1) * 128], scT[:, kt, :], start=(kt == 0), stop=(kt == 15))
                return ins
            E("pe", mm, r=[wt, scTt], w=[ps])
            for cj in range(2):
                n = ti * 2 + cj
                E("dve", lambda e, ps=ps, cj=cj, n=n, l=l: e.tensor_scalar(out=modall[:, l, n, :], in0=ps.ap[:, cj * 2:cj * 2 + 2], scalar1=bmod[:, l, n:n + 1], scalar2=None, op0=ALU.add),
                  r=[ps, "bmod"], w=[("modall", l)])
        ada_pend = [(l, ti) for l in range(NL) for ti in range(48)]

        def ada_drain(n=None, upto_layer=None, upto_ti=None):
            k = 0
            while ada_pend and (n is None or k < n):
                if upto_layer is not None and (ada_pend[0][0] > upto_layer or (ada_pend[0][0] == upto_layer and upto_ti is not None and ada_pend[0][1] >= upto_ti)):
                    break
                adaln_tile(*ada_pend.pop(0))
                k += 1

        CP("pre")
        def sumsq_bc(parts, ntc, ps_out):
            sqs = []
            for (src, k) in parts:
                sq = S2.take([128, ntc], BF16)
                E("act", lambda e, sq=sq, src=src, k=k: e.activation(out=sq.ap[0:k, :], in_=src.ap, func=AF.Square), r=[src], w=[sq])
                sqs.append((sq, k))

            def mm(e):
                ins = None
                for i, (sq, k) in enumerate(sqs):
                    ins = e.matmul(ps_out.ap[:, 0:ntc], onesb[0:k, :], sq.ap[0:k, :], start=(i == 0), stop=(i == len(sqs) - 1))
                return ins
            E("pe", mm, r=[onesT] + [s for s, _ in sqs], w=[ps_out])

        def rstd_from(ps_ss, ntc, dim, out):
            E("dve", lambda e: e.tensor_scalar(out=out.ap, in0=ps_ss.ap[:, 0:ntc], scalar1=1.0 / dim, scalar2=EPS, op0=ALU.mult, op1=ALU.add), r=[ps_ss], w=[out])
            E("act", lambda e: e.activation(out=out.ap, in_=out.ap, func=AF.Ln), r=[out], w=[out])
            E("act", lambda e: e.activation(out=out.ap, in_=out.ap, func=AF.Exp, scale=-0.5), r=[out], w=[out])

        trq = {"i": 0, "bufs": None}

        def tr_out(src_fm, k, ntok, dst_dram_rows, colsl):
            for t0 in range(0, ntok, 128):
                ps = psC
                E("pe", lambda e, ps=ps, t0=t0: e.transpose(ps.ap[:, 0:k], src_fm.ap[0:k, t0:t0 + 128], ident[0:k, 0:k]), r=[src_fm, idT], w=[ps])
                o = trq["bufs"][trq["i"] % 2]
                trq["i"] += 1
                copy(ev_eng(), o.ap[:, 0:k], ps.ap[:, 0:k], [ps], [o])
                DMA("sp", dst_dram_rows(t0)[:, colsl], o.ap[:, 0:k], r=[o])

        def norm_to_h(l, g, gsl, ssl):
            for tc in range(2):
                S2.off = 0
                sqs = [S2.take([128, 512], BF16) for _ in range(2)]
                rs = S2.take([128, 512])
                tmps = [S2.take([128, 512]) for _ in range(2)]
                ps_ss = psA
                for kt in range(KT):
                    sq = sqs[kt % 2]
                    x_ = xTt(kt, tc)
                    E("act", lambda e, sq=sq, x_=x_: e.activation(out=sq.ap, in_=x_.ap, func=AF.Square), r=[x_], w=[sq])
                    E("pe", lambda e, sq=sq, kt=kt: e.matmul(ps_ss.ap, onesb[:, :], sq.ap, start=(kt == 0), stop=(kt == KT - 1)), r=[sq, onesT], w=[ps_ss])
                rstd_from(ps_ss, 512, D, rs)
                for kt in range(KT):
                    tmp = tmps[kt % 2]
                    x_ = xTt(kt, tc)
                    h_ = hTt(kt, tc)
                    E("dve", lambda e, tmp=tmp, x_=x_, kt=kt: e.scalar_tensor_tensor(out=tmp.ap, in0=x_.ap, scalar=mods[:, gsl, kt:kt + 1], in1=rs.ap, op0=ALU.mult, op1=ALU.mult),
                      r=[x_, rs, "mods"], w=[tmp])
                    E("act", lambda e, tmp=tmp, h_=h_, kt=kt: e.activation(out=h_.ap, in_=tmp.ap, func=AF.Identity, bias=mods[:, ssl, kt:kt + 1], scale=1.0), r=[tmp, "mods"], w=[h_])

        S2 = Scr()
        S3 = Scr(backing=True)
        pregen_done = set()

        for g in GROUPS:
            lat = (g == 1)
            S2.off = 0
            xtoks = [S2.take([128, D]) for _ in range(2)]
            for tt in range(8):
                xt = xtoks[tt % 2]
                DMA("sp", xt.ap, xin[g, tt * 128:(tt + 1) * 128, :], w=[xt])
                for k4 in range(4):
                    ps = psrot()

                    def trp(e, ps=ps, xt=xt, k4=k4):
                        ins = None
                        for j in range(4):
                            kt = k4 * 4 + j
                            ins = e.transpose(ps.ap[:, j * 128:(j + 1) * 128], xt.ap[:, kt * 128:(kt + 1) * 128], ident[:])
                        return ins
                    E("pe", trp, r=[xt, idT], w=[ps])
                    wkeys = [("xT", k4 * 4 + j, tt // 4) for j in range(4)]
                    copy(ev_eng(), xT[:, k4 * 4:k4 * 4 + 4, tt * 128:(tt + 1) * 128], ps.ap.rearrange("p (a b) -> p a b", a=4), [ps], wkeys)

            CP("load")
            for l in range(NL):
                ada_drain(upto_layer=l, upto_ti=16)
                mk = ("modall", l)
                E("dve", lambda e, l=l, g=g: e.scalar_tensor_tensor(out=mods[:, 0, :], in0=modall[:, l, 16:32, g], scalar=1.0, in1=gmix[:, l, :], op0=ALU.add, op1=ALU.mult), r=[mk, "gmix"], w=["mods"])
                E("dve", lambda e, l=l, g=g: e.tensor_copy(mods[:, 1, :], modall[:, l, 0:16, g]), r=[mk], w=["mods"])

                norm_to_h(l, g, 0, 1)
                rhs_h = lambda kt, tc: hTt(kt, tc)
                W = w_in[l]
                sp_ = lambda c: smallp[:, l, c:c + 1]
                NK = 1536 if lat else 1024
                KOFF = 512 if lat else 0
                if DBG is not None and l == 0:
                    dump("hT_g%d" % g, T(hT[:, 0, 0:512], [("hT", 0, 0)]), [128, 512])

                CP("norm1")
                S2.off = 0
                qT = S2.take([128, 6, NT], BF16)
                kT = S2.take([128, 2, NK], BF16)
                vtok = S2.take([128, NK // 128, 256], BF16)
                trq["bufs"] = [S2.take([128, 128]) for _ in range(2)]
                gq_base = S2.off
                if lat:
                    ctmp = S2.take([128, 4, 256])
                    DMA("sp", ctmp.ap, ck[l].rearrange("(a p) n -> p a n", p=128), w=[ctmp])
                    DMA("pool", vtok.ap[:, 0:4, :], cv[l].rearrange("(a p) n -> p a n", p=128), w=[vtok])
                    for hk in range(2):
                        ps = psrot()

                        def trk(e, ps=ps, hk=hk):
                            ins = None
                            for a in range(4):
                                ins = e.transpose(ps.ap[:, a * 128:(a + 1) * 128], ctmp.ap[:, a, hk * 128:(hk + 1) * 128], ident[:])
                            return ins
                        E("pe", trk, r=[ctmp, idT], w=[ps])
                        copy(ev_eng(), kT.ap[:, hk, 0:512], ps.ap, [ps], [kT])

                def gqa_qk(c0, cs, tc, ps, base):
                    S2.off = base
                    hidx = c0 // 128
                    isq = hidx < 6
                    ps_ss = psrot()
                    sumsq_bc([(ps, 128)], 512, ps_ss)
                    rs = S2.take([128, 512])
                    rstd_from(ps_ss, 512, 128, rs)
                    gcol = sp_(0) if isq else sp_(1)
                    if isq:
                        dst = T(qT.ap[:, hidx, tc * 512:(tc + 1) * 512], qT.keys)
                    else:
                        dst = T(kT.ap[:, hidx - 6, KOFF + tc * 512:KOFF + (tc + 1) * 512], kT.keys)
                    if not lat:
                        if isq:
                            E("dve", lambda e: e.scalar_tensor_tensor(out=dst.ap, in0=ps.ap, scalar=gcol, in1=rs.ap, op0=ALU.mult, op1=ALU.mult), r=[ps, rs, "smallp"], w=[dst])
                        else:
                            kn = S2.take([128, 512])
                            E("dve", lambda e: e.scalar_tensor_tensor(out=kn.ap, in0=ps.ap, scalar=gcol, in1=rs.ap, op0=ALU.mult, op1=ALU.mult), r=[ps, rs, "smallp"], w=[kn])
                            copy("act", dst.ap, kn.ap, [kn], [dst])
                            hk = hidx - 6
                            tr_out(kn, 128, 512, lambda t0: nk_d[l, tc * 512 + t0:tc * 512 + t0 + 128, :], slice(hk * 128, (hk + 1) * 128))
                    else:
                        qn = S2.take([128, 512], BF16)
                        E("dve", lambda e: e.scalar_tensor_tensor(out=qn.ap, in0=ps.ap, scalar=gcol, in1=rs.ap, op0=ALU.mult, op1=ALU.mult), r=[ps, rs, "smallp"], w=[qn])
                        ps2 = psrot()
                        E("pe", lambda e: e.matmul(ps2.ap, permB[:, :], qn.ap, start=True, stop=True), r=[qn, "permB"], w=[ps2])
                        t1 = S2.take([128, 512])
                        E("dve", lambda e: e.tensor_tensor(out=t1.ap, in0=ps2.ap, in1=ropeB[:, 1, tc * 512:(tc + 1) * 512], op=ALU.mult), r=[ps2, "ropeB"], w=[t1])
                        t2 = S2.take([128, 512])
                        E("pool", lambda e: e.tensor_tensor(out=t2.ap, in0=qn.ap, in1=ropeB[:, 0, tc * 512:(tc + 1) * 512], op=ALU.mult), r=[qn, "ropeB"], w=[t2])
                        E("dve", lambda e: e.tensor_tensor(out=dst.ap, in0=t1.ap, in1=t2.ap, op=ALU.add), r=[t1, t2], w=[dst])

                for cblk in range(4):
                    lin_fm(W[:, 512 + cblk * 256:512 + (cblk + 1) * 256], KT, 256, rhs_h, [0, 1], 512,
                           lambda c0, cs, tc, ps, cblk=cblk: gqa_qk(cblk * 256 + c0, cs, tc, ps, gq_base))
                wtv = load_w(W[:, 1536:1792], KT, 256)
                for tt in range(8):
                    S2.off = gq_base
                    ps = psrot()

                    def mmv(e, ps=ps, tt=tt):
                        ins = None
                        for kt in range(KT):
                            ins = e.matmul(ps.ap[:, 0:256], hT[:, kt, tt * 128:(tt + 1) * 128], wtv.ap[:, kt, :], start=(kt == 0), stop=(kt == KT - 1))
                        return ins
                    E("pe", mmv, r=[wtv] + [hTt(kt, tt // 4) for kt in range(KT)], w=[ps])
                    copy("act", vtok.ap[:, KOFF // 128 + tt, :], ps.ap[:, 0:256], [ps], [vtok])
                    if not lat:
                        vo = S2.take([128, 256])
                        copy("dve", vo.ap, ps.ap[:, 0:256], [ps], [vo])
                        DMA("sp", nv_d[l, tt * 128:(tt + 1) * 128, :], vo.ap, r=[vo])

                def attention(nheads, qfn, kfn, vfn, scale, mix0):
                    if lat:
                        blocks = [(0, 512, list(range(12))), (512, 512, list(range(12)))]
                    else:
                        blocks = [(s * 256, 256, [2 * s, 2 * s + 1]) for s in range(4)]
                    abase = S2.off
                    for h in range(nheads):
                        for (q0, nq, ktiles) in blocks:
                            S2.off = abase
                            ets = [S2.take([128, nq], BF16) for _ in range(3)]
                            qparts = qfn(h, q0, nq)
                            nk_ = len(ktiles)
                            prev = None

                            def pv(ki, et, vap, nq=nq, nk_=nk_):
                                E("pe", lambda e: e.matmul(psA.ap[:, 0:nq], vap, et.ap, start=(ki == 0), stop=(ki == nk_ - 1)), r=[et, vfn.keys], w=[psA])
                                E("pe", lambda e: e.matmul(psB.ap[:, 0:nq], onesb[:, :], et.ap, start=(ki == 0), stop=(ki == nk_ - 1)), r=[et, onesT], w=[psB])
                            for ki, kt_ in enumerate(ktiles):
                                ps = psrot()
                                kparts = kfn(h, kt_)

                                def mms(e, ps=ps, kparts=kparts, qparts=qparts, nq=nq):
                                    ins = None
                                    for i, ((kap, kk), (qt_, qk)) in enumerate(zip(kparts, qparts)):
                                        ins = e.matmul(ps.ap[:, 0:nq], kap, qt_.ap, start=(i == 0), stop=(i == len(kparts) - 1))
                                    return ins
                                E("pe", mms, r=[kfn.keys] + [q for q, _ in qparts], w=[ps])
                                et = ets[ki % 3]
                                E("act", lambda e, et=et, ps=ps, nq=nq: e.activation(out=et.ap, in_=ps.ap[:, 0:nq], func=AF.Exp, scale=scale), r=[ps], w=[et])
                                if prev is not None:
                                    pv(*prev)
                                prev = (ki, et, vfn(h, kt_))
                            pv(*prev)

                            rd = S2.take([128, nq])
                            E("dve", lambda e, rd=rd, nq=nq: e.reciprocal(rd.ap, psB.ap[:, 0:nq]), r=[psB], w=[rd])
                            mo = mixTt(mix0 + h, q0, nq)
                            E("dve", lambda e, rd=rd, mo=mo, nq=nq: e.tensor_tensor(out=mo.ap, in0=psA.ap[:, 0:nq], in1=rd.ap, op=ALU.mult), r=[psA, rd], w=[mo])

                def q_b(h, q0, nq):
                    return [(T(qT.ap[:, h, q0:q0 + nq], qT.keys), 128)]

                def k_b(h, kt_):
                    return [(kT.ap[:, h // 3, kt_ * 128:(kt_ + 1) * 128], 128)]
                k_b.keys = kT

                def v_b(h, kt_):
                    return vtok.ap[:, kt_, (h // 3) * 128:(h // 3 + 1) * 128]
                v_b.keys = vtok
                S2.off = gq_base
                attention(6, q_b, k_b, v_b, 1.0 / math.sqrt(128.0), 4)
                if DBG is not None and l == 0:
                    dump("mixB_g%d" % g, T(mixT[:, 4, 0:512], [("mixT", 4, 0)]), [128, 512])

                CP("gqa")
                S2.off = 0
                ckvT = S2.take([128, 4, NK], BF16)
                krT = S2.take([64, NK])
                krsq = S2.take([64, NK], BF16)
                trq["bufs"] = [S2.take([128, 128]) for _ in range(2)]
                mla_base = S2.off
                if lat:
                    for a in range(4):
                        S2.off = mla_base
                        ctmp = S2.take([128, 512])
                        DMA("sp", ctmp.ap, cckv[l, a * 128:(a + 1) * 128, :], w=[ctmp])
                        ps = psrot()

                        def trc(e, ps=ps, ctmp=ctmp):
                            ins = None
                            for r4 in range(4):
                                ins = e.transpose(ps.ap[:, r4 * 128:(r4 + 1) * 128], ctmp.ap[:, r4 * 128:(r4 + 1) * 128], ident[:])
                            return ins
                        E("pe", trc, r=[ctmp, idT], w=[ps])
                        copy(ev_eng(), ckvT.ap[:, :, a * 128:(a + 1) * 128], ps.ap.rearrange("p (a b) -> p a b", a=4), [ps], [ckvT])
                        ktmp = S2.take([128, 64])
                        DMA("sp", ktmp.ap, ckr[l, a * 128:(a + 1) * 128, :], w=[ktmp])
                        ps = psrot()
                        E("pe", lambda e, ps=ps, ktmp=ktmp: e.transpose(ps.ap[0:64, 0:128], ktmp.ap, ident[:]), r=[ktmp, idT], w=[ps])
                        copy(ev_eng(), krT.ap[:, a * 128:(a + 1) * 128], ps.ap[0:64, 0:128], [ps], [krT])
                for tc in range(2):
                    S2.off = mla_base
                    raws = [S2.take([128, 512]) for _ in range(4)]

                    def cons_ckv(c0, cs, tc_, ps, raws=raws):
                        copy(ev_eng(), raws[c0 // 128].ap, ps.ap, [ps], [raws[c0 // 128]])
                    for half in range(2):
                        lin_fm(W[:, 2944 + half * 256:2944 + (half + 1) * 256], KT, 256, rhs_h, [tc], 512,
                               lambda c0, cs, tc_, ps, half=half: cons_ckv(half * 256 + c0, cs, tc_, ps))
                    ps_ss = psrot()
                    sumsq_bc([(raws[i], 128) for i in range(4)], 512, ps_ss)
                    rs = S2.take([128, 512])
                    rstd_from(ps_ss, 512, 512, rs)
                    for i in range(4):
                        cn = raws[i]
                        E("dve", lambda e, cn=cn, i=i: e.scalar_tensor_tensor(out=cn.ap, in0=cn.ap, scalar=sp_(2 + i), in1=rs.ap, op0=ALU.mult, op1=ALU.mult), r=[cn, rs, "smallp"], w=[cn])
                        copy("act", ckvT.ap[:, i, KOFF + tc * 512:KOFF + (tc + 1) * 512], cn.ap, [cn], [ckvT])
                        if not lat:
                            tr_out(cn, 128, 512, lambda t0, tc=tc: nckv_d[l, tc * 512 + t0:tc * 512 + t0 + 128, :], slice(i * 128, (i + 1) * 128))

                CP("mla_a")

                def cons_kr(c0, cs, tc, ps):
                    copy("act", krT.ap[:, KOFF + tc * 512:KOFF + (tc + 1) * 512], ps.ap[0:64, :], [ps], [krT])
                    if not lat:
                        S2.off = mla_base
                        kro = T(krT.ap[:, tc * 512:(tc + 1) * 512], krT.keys)
                        tr_out(kro, 64, 512, lambda t0: nkr_d[l, tc * 512 + t0:tc * 512 + t0 + 128, :], slice(0, 64))
                lin_fm(W[:, 3456:3520], KT, 64, rhs_h, [0, 1], 512, cons_kr)
                E("act", lambda e: e.activation(out=krsq.ap, in_=krT.ap, func=AF.Square), r=[krT], w=[krsq])
                CP("mla_b")
                head_base = mla_base
                for h in range(6):
                    S2.off = head_base
                    wuk = S2.take([128, 4, 128], BF16)
                    wuv = S2.take([128, 4, 128], BF16)
                    DMA("pool", wuk.ap, w_uk[l][:, h * 128:(h + 1) * 128].rearrange("(a p) n -> p a n", p=128), w=[wuk])
                    DMA("pool", wuv.ap, w_uv[l][:, h * 128:(h + 1) * 128].rearrange("(a p) n -> p a n", p=128), w=[wuv])
                    knT = S2.take([128, NK], BF16)
                    krn = S2.take([64, NK], BF16)
                    vh = S2.take([128, NK // 128, 128], BF16)
                    qn_n = S2.take([128, NT], BF16)
                    qn_r = S2.take([64, NT], BF16)
                    hb2 = S2.off
                    for kc in range(NK // 512):
                        S2.off = hb2
                        ps = psrot()

                        def mmk(e, ps=ps, kc=kc, h=h):
                            ins = None
                            for r4 in range(4):
                                ins = e.matmul(ps.ap, wuk.ap[:, r4, :], ckvT.ap[:, r4, kc * 512:(kc + 1) * 512], start=(r4 == 0), stop=(r4 == 3))
                            return ins
                        E("pe", mmk, r=[wuk, ckvT], w=[ps])
                        sq = S2.take([128, 512], BF16)
                        E("act", lambda e, sq=sq, ps=ps: e.activation(out=sq.ap, in_=ps.ap, func=AF.Square), r=[ps], w=[sq])
                        ps_ss = psrot()

                        def mmss(e, ps_ss=ps_ss, sq=sq, kc=kc):
                            e.matmul(ps_ss.ap, onesb[:, :], sq.ap, start=True, stop=False)
                            return e.matmul(ps_ss.ap, onesb[0:64, :], krsq.ap[:, kc * 512:(kc + 1) * 512], start=False, stop=True)
                        E("pe", mmss, r=[sq, krsq, onesT], w=[ps_ss])
                        rs = S2.take([128, 512])
                        rstd_from(ps_ss, 512, 192, rs)
                        E("dve", lambda e, ps=ps, rs=rs, kc=kc: e.scalar_tensor_tensor(out=knT.ap[:, kc * 512:(kc + 1) * 512], in0=ps.ap, scalar=sp_(8), in1=rs.ap, op0=ALU.mult, op1=ALU.mult),
                          r=[ps, rs, "smallp"], w=[knT])
                        newtok = lat and kc >= 1
                        if not newtok:
                            E("dve", lambda e, rs=rs, kc=kc: e.scalar_tensor_tensor(out=krn.ap[:, kc * 512:(kc + 1) * 512], in0=krT.ap[:, kc * 512:(kc + 1) * 512], scalar=smallp[0:64, l, 9:10], in1=rs.ap[0:64, :], op0=ALU.mult, op1=ALU.mult),
                              r=[krT, rs, "smallp"], w=[krn])
                        else:
                            tcn = kc - 1
                            kq = S2.take([64, 512], BF16)
                            E("dve", lambda e, rs=rs, kc=kc, kq=kq: e.scalar_tensor_tensor(out=kq.ap, in0=krT.ap[:, kc * 512:(kc + 1) * 512], scalar=smallp[0:64, l, 9:10], in1=rs.ap[0:64, :], op0=ALU.mult, op1=ALU.mult),
                              r=[krT, rs, "smallp"], w=[kq])
                            ps2 = psrot()
                            E("pe", lambda e, ps2=ps2, kq=kq: e.matmul(ps2.ap[0:64, :], permC[:, :], kq.ap, start=True, stop=True), r=[kq, "permC"], w=[ps2])
                            t1 = S2.take([64, 512])
                            E("dve", lambda e, t1=t1, ps2=ps2, tcn=tcn: e.tensor_tensor(out=t1.ap, in0=ps2.ap[0:64, :], in1=ropeC[:, 1, tcn * 512:(tcn + 1) * 512], op=ALU.mult), r=[ps2, "ropeC"], w=[t1])
                            t2 = S2.take([64, 512])
                            E("pool", lambda e, t2=t2, kq=kq, tcn=tcn: e.tensor_tensor(out=t2.ap, in0=kq.ap, in1=ropeC[:, 0, tcn * 512:(tcn + 1) * 512], op=ALU.mult), r=[kq, "ropeC"], w=[t2])
                            E("dve", lambda e, t1=t1, t2=t2, kc=kc: e.tensor_tensor(out=krn.ap[:, kc * 512:(kc + 1) * 512], in0=t1.ap, in1=t2.ap, op=ALU.add), r=[t1, t2], w=[krn])
                    CP("mla_c")
                    for kt_ in range(NK // 128):
                        ps = psrot()

                        def mmv2(e, ps=ps, kt_=kt_, h=h):
                            ins = None
                            for r4 in range(4):
                                ins = e.matmul(ps.ap[:, 0:128], ckvT.ap[:, r4, kt_ * 128:(kt_ + 1) * 128], wuv.ap[:, r4, :], start=(r4 == 0), stop=(r4 == 3))
                            return ins
                        E("pe", mmv2, r=[wuv, ckvT], w=[ps])
                        copy(ev_eng(), vh.ap[:, kt_, :], ps.ap[:, 0:128], [ps], [vh])
                    CP("mla_d")
                    wq_ = load_w(W[:, 1792 + h * 192:1792 + (h + 1) * 192], KT, 192)
                    for tc in range(2):
                        S2.off = hb2
                        psn = psrot()
                        psr = psrot()

                        def mmq(e, psn=psn, psr=psr, tc=tc):
                            ins = None
                            for kt in range(KT):
                                ins = e.matmul(psn.ap, wq_.ap[:, kt, 0:128], hT[:, kt, tc * 512:(tc + 1) * 512], start=(kt == 0), stop=(kt == KT - 1))
                            for kt in range(KT):
                                ins = e.matmul(psr.ap[0:64, :], wq_.ap[:, kt, 128:192], hT[:, kt, tc * 512:(tc + 1) * 512], start=(kt == 0), stop=(kt == KT - 1))
                            return ins
                        E("pe", mmq, r=[wq_] + [hTt(kt, tc) for kt in range(KT)], w=[psn, psr])
                        ps_ss = psrot()
                        sumsq_bc([(psn, 128), (T(psr.ap[0:64, :], psr.keys), 64)], 512, ps_ss)
                        rs = S2.take([128, 512])
                        rstd_from(ps_ss, 512, 192, rs)
                        E("dve", lambda e, psn=psn, rs=rs, tc=tc: e.scalar_tensor_tensor(out=qn_n.ap[:, tc * 512:(tc + 1) * 512], in0=psn.ap, scalar=sp_(6), in1=rs.ap, op0=ALU.mult, op1=ALU.mult),
                          r=[psn, rs, "smallp"], w=[qn_n])
                        if not lat:
                            E("dve", lambda e, psr=psr, rs=rs, tc=tc: e.scalar_tensor_tensor(out=qn_r.ap[:, tc * 512:(tc + 1) * 512], in0=psr.ap[0:64, :], scalar=smallp[0:64, l, 7:8], in1=rs.ap[0:64, :], op0=ALU.mult, op1=ALU.mult),
                              r=[psr, rs, "smallp"], w=[qn_r])
                        else:
                            kq = S2.take([64, 512], BF16)
                            E("dve", lambda e, psr=psr, rs=rs, kq=kq: e.scalar_tensor_tensor(out=kq.ap, in0=psr.ap[0:64, :], scalar=smallp[0:64, l, 7:8], in1=rs.ap[0:64, :], op0=ALU.mult, op1=ALU.mult),
                              r=[psr, rs, "smallp"], w=[kq])
                            ps2 = psrot()
                            E("pe", lambda e, ps2=ps2, kq=kq: e.matmul(ps2.ap[0:64, :], permC[:, :], kq.ap, start=True, stop=True), r=[kq, "permC"], w=[ps2])
                            t1 = S2.take([64, 512])
                            E("dve", lambda e, t1=t1, ps2=ps2, tc=tc: e.tensor_tensor(out=t1.ap, in0=ps2.ap[0:64, :], in1=ropeC[:, 1, tc * 512:(tc + 1) * 512], op=ALU.mult), r=[ps2, "ropeC"], w=[t1])
                            t2 = S2.take([64, 512])
                            E("pool", lambda e, t2=t2, kq=kq, tc=tc: e.tensor_tensor(out=t2.ap, in0=kq.ap, in1=ropeC[:, 0, tc * 512:(tc + 1) * 512], op=ALU.mult), r=[kq, "ropeC"], w=[t2])
                            E("dve", lambda e, t1=t1, t2=t2, tc=tc: e.tensor_tensor(out=qn_r.ap[:, tc * 512:(tc + 1) * 512], in0=t1.ap, in1=t2.ap, op=ALU.add), r=[t1, t2], w=[qn_r])
                    S2.off = hb2
                    CP("mla_e")

                    def q_c(h_, q0, nq):
                        return [(T(qn_n.ap[:, q0:q0 + nq], qn_n.keys), 128), (T(qn_r.ap[:, q0:q0 + nq], qn_r.keys), 64)]

                    def k_c(h_, kt_):
                        return [(knT.ap[:, kt_ * 128:(kt_ + 1) * 128], 128), (krn.ap[:, kt_ * 128:(kt_ + 1) * 128], 64)]
                    k_c.keys = T(None, knT.keys + krn.keys)

                    def v_c(h_, kt_):
                        return vh.ap[:, kt_, :]
                    v_c.keys = vh
                    attention(1, q_c, k_c, v_c, 1.0 / math.sqrt(192.0), 10 + h)
                    CP("mla_f%d" % h)
                if DBG is not None and l == 0:
                    dump("mixC_g%d" % g, T(mixT[:, 10, 0:512], [("mixT", 10, 0)]), [128, 512])

                CP("mla")
                S2.off = 0
                uT = S2.take([128, 4, NT], BF16)
                yacc = S2.take([128, 4, NT])
                s5base = S2.off

                def cons_u(c0, cs, tc, ps, blk):
                    ct = blk * 2 + c0 // 128
                    if cfg.get("var") != "a":
                        copy("act", uT.ap[:, ct, tc * 512:(tc + 1) * 512], ps.ap, [ps], [uT])
                    if cfg.get("var") != "b":
                        E("dve", lambda e: e.tensor_scalar(out=yacc.ap[:, ct, tc * 512:(tc + 1) * 512], in0=ps.ap, scalar1=sp_(10 + ct), scalar2=None, op0=ALU.mult), r=[ps, "smallp"], w=[yacc])
                for blk in range(2):
                    lin_fm(W[:, blk * 256:(blk + 1) * 256], KT, 256, rhs_h, [0, 1], 512, lambda c0, cs, tc, ps, blk=blk: cons_u(c0, cs, tc, ps, blk))
                CP("s5_0")
                c5 = lambda i: s5c[:, i, :]
                K5 = ["s5c"]
                L_ = lambda nm: {"lamr": lamr, "lami": lami, "ldt": ldt}[nm][:, l, :]
                E("act", lambda e: e.activation(out=c5(6), in_=L_("ldt"), func=AF.Exp), r=["ldt"], w=K5)
                E("dve", lambda e: e.tensor_tensor(out=c5(0), in0=L_("lami"), in1=c5(6), op=ALU.mult), r=["lami"] + K5, w=K5)
                E("dve", lambda e: e.tensor_tensor(out=c5(7), in0=L_("lamr"), in1=c5(6), op=ALU.mult), r=["lamr"] + K5, w=K5)
                E("act", lambda e: e.activation(out=c5(1), in_=c5(7), func=AF.Exp), r=K5, w=K5)
                CP("s5_1")
                S3.off = 0
                rti, rtf, rty = S3.take([128, 32]), S3.take([128, 32]), S3.take([128, 32])
                th_all = T(c5(0), K5)
                sin_rr(th_all, 0.0, c5(3), K5, rti, rtf, rty, 32)
                sin_rr(th_all, 0.5 * math.pi, c5(2), K5, rti, rtf, rty, 32)
                CP("s5_2")
                cf = S2.take([128, 6, 32])
                cfa = lambda i: cf.ap[:, i, :]
                E("dve", lambda e: e.tensor_tensor(out=cfa(0), in0=c5(1), in1=c5(2), op=ALU.mult), r=K5, w=[cf])
                E("dve", lambda e: e.tensor_scalar(out=cfa(0), in0=cfa(0), scalar1=-1.0, scalar2=None, op0=ALU.add), r=[cf], w=[cf])
                E("dve", lambda e: e.tensor_tensor(out=cfa(1), in0=c5(1), in1=c5(3), op=ALU.mult), r=K5, w=[cf])
                E("dve", lambda e: e.tensor_tensor(out=cfa(2), in0=L_("lamr"), in1=L_("lamr"), op=ALU.mult), r=["lamr"], w=[cf])
                E("dve", lambda e: e.tensor_tensor(out=cfa(3), in0=L_("lami"), in1=L_("lami"), op=ALU.mult), r=["lami"], w=[cf])
                E("dve", lambda e: e.tensor_tensor(out=cfa(2), in0=cfa(2), in1=cfa(3), op=ALU.add), r=[cf], w=[cf])
                E("dve", lambda e: e.reciprocal(cfa(2), cfa(2)), r=[cf], w=[cf])
                E("dve", lambda e: e.tensor_tensor(out=cfa(3), in0=cfa(0), in1=L_("lamr"), op=ALU.mult), r=[cf, "lamr"], w=[cf])
                E("dve", lambda e: e.tensor_tensor(out=cfa(4), in0=cfa(1), in1=L_("lami"), op=ALU.mult), r=[cf, "lami"], w=[cf])
                E("dve", lambda e: e.tensor_tensor(out=cfa(3), in0=cfa(3), in1=cfa(4), op=ALU.add), r=[cf], w=[cf])
                E("dve", lambda e: e.tensor_tensor(out=c5(4), in0=cfa(3), in1=cfa(2), op=ALU.mult), r=[cf], w=K5)
                E("dve", lambda e: e.tensor_tensor(out=cfa(3), in0=cfa(1), in1=L_("lamr"), op=ALU.mult), r=[cf, "lamr"], w=[cf])
                E("dve", lambda e: e.tensor_tensor(out=cfa(4), in0=cfa(0), in1=L_("lami"), op=ALU.mult), r=[cf, "lami"], w=[cf])
                E("dve", lambda e: e.tensor_tensor(out=cfa(3), in0=cfa(3), in1=cfa(4), op=ALU.subtract), r=[cf], w=[cf])
                E("dve", lambda e: e.tensor_tensor(out=c5(5), in0=cfa(3), in1=cfa(2), op=ALU.mult), r=[cf], w=K5)
                init0 = S2.take([128, 2, 32])
                if lat:
                    h0 = S2.take([128, 2, 32])
                    DMA("sp", h0.ap[:, 0, :], h0r_d[l], w=[h0])
                    DMA("sp", h0.ap[:, 1, :], h0i_d[l], w=[h0])
                    E("dve", lambda e: e.tensor_tensor(out=cfa(0), in0=c5(2), in1=h0.ap[:, 0, :], op=ALU.mult), r=K5 + [h0], w=[cf])
                    E("dve", lambda e: e.tensor_tensor(out=cfa(1), in0=c5(3), in1=h0.ap[:, 1, :], op=ALU.mult), r=K5 + [h0], w=[cf])
                    E("dve", lambda e: e.tensor_tensor(out=init0.ap[:, 0, :], in0=cfa(0), in1=cfa(1), op=ALU.subtract), r=[cf], w=[init0])
                    E("dve", lambda e: e.tensor_tensor(out=cfa(0), in0=c5(3), in1=h0.ap[:, 0, :], op=ALU.mult), r=K5 + [h0], w=[cf])
                    E("dve", lambda e: e.tensor_tensor(out=cfa(1), in0=c5(2), in1=h0.ap[:, 1, :], op=ALU.mult), r=K5 + [h0], w=[cf])
                    E("dve", lambda e: e.tensor_tensor(out=init0.ap[:, 1, :], in0=cfa(0), in1=cfa(1), op=ALU.add), r=[cf], w=[init0])
                else:
                    E("dve", lambda e: e.memset(init0.ap, 0.0), w=[init0])
                jj = S2.take([128, 2, 256])
                DMA("sp", jj.ap, jj_d, w=[jj])
                CP("s5_a")
                LCH = 256
                UN = 512
                tab4s = [S2.take([128, 4, UN]) for _ in range(2)]
                chain = S2.take([128, 2, 2])
                if not lat:
                    maskz = S2.take([128, 2 * UN])
                    E("dve", lambda e: e.memset(maskz.ap, 1.0), w=[maskz])
                    for z in range(4):
                        E("dve", lambda e, z=z: e.memset(maskz.ap[:, z * 256:z * 256 + 1], 0.0), w=[maskz])
                bu2 = T(psall[:, 5:7, :], [("ps", 5), ("ps", 6)])
                pend = []

                def flush():
                    while pend:
                        psy, ct_, t0_ = pend.pop(0)
                        E("dve", lambda e: e.tensor_tensor(out=yacc.ap[:, ct_, t0_:t0_ + UN], in0=yacc.ap[:, ct_, t0_:t0_ + UN], in1=psy.ap, op=ALU.add), r=[psy, yacc], w=[yacc])
                for gp in range(16):
                    flush()
                    S3.off = 0
                    ct = gp // 4
                    Bst = S3.take([128, 4, 128])
                    if g == GROUPS[0]:
                        DMA("sp", Bst.ap, Bpad_d[l, gp], w=[Bst])
                    Cw = S3.take([128, 4, 128], BF16)
                    DMA("pool", Cw.ap, Cpad_d[l, gp], w=[Cw])
                    Bb = S3.take([128, 4, 128], BF16)
                    tmpB = S3.take([128, 128])
                    BT = S3.take([128, 4, 128], BF16)
                    a1 = S3.take([128, LCH])
                    a2 = S3.take([128, LCH])
                    a3 = S3.take([128, LCH])
                    a4 = S3.take([128, LCH])
                    rTz = S3.take([128, 2 * UN])
                    Pq = S3.take([128, 4, UN])
                    Xq = S3.take([128, 2, UN])
                    Gq = S3.take([128, 2, UN])
                    Hq = S3.take([128, 2, UN], BF16)
                    he = S3.take([128, 8])
                    first = (g == GROUPS[0])
                    if not first:
                        DMA("sp", BT.ap.rearrange("p a b -> p (a b)"), btc[l, gp], r=[("btc", l, gp)], w=[BT])
                    for dr in (range(2) if first else ()):
                        ci = dr * 16 + gp
                        cr_, ci_ = s5c[:, 4, ci:ci + 1], s5c[:, 5, ci:ci + 1]
                        br, bi = Bst.ap[:, dr * 2, :], Bst.ap[:, dr * 2 + 1, :]
                        E("dve", lambda e, bi=bi, ci_=ci_: e.tensor_scalar(out=tmpB.ap, in0=bi, scalar1=ci_, scalar2=None, op0=ALU.mult), r=[Bst] + K5, w=[tmpB])
                        E("dve", lambda e, br=br, cr_=cr_, dr=dr: e.scalar_tensor_tensor(out=Bb.ap[:, dr * 2, :], in0=br, scalar=cr_, in1=tmpB.ap, op0=ALU.mult, op1=ALU.subtract), r=[Bst, tmpB] + K5, w=[Bb])
                        E("dve", lambda e, br=br, ci_=ci_: e.tensor_scalar(out=tmpB.ap, in0=br, scalar1=ci_, scalar2=None, op0=ALU.mult), r=[Bst] + K5, w=[tmpB])
                        E("dve", lambda e, bi=bi, cr_=cr_, dr=dr: e.scalar_tensor_tensor(out=Bb.ap[:, dr * 2 + 1, :], in0=bi, scalar=cr_, in1=tmpB.ap, op0=ALU.mult, op1=ALU.add), r=[Bst, tmpB] + K5, w=[Bb])
                    if first:
                        ps = psrot()

                        def trB(e, ps=ps):
                            ins = None
                            for q in range(4):
                                ins = e.matmul(ps.ap[:, q * 128:(q + 1) * 128], Bb.ap[:, q, :], identb[:, :], start=True, stop=True)
                            return ins
                        E("pe", trB, r=[Bb, identbT], w=[ps])
                        copy("act", BT.ap, ps.ap.rearrange("p (a b) -> p a b", a=4), [ps], [BT])
                        DMA("sp", btc[l, gp], BT.ap.rearrange("p a b -> p (a b)"), r=[BT], w=[("btc", l, gp)])
                    for dr in range(2):
                        ci = dr * 16 + gp
                        tab4 = tab4s[dr]
                        th = s5c[:, 0, ci:ci + 1]
                        rsc = s5c[:, 1, ci:ci + 1]
                        if first and l not in pregen_done:
                            E("dve", lambda e, th=th, dr=dr: e.tensor_scalar(out=a1.ap, in0=jj.ap[:, dr, :], scalar1=th, scalar2=None, op0=ALU.mult), r=[jj] + K5, w=[a1])
                            bc = lambda t_: t_.ap.unsqueeze(1).broadcast_to([128, 2, LCH])
                            halves = lambda pl: tab4.ap[:, pl, :].rearrange("p (a b) -> p a b", a=2)
                            sin_rr(a1, 0.0, None, None, a2, a3, a4, LCH)
                            E("act", lambda e: e.activation(out=halves(1), in_=bc(a4), func=AF.Sin), r=[a4], w=[tab4])
                            E("act", lambda e: e.activation(out=halves(2), in_=bc(a4), func=AF.Sin, scale=-1.0), r=[a4], w=[tab4])
                            E("dve", lambda e: e.tensor_scalar(out=a2.ap, in0=a4.ap, scalar1=0.5 * math.pi, scalar2=None, op0=ALU.add), r=[a4], w=[a2])
                            E("dve", lambda e: e.tensor_scalar(out=a3.ap, in0=a2.ap, scalar1=math.pi, scalar2=-TWO_PI, op0=ALU.is_gt, op1=ALU.mult), r=[a2], w=[a3])
                            E("dve", lambda e: e.tensor_tensor(out=a2.ap, in0=a2.ap, in1=a3.ap, op=ALU.add), r=[a2, a3], w=[a2])
                            E("act", lambda e: e.activation(out=halves(0), in_=bc(a2), func=AF.Sin), r=[a2], w=[tab4])
                            E("act", lambda e: e.activation(out=halves(3), in_=bc(a2), func=AF.Sin), r=[a2], w=[tab4])
                            DMA("sp", tabc[l, ci], tab4.ap.rearrange("p a b -> p (a b)"), r=[tab4], w=[("tabc", l, ci)])
                        else:
                            DMA("sp", tab4.ap.rearrange("p a b -> p (a b)"), tabc[l, ci], r=[("tabc", l, ci)], w=[tab4])
                        if not lat:
                            E("dve", lambda e, rsc=rsc: e.tensor_scalar(out=rTz.ap, in0=maskz.ap, scalar1=rsc, scalar2=None, op0=ALU.mult), r=[maskz] + K5, w=[rTz])
                        else:
                            E("dve", lambda e, rsc=rsc: e.tensor_scalar(out=rTz.ap[:, 0:LCH], in0=jj.ap[:, 0, :], scalar1=0.0, scalar2=rsc, op0=ALU.mult, op1=ALU.add), r=[jj] + K5, w=[rTz])
                            E("dve", lambda e, dr=dr, ci=ci: e.tensor_copy(chain.ap[:, dr, :], init0.ap[:, :, ci]), r=[init0], w=[chain])
                            le_ = LCH - 1 if dr == 0 else 0
                            cl_, sl__ = tab4.ap[:, 0, le_:le_ + 1], tab4.ap[:, 1, le_:le_ + 1]
                            cth_, sth_ = s5c[:, 2, ci:ci + 1], s5c[:, 3, ci:ci + 1]
                            E("dve", lambda e, sl__=sl__, sth_=sth_: e.tensor_tensor(out=he.ap[:, 2:3], in0=sl__, in1=sth_, op=ALU.mult), r=[tab4] + K5, w=[he])
                            E("dve", lambda e, cl_=cl_, cth_=cth_: e.scalar_tensor_tensor(out=he.ap[:, 4:5], in0=cl_, scalar=cth_, in1=he.ap[:, 2:3], op0=ALU.mult, op1=ALU.subtract), r=[tab4, he] + K5, w=[he])
                            E("dve", lambda e, sl__=sl__, cth_=cth_: e.tensor_tensor(out=he.ap[:, 3:4], in0=sl__, in1=cth_, op=ALU.mult), r=[tab4] + K5, w=[he])
                            E("dve", lambda e, cl_=cl_, sth_=sth_: e.scalar_tensor_tensor(out=he.ap[:, 5:6], in0=cl_, scalar=sth_, in1=he.ap[:, 3:4], op0=ALU.mult, op1=ALU.add), r=[tab4, he] + K5, w=[he])
                        tab2 = tab4.ap.rearrange("p (a b) c -> p a (b c)", a=2)
                        for ui in range(2):
                            u = ui if (dr == 0 or not lat) else 1 - ui
                            t0 = u * UN

                            def mmbu(e, dr=dr, t0=t0):
                                e.matmul(bu2.ap[:, 0, :], BT.ap[:, dr * 2, :], uT.ap[:, ct, t0:t0 + UN], start=True, stop=True)
                                return e.matmul(bu2.ap[:, 1, :], BT.ap[:, dr * 2 + 1, :], uT.ap[:, ct, t0:t0 + UN], start=True, stop=True)
                            E("pe", mmbu, r=[BT, uT], w=[bu2])
                            buf = bu2.ap.rearrange("p a b -> p (a b)").unsqueeze(1).broadcast_to([128, 2, 2 * UN])
                            P2 = Pq.ap.rearrange("p (a b) c -> p a (b c)", a=2)
                            E("dve", lambda e, buf=buf, P2=P2, tab2=tab2: e.tensor_tensor(out=P2, in0=buf, in1=tab2, op=ALU.mult), r=[bu2, tab4], w=[Pq])
                            P4 = Pq.ap.rearrange("p (a b) c -> p a b c", a=2)
                            E("dve", lambda e, P4=P4: e.tensor_tensor(out=Xq.ap, in0=P4[:, :, 0, :], in1=P4[:, :, 1, :], op=ALU.add), r=[Pq], w=[Xq])
                            flush()
                            Xf = Xq.ap.rearrange("p a b -> p (a b)")
                            Gf = Gq.ap.rearrange("p a b -> p (a b)")
                            if not lat:
                                sl = slice(None) if dr == 0 else slice(None, None, -1)
                                E("dve", lambda e, sl=sl, Xf=Xf, Gf=Gf: e.tensor_tensor_scan(out=Gf[:, sl], data0=rTz.ap, data1=Xf[:, sl], initial=0.0, op0=ALU.mult, op1=ALU.add), r=[Xq, rTz], w=[Gq])
                            else:
                                for cj in range(2):
                                    c = cj if dr == 0 else 1 - cj
                                    cs0 = c * LCH
                                    sl = slice(cs0, cs0 + LCH) if dr == 0 else slice(cs0 + LCH - 1, cs0 - 1 if cs0 > 0 else None, -1)
                                    for pl in range(2):
                                        E("dve", lambda e, sl=sl, pl=pl, dr=dr: e.tensor_tensor_scan(out=Gq.ap[:, pl, sl], data0=rTz.ap[:, 0:LCH], data1=Xq.ap[:, pl, sl], initial=chain.ap[:, dr, pl:pl + 1], op0=ALU.mult, op1=ALU.add), r=[Xq, rTz, chain], w=[Gq])
                                    le = LCH - 1 if dr == 0 else 0
                                    grl, gil = Gq.ap[:, 0, cs0 + le:cs0 + le + 1], Gq.ap[:, 1, cs0 + le:cs0 + le + 1]
                                    wr_, wi_ = he.ap[:, 4:5], he.ap[:, 5:6]
                                    E("dve", lambda e, gil=gil: e.tensor_tensor(out=he.ap[:, 0:1], in0=gil, in1=wi_, op=ALU.mult), r=[Gq, he], w=[he])
                                    E("dve", lambda e, grl=grl, dr=dr: e.scalar_tensor_tensor(out=chain.ap[:, dr, 0:1], in0=grl, scalar=wr_, in1=he.ap[:, 0:1], op0=ALU.mult, op1=ALU.subtract), r=[Gq, he], w=[chain])
                                    E("dve", lambda e, gil=gil: e.tensor_tensor(out=he.ap[:, 1:2], in0=gil, in1=wr_, op=ALU.mult), r=[Gq, he], w=[he])
                                    E("dve", lambda e, grl=grl, dr=dr: e.scalar_tensor_tensor(out=chain.ap[:, dr, 1:2], in0=grl, scalar=wi_, in1=he.ap[:, 1:2], op0=ALU.mult, op1=ALU.add), r=[Gq, he], w=[chain])
                            gbf = Gf.unsqueeze(1).broadcast_to([128, 2, 2 * UN])
                            E("dve", lambda e, gbf=gbf, P2=P2, tab2=tab2: e.tensor_tensor(out=P2, in0=gbf, in1=tab2, op=ALU.mult), r=[Gq, tab4], w=[Pq])
                            E("dve", lambda e, P4=P4: e.tensor_tensor(out=Hq.ap, in0=P4[:, :, 0, :], in1=P4[:, :, 1, :], op=ALU.subtract), r=[Pq], w=[Hq])
                            if not lat:
                                le = LCH - 1 if dr == 0 else 0
                                col = ((l * 4 + 2 * u) * 2 + dr) * 32 + gp * 2
                                E("dve", lambda e, le=le, col=col: e.tensor_tensor(out=hfin[:, col:col + 65:64], in0=Pq.ap[:, 0, le:le + 257:256], in1=Pq.ap[:, 1, le:le + 257:256], op=ALU.subtract), r=[Pq], w=["hfin"])
                                E("dve", lambda e, le=le, col=col: e.tensor_tensor(out=hfin[:, col + 1:col + 66:64], in0=Pq.ap[:, 3, le:le + 257:256], in1=Pq.ap[:, 2, le:le + 257:256], op=ALU.subtract), r=[Pq], w=["hfin"])
                            psy = psrot()

                            def mmy(e, psy=psy, dr=dr):
                                e.matmul(psy.ap, Cw.ap[:, dr * 2, :], Hq.ap[:, 0, :], start=True, stop=False)
                                return e.matmul(psy.ap, Cw.ap[:, dr * 2 + 1, :], Hq.ap[:, 1, :], start=False, stop=True)
                            E("pe", mmy, r=[Cw, Hq], w=[psy])
                            pend.append((psy, ct, t0))
                            ada_drain(1)
                flush()
                CP("s5_d")
                S2.off = s5base
                yb = S2.take([128, 4, NT], BF16)
                copy("act", yb.ap, yacc.ap, [yacc], [yb])
                if DBG is not None and l == 0:
                    dump("yacc_g%d" % g, T(yacc.ap[:, 0, 0:512], yacc.keys), [128, 512])
                S3.off = 0
                wg = S3.take([128, 4, 1024], BF16)
                sg = S3.take([128, 512])
                DMA("pool", wg.ap, w_glu[l].rearrange("(a p) n -> p a n", p=128), w=[wg])
                for j in range(4):
                    for tc in range(2):
                        psv, psg = psrot(), psrot()

                        def mmg(e, psv=psv, psg=psg, j=j, tc=tc):
                            ins = None
                            for a in range(4):
                                ins = e.matmul(psv.ap, wg.ap[:, a, j * 128:(j + 1) * 128], yb.ap[:, a, tc * 512:(tc + 1) * 512], start=(a == 0), stop=(a == 3))
                            for a in range(4):
                                ins = e.matmul(psg.ap, wg.ap[:, a, 512 + j * 128:512 + (j + 1) * 128], yb.ap[:, a, tc * 512:(tc + 1) * 512], start=(a == 0), stop=(a == 3))
                            return ins
                        E("pe", mmg, r=[wg, yb], w=[psv, psg])
                        E("act", lambda e, psg=psg, sg=sg: e.activation(out=sg.ap, in_=psg.ap, func=AF.Sigmoid), r=[psg], w=[sg])
                        mo = mixTt(j, tc * 512, 512)
                        E("dve", lambda e, psv=psv, sg=sg, mo=mo: e.tensor_tensor(out=mo.ap, in0=psv.ap, in1=sg.ap, op=ALU.mult), r=[psv, sg], w=[mo])
                if DBG is not None and l == 0:
                    dump("mixA_g%d" % g, T(mixT[:, 0, 0:512], [("mixT", 0, 0)]), [128, 512])

                CP("s5")
                ada_drain(upto_layer=l)
                E("dve", lambda e, l=l, g=g: e.tensor_copy(mods[:, 2, :], modall[:, l, 32:48, g]), r=[mk], w=["mods"])
                E("dve", lambda e, l=l, g=g: e.scalar_tensor_tensor(out=mods[:, 3, :], in0=modall[:, l, 64:80, g], scalar=1.0, in1=gmlp[:, l, :], op0=ALU.add, op1=ALU.mult), r=[mk, "gmlp"], w=["mods"])
                E("dve", lambda e, l=l, g=g: e.tensor_copy(mods[:, 4, :], modall[:, l, 48:64, g]), r=[mk], w=["mods"])
                E("dve", lambda e, l=l, g=g: e.tensor_copy(mods[:, 5, :], modall[:, l, 80:96, g]), r=[mk], w=["mods"])

                def resid(gsl, fc, tc, ps):
                    x_ = xTt(fc, tc)
                    E("dve", lambda e: e.scalar_tensor_tensor(out=x_.ap, in0=ps.ap, scalar=mods[:, gsl, fc:fc + 1], in1=x_.ap, op0=ALU.mult, op1=ALU.add), r=[ps, x_, "mods"], w=[x_])
                rhs_m = lambda kt, tc: mixTt(kt, tc * 512, 512)
                for cb in range(8):
                    lin_fm(w_out[l][:, cb * 256:(cb + 1) * 256], KT, 256, rhs_m, [0, 1], 512, lambda c0, cs, tc, ps, cb=cb: resid(2, cb * 2 + c0 // 128, tc, ps))
                if DBG is not None and l == 0:
                    dump("x1_g%d" % g, T(xT[:, 0, 0:512], [("xT", 0, 0)]), [128, 512])

                CP("out")
                norm_to_h(l, g, 3, 4)
                pg_list = []
                if g == GROUPS[0] and l + 1 < NL:
                    l1 = l + 1
                    S2.off = 4096
                    pg_dt = S2.take([128, 32])
                    pg_th = S2.take([128, 32])
                    pg_jj = S2.take([128, 2, 256])
                    pg_a = [S2.take([128, 256]) for _ in range(4)]
                    pg_tab = S2.take([128, 4, 512])
                    E("act", lambda e: e.activation(out=pg_dt.ap, in_=ldt[:, l1, :], func=AF.Exp), r=["ldt"], w=[pg_dt])
                    E("dve", lambda e: e.tensor_tensor(out=pg_th.ap, in0=lami[:, l1, :], in1=pg_dt.ap, op=ALU.mult), r=["lami", pg_dt], w=[pg_th])
                    DMA("sp", pg_jj.ap, jj_d, w=[pg_jj])
                    pg_list = [(dr_, gp_) for gp_ in range(16) for dr_ in range(2)]
                    pregen_done.add(l1)

                def pregen_one():
                    if not pg_list:
                        return
                    dr_, gp_ = pg_list.pop(0)
                    ci_ = dr_ * 16 + gp_
                    a1_, a2_, a3_, a4_ = pg_a
                    th_ = pg_th.ap[:, ci_:ci_ + 1]
                    E("dve", lambda e: e.tensor_scalar(out=a1_.ap, in0=pg_jj.ap[:, dr_, :], scalar1=th_, scalar2=None, op0=ALU.mult), r=[pg_jj, pg_th], w=[a1_])
                    bc_ = lambda t_: t_.ap.unsqueeze(1).broadcast_to([128, 2, 256])
                    hv_ = lambda pl: pg_tab.ap[:, pl, :].rearrange("p (a b) -> p a b", a=2)
                    sin_rr(a1_, 0.0, None, None, a2_, a3_, a4_, 256)
                    E("act", lambda e: e.activation(out=hv_(1), in_=bc_(a4_), func=AF.Sin), r=[a4_], w=[pg_tab])
                    E("act", lambda e: e.activation(out=hv_(2), in_=bc_(a4_), func=AF.Sin, scale=-1.0), r=[a4_], w=[pg_tab])
                    E("dve", lambda e: e.tensor_scalar(out=a2_.ap, in0=a4_.ap, scalar1=0.5 * math.pi, scalar2=None, op0=ALU.add), r=[a4_], w=[a2_])
                    E("dve", lambda e: e.tensor_scalar(out=a3_.ap, in0=a2_.ap, scalar1=math.pi, scalar2=-TWO_PI, op0=ALU.is_gt, op1=ALU.mult), r=[a2_], w=[a3_])
                    E("dve", lambda e: e.tensor_tensor(out=a2_.ap, in0=a2_.ap, in1=a3_.ap, op=ALU.add), r=[a2_, a3_], w=[a2_])
                    E("act", lambda e: e.activation(out=hv_(0), in_=bc_(a2_), func=AF.Sin), r=[a2_], w=[pg_tab])
                    E("act", lambda e: e.activation(out=hv_(3), in_=bc_(a2_), func=AF.Sin), r=[a2_], w=[pg_tab])
                    DMA("sp", tabc[l + 1, ci_], pg_tab.ap.rearrange("p a b -> p (a b)"), r=[pg_tab], w=[("tabc", l + 1, ci_)])
                for qd in range(4):
                    S2.off = 0
                    rl = S2.take([128, 512], BF16)
                    rl2 = S2.take([128, 512], BF16)
                    rls = [rl, rl2]
                    cnt = {"i": 0}

                    def cons_ff1(c0, cs, tc, ps, jb):
                        j = jb * 2 + c0 // 128
                        r_ = rls[cnt["i"] % 2]
                        cnt["i"] += 1
                        E("act", lambda e: e.activation(out=r_.ap, in_=ps.ap, func=AF.Relu), r=[ps], w=[r_])
                        a_ = mixTt(j, tc * 512, 512)
                        E("dve", lambda e: e.tensor_tensor(out=a_.ap, in0=r_.ap, in1=r_.ap, op=ALU.mult), r=[r_], w=[a_])
                    for jb in range(8):
                        lin_fm(w_ff1[l][:, qd * 2048 + jb * 256:qd * 2048 + (jb + 1) * 256], KT, 256, rhs_h, [0, 1], 512, lambda c0, cs, tc, ps, jb=jb: cons_ff1(c0, cs, tc, ps, jb))
                        pregen_one()
                    for cb in range(8):
                        lin_fm(w_ff2[l][qd * 2048:(qd + 1) * 2048, cb * 256:(cb + 1) * 256], KT, 256, rhs_m, [0, 1], 512, lambda c0, cs, tc, ps, cb=cb: resid(5, cb * 2 + c0 // 128, tc, ps))

            S2.off = 0
            ybufs = [S2.take([128, D]) for _ in range(2)]
            for tt in range(8):
                yb_ = ybufs[tt % 2]
                for k4 in range(4):
                    ps = psrot()

                    def trp2(e, ps=ps, k4=k4, tt=tt):
                        ins = None
                        for j in range(4):
                            kt = k4 * 4 + j
                            ins = e.transpose(ps.ap[:, j * 128:(j + 1) * 128], xT[:, kt, tt * 128:(tt + 1) * 128], ident[:])
                        return ins
                    E("pe", trp2, r=[idT] + [("xT", k4 * 4 + j, tt // 4) for j in range(4)], w=[ps])
                    copy(ev_eng(), yb_.ap[:, k4 * 512:(k4 + 1) * 512], ps.ap, [ps], [yb_])
                DMA("sp", y_d[g, tt * 128:(tt + 1) * 128, :], yb_.ap, r=[yb_])

        S2.off = 0
        for q in range(4):
            ps = psrot()
            E("pe", lambda e, ps=ps, q=q: e.transpose(ps.ap[:, 0:128], hfin[:, q * 128:(q + 1) * 128], ident[:]), r=["hfin", idT], w=[ps])
            o = S2.take([128, 128])
            copy("dve", o.ap, ps.ap[:, 0:128], [ps], [o])
            DMA("sp", nssm_d[q * 128:(q + 1) * 128, :], o.ap, r=[o])


    except _Stop:
        pass
    stats = P.finalize(st)
    st.close()
    return nc, stats


def _rope_tables(d):
    half, quarter = d // 2, d // 4
    t = np.arange(NT)
    row, col = (t // 64).astype(np.float32), (t % 64).astype(np.float32)
    inv = (10000.0 ** (-(np.arange(quarter, dtype=np.float32) / quarter))).astype(np.float32)
    cos = np.zeros((d, NT), np.float32)
    sin = np.zeros((d, NT), np.float32)
    perm = np.zeros((d, d), np.float32)
    for dd in range(d):
        pos = row if dd < half else col
        i = dd % quarter
        first = (dd % half) < quarter
        ang = pos * inv[i]
        cos[dd] = np.cos(ang)
        sin[dd] = -np.sin(ang) if first else np.sin(ang)
        partner = dd + quarter if first else dd - quarter
        perm[partner, dd] = 1.0
    return np.stack([cos, sin], axis=1).astype(np.float32), perm


def prep_inputs(inp):
    f = lambda a: np.ascontiguousarray(np.asarray(a, dtype=np.float32))
    sh = {}
    fm = lambda v: f(v.reshape(-1, 128).T)
    sh["w_mod"] = f(inp["w_mod"])
    sh["bmodT"] = f(np.stack([fm(inp["b_mod"][l]) for l in range(2)], axis=1))
    sh["gmixT"] = f(np.stack([fm(inp["norm_mix"][l]) for l in range(2)], axis=1))
    sh["gmlpT"] = f(np.stack([fm(inp["norm_mlp"][l]) for l in range(2)], axis=1))
    sh["w_in"] = f(inp["w_in"])
    sm = np.zeros((128, 2, 16), np.float32)
    for l in range(2):
        sm[:, l, 0] = inp["gqa_q_norm"][l]
        sm[:, l, 1] = inp["gqa_k_norm"][l]
        sm[:, l, 2:6] = fm(inp["mla_kv_norm"][l])
        sm[:, l, 6] = inp["mla_q_norm"][l][:128]
        sm[:64, l, 7] = inp["mla_q_norm"][l][128:]
        sm[:, l, 8] = inp["mla_k_norm"][l][:128]
        sm[:64, l, 9] = inp["mla_k_norm"][l][128:]
        sm[:, l, 10:14] = fm(inp["ssm_d"][l])
    sh["smallp"] = sm
    sh["w_uk"] = f(np.asarray(inp["mla_w_uk"]).reshape(2, 512, 768))
    sh["w_uv"] = f(np.asarray(inp["mla_w_uv"]).reshape(2, 512, 768))

    def st_layout(a):
        a = np.asarray(a, np.float32).reshape(2, 2, 16, 2, 64)
        return f(a.transpose(3, 4, 0, 1, 2).reshape(128, 2, 32))
    sh["lamr"] = st_layout(inp["ssm_lam_re"])
    sh["lami"] = st_layout(inp["ssm_lam_im"])
    sh["ldt"] = st_layout(np.broadcast_to(np.asarray(inp["ssm_log_dt"])[..., None], (2, 2, 32, 64)))
    Bpad = np.zeros((2, 16, 128, 4, 128), np.float32)
    Cpad = np.zeros((2, 16, 128, 4, 128), np.float32)
    bre, bim = np.asarray(inp["ssm_b_re"]), np.asarray(inp["ssm_b_im"])
    cre, cim = np.asarray(inp["ssm_c_re"]), np.asarray(inp["ssm_c_im"])
    for gp in range(16):
        for g2 in range(2):
            gg = gp * 2 + g2
            c0 = (gp % 4) * 32 + g2 * 16
            for dr in range(2):
                Bpad[:, gp, g2 * 64:(g2 + 1) * 64, dr * 2, c0:c0 + 16] = bre[:, dr, gg]
                Bpad[:, gp, g2 * 64:(g2 + 1) * 64, dr * 2 + 1, c0:c0 + 16] = bim[:, dr, gg]
                Cpad[:, gp, g2 * 64:(g2 + 1) * 64, dr * 2, c0:c0 + 16] = cre[:, dr, gg].transpose(0, 2, 1)
                Cpad[:, gp, g2 * 64:(g2 + 1) * 64, dr * 2 + 1, c0:c0 + 16] = cim[:, dr, gg].transpose(0, 2, 1)
    sh["Bpad"], sh["Cpad"] = Bpad, Cpad
    sh["w_glu"] = f(inp["ssm_w_glu"])
    sh["w_out"] = f(inp["w_out"])
    sh["w_ff1"] = f(inp["w_ff1"])
    sh["w_ff2"] = f(inp["w_ff2"])
    sh["ident"] = np.eye(128, dtype=np.float32)
    sh["ropeB"], sh["permB"] = _rope_tables(128)
    sh["ropeC"], sh["permC"] = _rope_tables(64)
    j = np.arange(256, dtype=np.float32)
    sh["jj"] = f(np.broadcast_to(np.stack([j, 255.0 - j])[None], (128, 2, 256)))
    percore = []
    xp, xs = np.asarray(inp["x_prompt"]), np.asarray(inp["x_sample"])
    for i in range(NCORES):
        d = dict(sh)
        d["xin"] = f(np.stack([xp[4 * i:4 * i + 4].reshape(NT, D), xs[i]]))
        d["ck"] = f(np.asarray(inp["cache_attn_k"])[i].reshape(2, 512, 256))
        d["cv"] = f(np.asarray(inp["cache_attn_v"])[i].reshape(2, 512, 256))
        d["cckv"] = f(np.asarray(inp["cache_mla_ckv"])[i])
        d["ckr"] = f(np.asarray(inp["cache_mla_krope"])[i])
        s = np.asarray(inp["state_ssm"])[i].reshape(2, 2, 16, 2, 64, 2)
        s = s.transpose(0, 3, 4, 1, 2, 5).reshape(2, 128, 32, 2)
        d["h0r"], d["h0i"] = f(s[..., 0]), f(s[..., 1])
        cv_ = np.stack([np.asarray(inp["c_ctx"]), np.asarray(inp["c"])[i]])
        d["cT"] = f(cv_.reshape(2, 16, 128).transpose(2, 1, 0))
        percore.append(d)
    return percore


_CACHE = {}


def kernel(**inputs):
    if "nc" not in _CACHE:
        _CACHE["nc"] = build()[0]
    nc = _CACHE["nc"]
    in_maps = prep_inputs(inputs)
    res = run_bass_kernel_spmd(nc, in_maps, core_ids=list(range(NCORES)))
    R = res.results
    y_prompt = np.concatenate([r["y"][0].reshape(4, 256, D) for r in R], axis=0)
    y_sample = np.stack([r["y"][1] for r in R], axis=0)

    def seqout(name, tail):
        return np.concatenate([r[name].reshape(2, 4, 256, -1).transpose(1, 0, 2, 3).reshape((4, 2, 256) + tail) for r in R], axis=0)
    new_k = seqout("nk", (2, 128))
    new_v = seqout("nv", (2, 128))
    new_ckv = seqout("nckv", (512,))
    new_kr = seqout("nkr", (64,))
    ss = []
    for r in R:
        a = r["nssm"].reshape(2, 4, 2, 16, 2, 2, 64)
        a = a.transpose(1, 0, 2, 3, 5, 6, 4).reshape(4, 2, 2, 32, 64, 2)
        ss.append(a)
    new_ssm = np.concatenate(ss, axis=0)
    f = lambda a: np.ascontiguousarray(a, dtype=np.float32)
    return (f(y_prompt), f(y_sample), f(new_k), f(new_v), f(new_ckv), f(new_kr), f(new_ssm))
```
